# Optimizing a Trainium2 kernel written in Bass

```python
import jax
import jax.numpy as jnp
from jax import lax
import numpy as np

D_MODEL = 1024
BATCH = 2
SEQ = 8192
DEPTH = 4

GRID_W = 64
CTX_LEN = 256

N_MIXERS = 3
N_LAYERS_A = (DEPTH + 2) // 3
N_LAYERS_B = (DEPTH + 1) // 3
N_LAYERS_C = DEPTH // 3

A_HEADS = 8
A_QK_DIM = D_MODEL // 2
A_V_DIM = D_MODEL
A_DK = A_QK_DIM // A_HEADS
A_DV = A_V_DIM // A_HEADS
A_CHUNK = 128

B_Q_HEADS = 16
B_KV_HEADS = 4
B_HEAD_DIM = D_MODEL // B_Q_HEADS
B_GROUP = B_Q_HEADS // B_KV_HEADS
B_Q_DIM = B_Q_HEADS * B_HEAD_DIM
B_KV_DIM = B_KV_HEADS * B_HEAD_DIM
B_WINDOW = 128
B_BLOCK = 128
ROPE_BASE = 10000.0

C_CONV_W = 3

MLP_HIDDEN = 4 * D_MODEL
NORM_EPS = 1e-6

kernel_name = 'hybrid_mlstm_swa_shortconv_dit'


def rmsnorm(x, g):
    xf = x.astype(jnp.float32)
    y = xf * lax.rsqrt(jnp.mean(xf * xf, axis=-1, keepdims=True) + NORM_EPS)
    return (y * g.astype(jnp.float32)).astype(x.dtype)


def modulate(x, shift, scale):
    return x * (1 + scale) + shift


def sq_relu_mlp(x, w1, w2):
    return jnp.square(jax.nn.relu(x @ w1)) @ w2


def axial_angles(rows):
    row = jnp.repeat(jnp.arange(rows, dtype=jnp.float32), GRID_W)
    col = jnp.tile(jnp.arange(GRID_W, dtype=jnp.float32), rows)
    n_freq = B_HEAD_DIM // 4
    inv_freq = ROPE_BASE ** (-jnp.arange(n_freq, dtype=jnp.float32) / n_freq)
    return row[:, None] * inv_freq, col[:, None] * inv_freq


def rope_half(x, ang):
    cos = jnp.cos(ang)[None, :, None, :]
    sin = jnp.sin(ang)[None, :, None, :]
    x1, x2 = jnp.split(x.astype(jnp.float32), 2, axis=-1)
    return jnp.concatenate([x1 * cos - x2 * sin, x2 * cos + x1 * sin], axis=-1)


def axial_rope(x, ang_row, ang_col):
    half = x.shape[-1] // 2
    y = jnp.concatenate([rope_half(x[..., :half], ang_row), rope_half(x[..., half:], ang_col)], axis=-1)
    return y.astype(x.dtype)


def sink_softmax(scores, sink):
    m = sink
    for s in scores:
        m = jnp.maximum(m, jnp.max(s, axis=-1))
    ps = [jnp.exp(s - m[..., None]) for s in scores]
    denom = jnp.exp(sink - m)
    for p in ps:
        denom = denom + jnp.sum(p, axis=-1)
    return [p / denom[..., None] for p in ps]


def mlstm_chunk_scan(q, k, v, log_i, log_f, state, with_output):
    n_chunks = q.shape[2] // A_CHUNK

    def chunks(a):
        a = a.reshape(a.shape[:2] + (n_chunks, A_CHUNK) + a.shape[3:])
        return jnp.moveaxis(a, 2, 0)

    tri = jnp.tril(jnp.ones((A_CHUNK, A_CHUNK), dtype=bool))

    def step(carry, inp):
        C, n, m = carry
        qc, kc, vc, li, lf = inp
        b = jnp.cumsum(lf, axis=-1)
        b_end = b[..., -1]
        w_end = b_end[..., None] - b + li
        m_new = jnp.maximum(b_end + m, jnp.max(w_end, axis=-1))
        decay = jnp.exp(b_end + m - m_new)
        ws = jnp.exp(w_end - m_new[..., None])
        C_new = decay[..., None, None] * C + jnp.einsum('bhs,bhsk,bhsv->bhkv', ws, kc, vc)
        n_new = decay[..., None] * n + jnp.einsum('bhs,bhsk->bhk', ws, kc)
        if with_output:
            log_d = jnp.where(tri, b[..., :, None] - b[..., None, :] + li[..., None, :], -jnp.inf)
            inter = b + m[..., None]
            m_comb = jnp.maximum(inter, jnp.max(log_d, axis=-1))
            d_mat = jnp.exp(log_d - m_comb[..., None])
            sc = jnp.einsum('bhtk,bhsk->bhts', qc, kc) * d_mat
            a_inter = jnp.exp(inter - m_comb)
            num = jnp.einsum('bhts,bhsv->bhtv', sc, vc) + a_inter[..., None] * jnp.einsum('bhtk,bhkv->bhtv', qc, C)
            den = jnp.sum(sc, axis=-1) + a_inter * jnp.einsum('bhtk,bhk->bht', qc, n)
            h = num / jnp.maximum(jnp.abs(den), jnp.exp(-m_comb))[..., None]
        else:
            h = None
        return (C_new, n_new, m_new), h

    state, hs = lax.scan(step, state, (chunks(q), chunks(k), chunks(v), chunks(log_i), chunks(log_f)))
    if with_output:
        hs = jnp.moveaxis(hs, 0, 2).reshape(q.shape[:3] + (v.shape[-1],))
    return state, hs


def mlstm_mixer(xc, xl, w_in, w_gate, b_gate, head_g, w_out, ctx_out):
    def project(x):
        bn, L, _ = x.shape
        q, k, v, o = jnp.split(x @ w_in, [A_QK_DIM, 2 * A_QK_DIM, 2 * A_QK_DIM + A_V_DIM], axis=-1)

        def heads(a, d):
            return a.reshape(bn, L, A_HEADS, d).transpose(0, 2, 1, 3).astype(jnp.float32)

        g = (x @ w_gate + b_gate).astype(jnp.float32).transpose(0, 2, 1)
        li_f, lf_f, li_b, lf_b = jnp.split(g, 4, axis=1)
        gates = ((li_f, jax.nn.log_sigmoid(lf_f)), (li_b, jax.nn.log_sigmoid(lf_b)))
        return heads(q, A_DK) * (A_DK ** -0.5), heads(k, A_DK), heads(v, A_DV), o, gates

    def flip(a):
        return jnp.flip(a, axis=2)

    def finish(h, o):
        h = h * lax.rsqrt(jnp.mean(h * h, axis=-1, keepdims=True) + NORM_EPS)
        bn, _, L, _ = h.shape
        h = h.transpose(0, 2, 1, 3).reshape(bn, L, A_V_DIM) * head_g.astype(jnp.float32)
        return (jax.nn.sigmoid(o) * h.astype(o.dtype)) @ w_out

    qc, kc, vc, oc, gc = project(xc)
    ql, kl, vl, ol, gl = project(xl)
    bn = xc.shape[0]
    init = (jnp.zeros((bn, A_HEADS, A_DK, A_DV), jnp.float32),
            jnp.zeros((bn, A_HEADS, A_DK), jnp.float32),
            jnp.full((bn, A_HEADS), -jnp.inf, jnp.float32))
    st_f, hc_f = mlstm_chunk_scan(qc, kc, vc, gc[0][0], gc[0][1], init, ctx_out)
    st_b, hc_b = mlstm_chunk_scan(flip(qc), flip(kc), flip(vc), flip(gc[1][0]), flip(gc[1][1]), init, ctx_out)
    _, hl_f = mlstm_chunk_scan(ql, kl, vl, gl[0][0], gl[0][1], st_f, True)
    _, hl_b = mlstm_chunk_scan(flip(ql), flip(kl), flip(vl), flip(gl[1][0]), flip(gl[1][1]), st_b, True)
    yl = finish(hl_f + flip(hl_b), ol)
    yc = finish(hc_f + flip(hc_b), oc) if ctx_out else None
    return yc, yl


def swa_mixer(xc, xl, w_qkv, sinks, w_out, ang_row, ang_col, ctx_out):
    scale = B_HEAD_DIM ** -0.5

    def project(x, rotate):
        bn, L, _ = x.shape
        q, k, v = jnp.split(x @ w_qkv, [B_Q_DIM, B_Q_DIM + B_KV_DIM], axis=-1)
        q = q.reshape(bn, L, B_Q_HEADS, B_HEAD_DIM)
        k = k.reshape(bn, L, B_KV_HEADS, B_HEAD_DIM)
        v = v.reshape(bn, L, B_KV_HEADS, B_HEAD_DIM)
        if rotate:
            q = axial_rope(q, ang_row, ang_col)
            k = axial_rope(k, ang_row, ang_col)
        return q.reshape(bn, L, B_KV_HEADS, B_GROUP, B_HEAD_DIM), k, v

    sink = sinks.reshape(B_KV_HEADS, B_GROUP).astype(jnp.float32)
    qc, kc, vc = project(xc, False)
    ql, kl, vl = project(xl, True)
    bn, S = xl.shape[:2]
    nb = S // B_BLOCK
    qb = ql.reshape(bn, nb, B_BLOCK, B_KV_HEADS, B_GROUP, B_HEAD_DIM)

    def band(a):
        ap = jnp.pad(a, ((0, 0), (B_BLOCK, B_BLOCK), (0, 0), (0, 0)))
        ap = ap.reshape(bn, nb + 2, B_BLOCK, B_KV_HEADS, B_HEAD_DIM)
        return jnp.concatenate([ap[:, :-2], ap[:, 1:-1], ap[:, 2:]], axis=2)

    kb, vb = band(kl), band(vl)
    blk = jnp.arange(nb)[:, None, None]
    qpos = blk * B_BLOCK + jnp.arange(B_BLOCK)[None, :, None]
    kpos = (blk - 1) * B_BLOCK + jnp.arange(3 * B_BLOCK)[None, None, :]
    mask = (jnp.abs(qpos - kpos) <= B_WINDOW) & (kpos >= 0) & (kpos < S)
    s_loc = jnp.einsum('bnqhgd,bnkhd->bnhgqk', qb, kb).astype(jnp.float32) * scale
    s_loc = jnp.where(mask[None, :, None, None], s_loc, -jnp.inf)
    s_ctx = jnp.einsum('bnqhgd,bchd->bnhgqc', qb, kc).astype(jnp.float32) * scale
    p_loc, p_ctx = sink_softmax([s_loc, s_ctx], sink[None, None, :, :, None])
    o_l = (jnp.einsum('bnhgqk,bnkhd->bnqhgd', p_loc.astype(vb.dtype), vb)
           + jnp.einsum('bnhgqc,bchd->bnqhgd', p_ctx.astype(vc.dtype), vc))
    yl = o_l.reshape(bn, S, D_MODEL) @ w_out
    if ctx_out:
        s_c = jnp.einsum('bqhgd,bkhd->bhgqk', qc, kc).astype(jnp.float32) * scale
        (p_c,) = sink_softmax([s_c], sink[None, :, :, None])
        o_c = jnp.einsum('bhgqk,bkhd->bqhgd', p_c.astype(vc.dtype), vc)
        yc = o_c.reshape(bn, xc.shape[1], D_MODEL) @ w_out
    else:
        yc = None
    return yc, yl


def shortconv_mixer(xc, xl, w_in, conv_w, conv_b, w_out, ctx_out):
    def run(x):
        bg, cg, xt = jnp.split(x @ w_in, 3, axis=-1)
        u = cg * xt
        u = lax.conv_general_dilated(
            u, conv_w[:, None, :].astype(u.dtype), window_strides=(1,),
            padding=((C_CONV_W // 2, C_CONV_W // 2),),
            dimension_numbers=('NWC', 'WIO', 'NWC'), feature_group_count=D_MODEL) + conv_b
        return (bg * u) @ w_out

    return (run(xc) if ctx_out else None), run(xl)


def setup_inputs(seed: int = 0) -> dict:
    key = jax.random.key(seed)
    ks = jax.random.split(key, 24)
    D = D_MODEL
    f32 = jnp.float32

    def nrm(k, shape, fan_in, mult=1.0):
        return jax.random.normal(k, shape, f32) * (mult * fan_in ** -0.5)

    gate_noise = jax.random.normal(ks[9], (N_LAYERS_A, 4, A_HEADS), f32)
    gate_center = jnp.array([0.0, 3.0, 0.0, 3.0], f32)[None, :, None]
    gate_spread = jnp.array([0.1, 0.5, 0.1, 0.5], f32)[None, :, None]
    a_b_gate = (gate_center + gate_spread * gate_noise).reshape(N_LAYERS_A, 4 * A_HEADS)
    return {
        'x': jax.random.normal(ks[0], (BATCH, SEQ, D), f32),
        'c': jax.random.normal(ks[1], (BATCH, D), f32),
        'ctx': jax.random.normal(ks[2], (BATCH, CTX_LEN, D), f32),
        'c_ctx': jax.random.normal(ks[3], (D,), f32),
        'ada_w': nrm(ks[4], (DEPTH, D, 6 * D), D, 0.5),
        'ada_b': 0.02 * jax.random.normal(ks[5], (DEPTH, 6 * D), f32),
        'norm_g': 1.0 + 0.05 * jax.random.normal(ks[6], (DEPTH, 2, D), f32),
        'final_g': 1.0 + 0.05 * jax.random.normal(ks[7], (D,), f32),
        'mlp_w1': nrm(ks[8], (DEPTH, D, MLP_HIDDEN), D),
        'mlp_w2': nrm(ks[10], (DEPTH, MLP_HIDDEN, D), MLP_HIDDEN),
        'a_w_in': nrm(ks[11], (N_LAYERS_A, D, 2 * A_QK_DIM + 2 * A_V_DIM), D),
        'a_w_gate': nrm(ks[12], (N_LAYERS_A, D, 4 * A_HEADS), D, 0.5),
        'a_b_gate': a_b_gate,
        'a_head_g': 1.0 + 0.05 * jax.random.normal(ks[13], (N_LAYERS_A, A_V_DIM), f32),
        'a_w_out': nrm(ks[14], (N_LAYERS_A, A_V_DIM, D), A_V_DIM),
        'b_w_qkv': nrm(ks[15], (N_LAYERS_B, D, B_Q_DIM + 2 * B_KV_DIM), D),
        'b_sinks': 0.5 * jax.random.normal(ks[16], (N_LAYERS_B, B_Q_HEADS), f32),
        'b_w_out': nrm(ks[17], (N_LAYERS_B, B_Q_DIM, D), B_Q_DIM),
        'c_w_in': nrm(ks[18], (N_LAYERS_C, D, 3 * D), D),
        'c_conv_w': nrm(ks[19], (N_LAYERS_C, C_CONV_W, D), C_CONV_W),
        'c_conv_b': 0.02 * jax.random.normal(ks[20], (N_LAYERS_C, D), f32),
        'c_w_out': nrm(ks[21], (N_LAYERS_C, D, D), D),
    }


def reference(x, c, ctx, c_ctx, ada_w, ada_b, norm_g, final_g, mlp_w1, mlp_w2,
              a_w_in, a_w_gate, a_b_gate, a_head_g, a_w_out,
              b_w_qkv, b_sinks, b_w_out, c_w_in, c_conv_w, c_conv_b, c_w_out):
    rows = x.shape[1] // GRID_W
    ang_row, ang_col = axial_angles(rows)
    silu_c = jax.nn.silu(c)
    silu_cc = jax.nn.silu(c_ctx)
    h, hc = x, ctx
    for l in range(DEPTH):
        kind, j = l % N_MIXERS, l // N_MIXERS
        ctx_out = l != DEPTH - 1
        ml = jnp.split(silu_c @ ada_w[l] + ada_b[l], 6, axis=-1)
        mc = jnp.split(silu_cc @ ada_w[l] + ada_b[l], 6, axis=-1)
        xl = modulate(rmsnorm(h, norm_g[l, 0]), ml[0][:, None], ml[1][:, None])
        need_ctx_input = ctx_out or kind != 2
        xc = modulate(rmsnorm(hc, norm_g[l, 0]), mc[0], mc[1]) if need_ctx_input else None
        if kind == 0:
            yc, yl = mlstm_mixer(xc, xl, a_w_in[j], a_w_gate[j], a_b_gate[j], a_head_g[j], a_w_out[j], ctx_out)
        elif kind == 1:
            yc, yl = swa_mixer(xc, xl, b_w_qkv[j], b_sinks[j], b_w_out[j], ang_row, ang_col, ctx_out)
        else:
            yc, yl = shortconv_mixer(xc, xl, c_w_in[j], c_conv_w[j], c_conv_b[j], c_w_out[j], ctx_out)
        h = h + ml[2][:, None] * yl
        xl = modulate(rmsnorm(h, norm_g[l, 1]), ml[3][:, None], ml[4][:, None])
        h = h + ml[5][:, None] * sq_relu_mlp(xl, mlp_w1[l], mlp_w2[l])
        if ctx_out:
            hc = hc + mc[2] * yc
            xc = modulate(rmsnorm(hc, norm_g[l, 1]), mc[3], mc[4])
            hc = hc + mc[5] * sq_relu_mlp(xc, mlp_w1[l], mlp_w2[l])
    return rmsnorm(h, final_g)
```

```python
import numpy as np
import ml_dtypes
from contextlib import ExitStack
import concourse.bass as bass
import concourse.mybir as mybir
from concourse.bass_utils import run_bass_kernel_spmd

F32 = mybir.dt.float32
BF16 = mybir.dt.bfloat16
AF = mybir.ActivationFunctionType
ALU = mybir.AluOpType
AX = mybir.AxisListType

import os
DBG = os.environ.get('K_DBG', '')
D = 1024
KC = 8
CTX = 256
SEQ = 8192
EPS = 1e-6


class _Op:
    __slots__ = ("eng", "fn", "deps", "key", "tl", "seq", "clock", "signal", "waits", "idx")


class Prog:
    ENG = ("pe", "act", "dve", "pool", "sp")

    def __init__(self, nc):
        self.nc = nc
        self.ops = []
        self.lastw = {}
        self.readers = {}
        self.bar = None
        self.bar_idx = 0
        self.keymap = {}

    def add(self, eng, fn, reads=(), writes=(), dma=None, cc=False):
        if dma is not None:
            cls = "cc" if cc else ("sw" if eng == "pool" else "hw")
            km = self.keymap.setdefault(cls, {})
            dma = (cls, km.setdefault(dma, len(km)))
        op = _Op()
        op.eng = eng
        op.fn = fn
        op.key = dma
        op.signal = dma is not None
        op.idx = len(self.ops)
        deps = set()
        for t in reads:
            w = self.lastw.get(t)
            if w is not None:
                deps.add(w)
        for t in writes:
            w = self.lastw.get(t)
            if w is not None:
                deps.add(w)
            for r in self.readers.get(t, ()):
                deps.add(r)
        for t in writes:
            self.lastw[t] = op
            self.readers[t] = []
        for t in reads:
            self.readers.setdefault(t, []).append(op)
        if self.bar is not None:
            deps.add(self.bar)
        deps.discard(op)
        op.deps = deps
        self.ops.append(op)
        return op

    def barrier(self, dummy):
        last = {}
        for o in self.ops[self.bar_idx:]:
            last[("dma", o.key) if o.key is not None else o.eng] = o
        op = self.add("dve", lambda h: h.memset(dummy[:], 0.0), writes=["__bar"])
        op.deps |= set(last.values())
        op.deps.discard(op)
        self.bar = op
        self.bar_idx = len(self.ops)
        self.keymap = {}

    def _plan(self):
        def skip(d, op):
            return d.key is None and d.eng == "pe" and op.eng == "pe" and op.key is None

        for op in self.ops:
            for d in op.deps:
                if not skip(d, op):
                    d.signal = True
        seqs = {}
        for op in self.ops:
            op.tl = ("dma", op.key) if op.key is not None else op.eng
            if op.signal:
                seqs[op.tl] = seqs.get(op.tl, 0) + 1
            op.seq = seqs.get(op.tl, 0)
        self.final = dict(seqs)
        eng_clock = {e: {} for e in self.ENG}
        for op in self.ops:
            ck = eng_clock[op.eng]
            need = {}
            for d in op.deps:
                if skip(d, op):
                    continue
                if ck.get(d.tl, 0) >= d.seq:
                    continue
                if need.get(d.tl, 0) < d.seq:
                    need[d.tl] = d.seq
            for d in op.deps:
                if skip(d, op):
                    continue
                for tl, s in d.clock.items():
                    if ck.get(tl, 0) < s:
                        ck[tl] = s
            for tl, s in need.items():
                if ck.get(tl, 0) < s:
                    ck[tl] = s
            op.waits = list(need.items())
            c = dict(ck)
            if op.signal:
                c[op.tl] = op.seq
            op.clock = c
            op.deps = None

    def emit(self):
        self._plan()
        nc = self.nc
        with ExitStack() as st:
            sems = {}
            for i, tl in enumerate(self.final):
                sems[tl] = st.enter_context(nc.semaphore("sm%d" % i))
            block = st.enter_context(nc.Block())
            per = {e: [o for o in self.ops if o.eng == e] for e in self.ENG}
            final = self.final

            def val(tl, s):
                return s if (isinstance(tl, str) or tl[1][0] == "cc") else s * 16

            def run(e, h):
                for o in per[e]:
                    for tl, s in o.waits:
                        h.wait_ge(sems[tl], val(tl, s))
                    ins = o.fn(h)
                    if o.signal:
                        if o.key is not None and o.key[0] == "cc":
                            ins.then_inc(sems[o.tl])
                        else:
                            ins.then_inc(sems[o.tl], 16 if o.key is not None else 1)
                if e == "sp":
                    for tl, s in final.items():
                        h.wait_ge(sems[tl], val(tl, s))

            @block.tensor
            def _(h):
                run("pe", h)

            @block.scalar
            def _(h):
                run("act", h)

            @block.vector
            def _(h):
                run("dve", h)

            @block.gpsimd
            def _(h):
                run("pool", h)

            @block.sync
            def _(h):
                run("sp", h)
        return dict(n_ops=len(self.ops), n_sems=len(self.final))


def build_program(LAT=SEQ // 4, layers=(0, 1, 2, 3), final=True, stop_after_mixer=False, R=4):
    T = CTX + LAT
    RG = [list(range(g * R, (g + 1) * R)) for g in range(8 // R)] if R > 1 else None
    NCK = T // 128
    groups = [(0, CTX)] + [(CTX + i * 512, 512) for i in range(LAT // 512)]
    nc = bass.Bass("TRN2", target_bir_lowering=False)

    def din(name, shape, dt=F32):
        return nc.dram_tensor(name, list(shape), dt, kind="ExternalInput").ap()

    def dscr(name, shape, dt):
        return nc.dram_tensor(name, list(shape), dt, kind="Internal").ap()

    xT = din("xT", [D, T])
    cvec = din("cvec", [128, 16])
    ada_w = din("ada_w", [4 if R == 1 else 1, D, 6 * D])
    ada_bT = din("ada_bT", [128, (4 if R == 1 else 1) * 48])
    norm_gT = din("norm_gT", [128, 64])
    final_gT = din("final_gT", [128, 8])
    mlp_w1 = din("mlp_w1", [4, D, 4 * D])
    mlp_w2 = din("mlp_w2", [4, 4 * D, D])
    a_w_in = din("a_w_in", [2, D, 3 * D])
    a_w_gate = din("a_w_gate", [2, D, 32])
    a_b_gate_r = din("a_b_gate_r", [128, 64])
    a_head_g_r = din("a_head_g_r", [128, 2 * D])
    a_w_out = din("a_w_out", [2, D, D])
    b_w_qkv = din("b_w_qkv", [1, D, 1536])
    b_w_sw = din("b_w_sw", [1, D, 1280])
    b_sinks_r = din("b_sinks_r", [128, 16])
    b_w_out = din("b_w_out", [1, D, D])
    ropeC = din("ropeC", [128, LAT])
    ropeS = din("ropeS", [128, LAT])
    c_w_in = din("c_w_in", [1, D, 3 * D])
    c_convT = din("c_convT", [128, 32])
    c_w_out = din("c_w_out", [1, D, D])
    ident_in = din("ident", [128, 128], BF16)
    tri_in = din("tri", [128, 256])
    msk_in = din("msk", [128, 1024], BF16)
    mske_in = din("mske", [128, 1024], BF16)
    sel_in = din("sel", [128, 16])
    if final:
        outT = nc.dram_tensor("outT", [D, LAT], F32, kind="ExternalOutput").ap()
    else:
        outT = nc.dram_tensor("outT", [D, T], F32, kind="ExternalOutput").ap()

    hT = dscr("hT", [D, T], F32)
    xnS = dscr("xnS", [D, T], BF16)
    UW = 1 + CTX + 1 + 1 + LAT + 1
    uS = dscr("uS", [D, UW], F32)
    bgS = dscr("bgS", [D, T], F32)
    qS = dscr("qS", [D, T], BF16)
    kS = dscr("kS", [512, T], BF16)
    vS = dscr("vS", [T, 256], BF16)
    aqS = dscr("aqS", [64, 8 * T], BF16)
    opS = dscr("opS", [T, 8 * 258], F32)
    sgS = dscr("sgS", [T, D], BF16)
    eiS = dscr("eiS", [T, 16], F32)
    etS = dscr("etS", [NCK * 64, 16], F32)
    usS = dscr("usS", [NCK * 64, 8 * 258], F32)
    csS = dscr("csS", [2 * NCK * 64, 8 * 129], BF16)
    RR = max(R, 1)
    xcI = dscr("xcI", [2, D], F32)
    xcO = dscr("xcO", [2 * RR, D], F32)
    xkI = dscr("xkI", [1024, 128], BF16)
    xkO = dscr("xkO", [1024 * RR, 128], BF16)
    xvI = dscr("xvI", [256, 256], BF16)
    xvO = dscr("xvO", [256 * RR, 256], BF16)
    XW = 2 * 8 * 129 + 16
    xaI = dscr("xaI", [64, XW], F32)
    xaO = dscr("xaO", [64 * RR, XW], F32)
    xmI = dscr("xmI", [128, 96], F32)
    xmO = dscr("xmO", [128 * RR, 96], F32)

    hTv = hT.rearrange("(c p) t -> p c t", p=128)
    xTv = xT.rearrange("(c p) t -> p c t", p=128)
    xnSv = xnS.rearrange("(c p) t -> p c t", p=128)
    uSv = uS.rearrange("(c p) t -> p c t", p=128)
    bgSv = bgS.rearrange("(c p) t -> p c t", p=128)
    qSv = qS.rearrange("(c p) t -> p c t", p=128)
    kSv = kS.rearrange("(c p) t -> p c t", p=128)
    outTv = outT.rearrange("(c p) t -> p c t", p=128)

    _cnt = [0]

    def sbuf_u(name, shape, dt):
        _cnt[0] += 1
        return nc.sbuf_tensor("%s_%d" % (name, _cnt[0]), shape, dt)

    P = Prog(nc)
    with ExitStack() as gst:
        def sb(name, shape, dt):
            return gst.enter_context(sbuf_u(name, list(shape), dt))

        ps = [gst.enter_context(nc.psum_tensor("ps%d" % i, [128, 512], F32)) for i in range(7)]
        psb = gst.enter_context(nc.psum_tensor("psb", [128, 1024], BF16))
        dummy = sb("dummyt", [128, 8], F32)
        modt = sb("modt", [128, 4 * 96], F32)
        gst_ = sb("gst", [128, 4 * 32], F32)
        ngt = sb("ngt", [128, 64], F32)
        fgt = sb("fgt", [128, 8], F32)
        adab = sb("adab", [128, 192], F32)
        cs_t = sb("cs_t", [128, 16], F32)
        ones_bf = sb("ones_bf", [128, 128], BF16)
        ones_f = sb("ones_f", [128, 128], F32)
        epsb = sb("epsb", [128, 1], F32)
        oneb = sb("oneb", [128, 1], F32)
        ident = sb("ident_sb", [128, 128], BF16)
        tri = sb("tri_sb", [128, 256], F32)
        msk = sb("msk_sb", [128, 1024], BF16)
        mske = sb("mske_sb", [128, 1024], BF16)
        selt = sb("sel_sb", [128, 16], F32)

        def PS(i):
            return ("ps", i)

        P.add("dve", lambda h: h.memset(ones_bf[:], 1.0), writes=["ones_bf"])
        P.add("dve", lambda h: h.memset(ones_f[:], 1.0), writes=["ones_f"])
        P.add("dve", lambda h: h.memset(epsb[:], EPS), writes=["epsb"])
        P.add("dve", lambda h: h.memset(oneb[:], 1.0), writes=["oneb"])
        for nm, dst, src in (("ngt", ngt, norm_gT), ("fgt", fgt, final_gT), ("adab", adab[:, 0:(192 if R == 1 else 48)], ada_bT),
                             ("cs_t", cs_t, cvec), ("ident", ident, ident_in), ("tri", tri, tri_in),
                             ("msk", msk, msk_in), ("mske", mske, mske_in), ("sel", selt, sel_in)):
            P.add("sp", lambda h, dst=dst, src=src: h.dma_start(out=(dst if nm == "adab" else dst[:]), in_=src[:, :]),
                  writes=[nm], dma="c_" + nm)
        for gi, (s0, n) in enumerate(groups):
            P.add("sp", lambda h, s0=s0, n=n: h.dma_start(out=hT[:, s0:s0 + n], in_=xT[:, s0:s0 + n]),
                  writes=[("hT", gi)], dma="cp%d" % (gi % 4))

        def allgather(name, src, dst):
            P.add("pool", lambda h: h.collective_compute("AllGather", ALU.bypass, replica_groups=RG, ins=[src], outs=[dst]),
                  reads=[("xi", name)], writes=[("xo", name)], dma="cc_" + name, cc=True)

        with ExitStack() as st:
            awt = [st.enter_context(sbuf_u("awt%d" % i, [128, 8, 1024], F32)) for i in range(2)]
            silu = st.enter_context(sbuf_u("silu", [128, 16], F32))
            P.add("act", lambda h: h.activation(out=silu[:], in_=cs_t[:], func=AF.Silu),
                  reads=["cs_t"], writes=["silu"])
            it = 0
            if R > 1:
                modq = st.enter_context(sbuf_u("modq", [128, 96], F32))
                ada_layers, dst_of = [0], (lambda l, c: modq[:, c:c + 8])
            else:
                ada_layers, dst_of = list(layers), (lambda l, c: modt[:, l * 96 + c:l * 96 + c + 8])
            for l in ada_layers:
                for v in range(6):
                    b = it % 2
                    it += 1
                    src = ada_w[l].rearrange("(c p) n -> p c n", p=128)[:, :, v * 1024:(v + 1) * 1024]
                    P.add("sp", lambda h, b=b, src=src: h.dma_start(out=awt[b][:], in_=src),
                          writes=[("awt", b)], dma="awt%d" % b)
                    pa = b
                    for j in range(8):
                        for k in range(8):
                            P.add("pe", lambda h, b=b, j=j, k=k, pa=pa: h.matmul(
                                ps[pa][:, 2 * j:2 * j + 2], awt[b][:, k, j * 128:(j + 1) * 128],
                                silu[:, k:k + 9:8], start=(k == 0), stop=(k == 7)),
                                reads=[("awt", b), "silu"], writes=[PS(pa)])
                    for w in range(2):
                        dst = dst_of(l, (v * 2 + w) * 8)
                        P.add("dve", lambda h, dst=dst, w=w, l=l, v=v, pa=pa: h.tensor_tensor(
                            out=dst, in0=ps[pa][:, w:16:2],
                            in1=adab[:, l * 48 + v * 8:l * 48 + v * 8 + 8], op=ALU.add),
                            reads=[PS(pa), "adab"], writes=["modt"])
            if R > 1:
                P.add("sp", lambda h: h.dma_start(out=xmI[:, :], in_=modq[:]), reads=["modt"], writes=[("xi", "m")], dma="xmi")
                allgather("m", xmI, xmO)
                P.add("sp", lambda h: h.dma_start(out=modt[:, :].rearrange("p (l c) -> p l c", l=4),
                                                  in_=xmO.rearrange("(l p) c -> p l c", p=128)),
                      reads=[("xo", "m")], writes=["modt"], dma="xmo")
            for l in (range(4) if R > 1 else layers):
                for i, v in ((0, 1), (1, 4)):
                    for w in range(2):
                        col = l * 96 + (v * 2 + w) * 8
                        gcol = l * 32 + (i * 2 + w) * 8
                        P.add("dve", lambda h, col=col, gcol=gcol, l=l, i=i: h.scalar_tensor_tensor(
                            out=gst_[:, gcol:gcol + 8], in0=modt[:, col:col + 8], scalar=1.0,
                            in1=ngt[:, l * 16 + i * 8:l * 16 + i * 8 + 8], op0=ALU.add, op1=ALU.mult),
                            reads=["modt", "ngt"], writes=["gst"])
            P.barrier(dummy)

        def mcol(l, v, w):
            return l * 96 + (v * 2 + w) * 8

        def gcol(l, i, w):
            return l * 32 + (i * 2 + w) * 8

        def norm_mod(ht, htok, n, xn, xntok, gs0, sh0, W):
            sq, tmp, rstd, pi = W["sq"], W["tmp"], W["rstd"], W["psn"]
            P.add("act", lambda h: h.activation(out=sq[:, :, :n], in_=ht[:, :, :n], func=AF.Square),
                  reads=[htok], writes=["sq"])
            for c in range(KC):
                P.add("pe", lambda h, c=c: h.matmul(ps[pi][:, :n], ones_bf[:], sq[:, c, :n],
                                                   start=(c == 0), stop=(c == KC - 1)),
                      reads=["sq", "ones_bf"], writes=[PS(pi)])
            P.add("act", lambda h: h.activation(out=rstd[:, :n], in_=ps[pi][:, :n], func=AF.Sqrt,
                                                bias=epsb[:, 0:1], scale=1.0 / D),
                  reads=[PS(pi), "epsb"], writes=["rstd"])
            P.add("dve", lambda h: h.reciprocal(out=rstd[:, :n], in_=rstd[:, :n]),
                  reads=["rstd"], writes=["rstd"])
            for c in range(KC):
                P.add("dve", lambda h, c=c: h.scalar_tensor_tensor(
                    out=tmp[:, c, :n], in0=ht[:, c, :n], scalar=gst_[:, gs0 + c:gs0 + c + 1],
                    in1=rstd[:, :n], op0=ALU.mult, op1=ALU.mult),
                    reads=[htok, "gst", "rstd"], writes=[("tmp", c)])
                if sh0 is None:
                    continue
                P.add("act", lambda h, c=c: h.activation(
                    out=xn[:, c, :n], in_=tmp[:, c, :n], func=AF.Identity,
                    bias=modt[:, sh0 + c:sh0 + c + 1], scale=1.0),
                    reads=[("tmp", c), "modt"], writes=[xntok])

        def load_h(ht, tok, key, gi):
            s0, n = groups[gi]
            P.add("sp", lambda h: h.dma_start(out=ht[:, :, :n], in_=hTv[:, :, s0:s0 + n]),
                  reads=[("hT", gi)], writes=[tok], dma=key)

        def store_h(ht, tok, key, gi, q="sp"):
            s0, n = groups[gi]
            P.add(q, lambda h: h.dma_start(out=hTv[:, :, s0:s0 + n], in_=ht[:, :, :n]),
                  reads=[tok], writes=[("hT", gi)], dma=key)

        def load_w(wt, tok, key, src):
            P.add("pool", lambda h: h.dma_start(out=wt, in_=src), writes=[tok], dma=key)

        def resid(pi, ht, htok, oc, n, gate_col):
            P.add("dve", lambda h: h.scalar_tensor_tensor(
                out=ht[:, oc, :n], in0=ps[pi][:, :n], scalar=modt[:, gate_col + oc:gate_col + oc + 1],
                in1=ht[:, oc, :n], op0=ALU.mult, op1=ALU.add),
                reads=[PS(pi), htok, "modt"], writes=[htok])

        def mlp_phase(l):
            with ExitStack() as st:
                def sbt(name, shape, dt):
                    return st.enter_context(sbuf_u(name, list(shape), dt))
                w1s = [sbt("w1s%d" % i, [128, 8, 1024], BF16) for i in range(2)]
                w2s = [sbt("w2s%d" % i, [128, 8, 1024], BF16) for i in range(2)]
                hts = [sbt("mht%d" % i, [128, 8, 512], F32) for i in range(2)]
                xns = [sbt("mxn%d" % i, [128, 8, 512], BF16) for i in range(2)]
                hid = sbt("mhid", [128, 8, 512], BF16)
                rl = [sbt("mrl%d" % i, [128, 512], BF16) for i in range(2)]
                W = dict(sq=sbt("msq", [128, 8, 512], BF16), tmp=sbt("mtmp", [128, 8, 512], F32),
                         rstd=sbt("mrstd", [128, 512], F32), psn=6)
                gl = [gi for gi in range(len(groups)) if not (l == 3 and gi == 0)]
                w1v = mlp_w1[l].rearrange("(c p) n -> p c n", p=128)
                w2v = mlp_w2[l].rearrange("(c p) n -> p c n", p=128)

                def load_slab(s):
                    b = s % 2
                    load_w(w1s[b][:], ("w1s", b), "w1s%d" % b, w1v[:, :, s * 1024:(s + 1) * 1024])
                    load_w(w2s[b][:], ("w2s", b), "w2s%d" % b, w2v[:, s * 8:(s + 1) * 8, :])

                load_slab(0)
                iters = [(s_, gi) for s_ in range(4) for gi in gl]

                def prep(it):
                    s_, gi = iters[it]
                    s0, n = groups[gi]
                    lw = 1 if gi == 0 else 0
                    ht, xn = hts[it % 2], xns[it % 2]
                    htok, xntok = ("mht", it % 2), ("mxn", it % 2)
                    load_h(ht, htok, "mldh%d" % (it % 2), gi)
                    if s_ == 0:
                        norm_mod(ht, htok, n, xn, xntok, gcol(l, 1, lw), mcol(l, 3, lw), W)
                        P.add("pool", lambda h: h.dma_start(out=xnSv[:, :, s0:s0 + n], in_=xn[:, :, :n]),
                              reads=[xntok], writes=[("xnS", gi)], dma="mstx%d" % (it % 2))
                    else:
                        P.add("sp", lambda h: h.dma_start(out=xn[:, :, :n], in_=xnSv[:, :, s0:s0 + n]),
                              reads=[("xnS", gi)], writes=[xntok], dma="mldx%d" % (it % 2))

                def compute(it):
                    s_, gi = iters[it]
                    b = s_ % 2
                    s0, n = groups[gi]
                    lw = 1 if gi == 0 else 0
                    ht, xn = hts[it % 2], xns[it % 2]
                    htok, xntok = ("mht", it % 2), ("mxn", it % 2)
                    if gi == gl[0] and s_ + 1 < 4:
                        load_slab(s_ + 1)
                    for hc in range(8):
                        pi = hc % 3
                        for k in range(KC):
                            P.add("pe", lambda h, pi=pi, hc=hc, k=k: h.matmul(
                                ps[pi][:, :n], w1s[b][:, k, hc * 128:(hc + 1) * 128], xn[:, k, :n],
                                start=(k == 0), stop=(k == KC - 1)),
                                reads=[("w1s", b), xntok], writes=[PS(pi)])
                        r = rl[hc % 2]
                        P.add("act", lambda h, pi=pi, r=r: h.activation(
                            out=r[:, :n], in_=ps[pi][:, :n], func=AF.Relu),
                            reads=[PS(pi)], writes=[("mrl", hc % 2)])
                        P.add("dve", lambda h, pi=pi, r=r, hc=hc: h.tensor_tensor(
                            out=hid[:, hc, :n], in0=ps[pi][:, :n], in1=r[:, :n], op=ALU.mult),
                            reads=[PS(pi), ("mrl", hc % 2)], writes=[("mhid", hc)])
                    for oc in range(8):
                        pi = 3 + oc % 3
                        for hc in range(8):
                            P.add("pe", lambda h, pi=pi, hc=hc, oc=oc: h.matmul(
                                ps[pi][:, :n], w2s[b][:, hc, oc * 128:(oc + 1) * 128], hid[:, hc, :n],
                                start=(hc == 0), stop=(hc == 7)),
                                reads=[("w2s", b), ("mhid", hc)], writes=[PS(pi)])
                        resid(pi, ht, htok, oc, n, mcol(l, 5, lw))
                    store_h(ht, htok, "msth%d" % (it % 2), gi, "pool")

                prep(0)
                for it in range(len(iters)):
                    ahead = it + 1 < len(iters) and iters[it + 1][1] != iters[it][1]
                    if ahead:
                        prep(it + 1)
                    compute(it)
                    if it + 1 < len(iters) and not ahead:
                        prep(it + 1)
                P.barrier(dummy)

        def conv_phase(l, j):
            ctx_out = l != 3
            with ExitStack() as st:
                def sbt(name, shape, dt):
                    return st.enter_context(sbuf_u(name, list(shape), dt))
                win = sbt("cwin", [128, 8, 3072], BF16)
                wout = sbt("cwout", [128, 8, 1024], BF16)
                cvt = sbt("cvt", [128, 32], F32)
                hts = [sbt("cht%d" % i, [128, 8, 512], F32) for i in range(2)]
                xn = sbt("cxn", [128, 8, 512], BF16)
                ut = sbt("cut", [128, 8, 514], F32)
                bgt = sbt("cbg", [128, 8, 512], F32)
                cgt = sbt("ccg", [128, 512], F32)
                zt = sbt("czt", [128, 8, 512], BF16)
                acc = sbt("cacc", [128, 512], F32)
                zero = sbt("czero", [128, 8], F32)
                W = dict(sq=sbt("csq", [128, 8, 512], BF16), tmp=sbt("ctmp", [128, 8, 512], F32),
                         rstd=sbt("crstd", [128, 512], F32), psn=6)
                load_w(win[:], "cwin", "cwin", c_w_in[j].rearrange("(c p) n -> p c n", p=128))
                load_w(wout[:], "cwout", "cwout", c_w_out[j].rearrange("(c p) n -> p c n", p=128))
                P.add("sp", lambda h: h.dma_start(out=cvt[:], in_=c_convT[:, :]), writes=["cvt"], dma="cvt")
                P.add("dve", lambda h: h.memset(zero[:], 0.0), writes=["czero"])
                gl = [gi for gi in range(len(groups)) if ctx_out or gi > 0]

                def ucol(gi):
                    s0, n = groups[gi]
                    return (1 + s0) if gi == 0 else (3 + s0)
                pads = (0, 1 + CTX, 2 + CTX, UW - 1) if R == 1 else (0, 1 + CTX)
                for ci, col in enumerate(pads):
                    P.add("sp", lambda h, col=col: h.dma_start(out=uSv[:, :, col:col + 1], in_=zero[:, :].rearrange("p (c o) -> p c o", o=1), allow_slow_non_contiguous=True),
                          reads=["czero"], writes=[("uSpad", ci)], dma="cpad%d" % ci)
                edge = [sbt("cedge%d" % i, [128, 8], F32) for i in range(2)]
                xg = sbt("cxg", [128, 2 * RR, 8], F32)
                halo = [sbt("chalo%d" % i, [128, 8], F32) for i in range(2)]
                for it, gi in enumerate(gl):
                    s0, n = groups[gi]
                    lw = 1 if gi == 0 else 0
                    ht, htok = hts[it % 2], ("cht", it % 2)
                    load_h(ht, htok, "cldh%d" % (it % 2), gi)
                    norm_mod(ht, htok, n, xn, "cxn", gcol(l, 0, lw), mcol(l, 0, lw), W)
                    for c in range(8):
                        for which, off, pi in (("bg", 0, 0), ("cg", 1024, 1), ("xt", 2048, 2)):
                            for k in range(KC):
                                P.add("pe", lambda h, pi=pi, off=off, c=c, k=k, n=n: h.matmul(
                                    ps[pi][:, :n], win[:, k, off + c * 128:off + (c + 1) * 128], xn[:, k, :n],
                                    start=(k == 0), stop=(k == KC - 1)),
                                    reads=["cwin", "cxn"], writes=[PS(pi)])
                        P.add("act", lambda h, c=c, n=n: h.activation(out=bgt[:, c, :n], in_=ps[0][:, :n], func=AF.Copy),
                              reads=[PS(0)], writes=["cbg"])
                        P.add("act", lambda h, n=n: h.activation(out=cgt[:, :n], in_=ps[1][:, :n], func=AF.Copy),
                              reads=[PS(1)], writes=["ccg"])
                        P.add("dve", lambda h, c=c, n=n: h.tensor_tensor(out=ut[:, c, :n], in0=ps[2][:, :n], in1=cgt[:, :n], op=ALU.mult),
                              reads=[PS(2), "ccg"], writes=["cut"])
                    uc = ucol(gi)
                    if R > 1 and gi == 1:
                        P.add("dve", lambda h: h.tensor_copy(out=edge[0][:], in_=ut[:, :, 0]), reads=["cut"], writes=[("cedge", 0)])
                    if R > 1 and gi == len(groups) - 1:
                        P.add("dve", lambda h, n=n: h.tensor_copy(out=edge[1][:], in_=ut[:, :, n - 1]), reads=["cut"], writes=[("cedge", 1)])
                    P.add("pool", lambda h, uc=uc, n=n: h.dma_start(out=uSv[:, :, uc:uc + n], in_=ut[:, :, :n]),
                          reads=["cut"], writes=[("uS", gi)], dma="cstu")
                    P.add("pool", lambda h, s0=s0, n=n: h.dma_start(out=bgSv[:, :, s0:s0 + n], in_=bgt[:, :, :n]),
                          reads=["cbg"], writes=[("bgS", gi)], dma="cstb")
                if R > 1:
                    for i in range(2):
                        P.add("sp", lambda h, i=i: h.dma_start(out=xcI[i, :].rearrange("(p c) -> p c", c=8), in_=edge[i][:]),
                              reads=[("cedge", i)], writes=[("xi", "c")], dma="cxi%d" % i)
                    allgather("c", xcI, xcO)
                    P.add("sp", lambda h: h.dma_start(out=xg[:], in_=xcO.rearrange("r (p c) -> p r c", c=8)),
                          reads=[("xo", "c")], writes=["cxg"], dma="cxo")
                    for side, (selo, rowo) in enumerate(((0, 1), (4, 0))):
                        for i in range(R):
                            if i == 0:
                                P.add("dve", lambda h, side=side, selo=selo, rowo=rowo, i=i: h.tensor_scalar(
                                    out=halo[side][:], in0=xg[:, 2 * i + rowo, :], scalar1=selt[:, selo + i:selo + i + 1], scalar2=None, op0=ALU.mult),
                                    reads=["cxg", "sel"], writes=[("chalo", side)])
                            else:
                                P.add("dve", lambda h, side=side, selo=selo, rowo=rowo, i=i: h.scalar_tensor_tensor(
                                    out=halo[side][:], in0=xg[:, 2 * i + rowo, :], scalar=selt[:, selo + i:selo + i + 1], in1=halo[side][:],
                                    op0=ALU.mult, op1=ALU.add), reads=["cxg", "sel", ("chalo", side)], writes=[("chalo", side)])
                        col = (2 + CTX, UW - 1)[side]
                        P.add("sp", lambda h, side=side, col=col: h.dma_start(
                            out=uSv[:, :, col:col + 1], in_=halo[side][:, :].rearrange("p (c o) -> p c o", o=1), allow_slow_non_contiguous=True),
                            reads=[("chalo", side)], writes=[("uSpad", 2 + side)], dma="cpad%d" % (2 + side))
                for it, gi in enumerate(gl):
                    s0, n = groups[gi]
                    lw = 1 if gi == 0 else 0
                    ht, htok = hts[it % 2], ("cht", it % 2)
                    load_h(ht, htok, "cldh%d" % (it % 2), gi)
                    uc = ucol(gi)
                    rd = [("uS", g2) for g2 in gl] + [("uSpad", i) for i in range(4)]
                    P.add("sp", lambda h, uc=uc, n=n: h.dma_start(out=ut[:, :, :n + 2], in_=uSv[:, :, uc - 1:uc + n + 1]),
                          reads=rd, writes=["cut"], dma="cldu")
                    P.add("sp", lambda h, s0=s0, n=n: h.dma_start(out=bgt[:, :, :n], in_=bgSv[:, :, s0:s0 + n]),
                          reads=[("bgS", gi)], writes=["cbg"], dma="cldb")
                    for c in range(8):
                        P.add("dve", lambda h, c=c, n=n: h.tensor_scalar(
                            out=acc[:, :n], in0=ut[:, c, 1:n + 1], scalar1=cvt[:, 8 + c:9 + c], scalar2=cvt[:, 24 + c:25 + c],
                            op0=ALU.mult, op1=ALU.add), reads=["cut", "cvt"], writes=["cacc"])
                        P.add("dve", lambda h, c=c, n=n: h.scalar_tensor_tensor(
                            out=acc[:, :n], in0=ut[:, c, 0:n], scalar=cvt[:, c:c + 1], in1=acc[:, :n],
                            op0=ALU.mult, op1=ALU.add), reads=["cut", "cvt", "cacc"], writes=["cacc"])
                        P.add("dve", lambda h, c=c, n=n: h.scalar_tensor_tensor(
                            out=acc[:, :n], in0=ut[:, c, 2:n + 2], scalar=cvt[:, 16 + c:17 + c], in1=acc[:, :n],
                            op0=ALU.mult, op1=ALU.add), reads=["cut", "cvt", "cacc"], writes=["cacc"])
                        P.add("dve", lambda h, c=c, n=n: h.tensor_tensor(
                            out=zt[:, c, :n], in0=acc[:, :n], in1=bgt[:, c, :n], op=ALU.mult),
                            reads=["cacc", "cbg"], writes=["czt"])
                    for oc in range(8):
                        pi = oc % 3
                        for k in range(KC):
                            P.add("pe", lambda h, pi=pi, oc=oc, k=k, n=n: h.matmul(
                                ps[pi][:, :n], wout[:, k, oc * 128:(oc + 1) * 128], zt[:, k, :n],
                                start=(k == 0), stop=(k == KC - 1)), reads=["cwout", "czt"], writes=[PS(pi)])
                        resid(pi, ht, htok, oc, n, mcol(l, 2, lw))
                    store_h(ht, htok, "csth%d" % (it % 2), gi, "pool")
                P.barrier(dummy)

        def swa_phase(l, j):
            ctx_out = l != 3
            NB = LAT // 128
            with ExitStack() as st:
                def sbt(name, shape, dt):
                    return st.enter_context(sbuf_u(name, list(shape), dt))
                with ExitStack() as st1:
                    def sb1(name, shape, dt):
                        return st1.enter_context(sbuf_u(name, list(shape), dt))
                    wq = sb1("bwq", [128, 8, 1024], BF16)
                    wqs = sb1("bwqs", [128, 8, 1024], BF16)
                    wk = sb1("bwk", [128, 8, 512], BF16)
                    wks = sb1("bwks", [128, 8, 512], BF16)
                    wv = sb1("bwv", [128, 8, 256], BF16)
                    rc = sb1("brc", [128, 512], F32)
                    rs = sb1("brs", [128, 512], F32)
                    hts = [sb1("bht%d" % i, [128, 8, 512], F32) for i in range(2)]
                    xn = sb1("bxn", [128, 8, 512], BF16)
                    qt = sb1("bqt", [128, 8, 512], BF16)
                    kt = sb1("bkt", [128, 4, 512], BF16)
                    vt = sb1("bvt", [128, 4, 256], BF16)
                    t1 = sb1("bt1", [128, 512], F32)
                    t2 = sb1("bt2", [128, 512], F32)
                    W = dict(sq=sb1("bsq", [128, 8, 512], BF16), tmp=sb1("btmp", [128, 8, 512], F32),
                             rstd=sb1("brstd", [128, 512], F32), psn=6)
                    qv = b_w_qkv[j].rearrange("(c p) n -> p c n", p=128)
                    sv = b_w_sw[j].rearrange("(c p) n -> p c n", p=128)
                    load_w(wq[:], "bwq", "bwq", qv[:, :, 0:1024])
                    load_w(wqs[:], "bwqs", "bwqs", sv[:, :, 0:1024])
                    for g in range(4):
                        for half in range(2):
                            load_w(wk[:, :, g * 128 + half * 64:g * 128 + half * 64 + 64], "bwk", "bwk%d" % (g * 2 + half),
                                   qv[:, :, 1024 + g * 64:1024 + (g + 1) * 64])
                            load_w(wks[:, :, g * 128 + half * 64:g * 128 + half * 64 + 64], "bwks", "bwks%d" % (g * 2 + half),
                                   sv[:, :, 1024 + g * 64:1024 + (g + 1) * 64])
                    load_w(wv[:], "bwv", "bwv", qv[:, :, 1280:1536])
                    for it, gi in enumerate(range(len(groups))):
                        s0, n = groups[gi]
                        lw = 1 if gi == 0 else 0
                        ht, htok = hts[it % 2], ("bht", it % 2)
                        load_h(ht, htok, "bldh%d" % (it % 2), gi)
                        norm_mod(ht, htok, n, xn, "bxn", gcol(l, 0, lw), mcol(l, 0, lw), W)
                        rope = gi > 0
                        if rope:
                            lp = s0 - CTX
                            P.add("sp", lambda h, lp=lp: h.dma_start(out=rc[:], in_=ropeC[:, lp:lp + 512]), writes=["brc"], dma="brc")
                            P.add("sp", lambda h, lp=lp: h.dma_start(out=rs[:], in_=ropeS[:, lp:lp + 512]), writes=["brs"], dma="brs")
                        for (wa, wb, nch, dst, dtok, cw) in ((wq, wqs, 8, qt, "bqt", 1024), (wk, wks, 4, kt, "bkt", 512)):
                            watok = "bwq" if nch == 8 else "bwk"
                            wbtok = "bwqs" if nch == 8 else "bwks"
                            for c in range(nch):
                                for k in range(KC):
                                    P.add("pe", lambda h, wa=wa, c=c, k=k, n=n: h.matmul(
                                        ps[0][:, :n], wa[:, k, c * 128:(c + 1) * 128], xn[:, k, :n],
                                        start=(k == 0), stop=(k == KC - 1)), reads=[watok, "bxn"], writes=[PS(0)])
                                if rope:
                                    for k in range(KC):
                                        P.add("pe", lambda h, wb=wb, c=c, k=k, n=n: h.matmul(
                                            ps[1][:, :n], wb[:, k, c * 128:(c + 1) * 128], xn[:, k, :n],
                                            start=(k == 0), stop=(k == KC - 1)), reads=[wbtok, "bxn"], writes=[PS(1)])
                                    P.add("dve", lambda h, n=n: h.tensor_tensor(out=t1[:, :n], in0=ps[0][:, :n], in1=rc[:, :n], op=ALU.mult),
                                          reads=[PS(0), "brc"], writes=["bt1"])
                                    P.add("dve", lambda h, n=n: h.tensor_tensor(out=t2[:, :n], in0=ps[1][:, :n], in1=rs[:, :n], op=ALU.mult),
                                          reads=[PS(1), "brs"], writes=["bt2"])
                                    P.add("pool", lambda h, dst=dst, c=c, n=n: h.tensor_tensor(out=dst[:, c, :n], in0=t1[:, :n], in1=t2[:, :n], op=ALU.add),
                                          reads=["bt1", "bt2"], writes=[dtok])
                                else:
                                    P.add("act", lambda h, dst=dst, c=c, n=n: h.activation(out=dst[:, c, :n], in_=ps[0][:, :n], func=AF.Copy),
                                          reads=[PS(0)], writes=[dtok])
                        for tb in range(n // 128):
                            for k in range(KC):
                                P.add("pe", lambda h, tb=tb, k=k: h.matmul(
                                    ps[2][:, :256], xn[:, k, tb * 128:(tb + 1) * 128], wv[:, k, :],
                                    start=(k == 0), stop=(k == KC - 1)), reads=["bwv", "bxn"], writes=[PS(2)])
                            P.add("act", lambda h, tb=tb: h.activation(out=vt[:, tb, :], in_=ps[2][:, :256], func=AF.Copy),
                                  reads=[PS(2)], writes=["bvt"])
                        P.add("sp", lambda h, s0=s0, n=n: h.dma_start(out=qSv[:, :, s0:s0 + n], in_=qt[:, :, :n]),
                              reads=["bqt"], writes=[("qS", gi)], dma="bstq")
                        P.add("sp", lambda h, s0=s0, n=n: h.dma_start(out=kSv[:, :, s0:s0 + n], in_=kt[:, :, :n]),
                              reads=["bkt"], writes=["kS"], dma="bstk")
                        nb = n // 128
                        P.add("sp", lambda h, s0=s0, n=n, nb=nb: h.dma_start(
                            out=vS[s0:s0 + n, :].rearrange("(b p) d -> p b d", p=128), in_=vt[:, :nb, :]),
                            reads=["bvt"], writes=["vS"], dma="bstv")
                    P.barrier(dummy)
                kall = sbt("bkall", [128, 4, T + 256], BF16)
                vall = sbt("bvall", [128, NCK + 2, 256], BF16)
                wo = sbt("bwo", [64, 16, 1024], BF16)
                esk = sbt("besk", [128, 16], F32)
                qz = [sbt("bqz%d" % i, [128, 8, 512], BF16) for i in range(2)]
                hts = [sbt("b2ht0", [128, 8, 512], F32)] * 2
                pt = [sbt("bpt%d" % i, [128, 512], BF16) for i in range(5)]
                oT = sbt("boT", [64, 16, 512], BF16)
                rd = sbt("brd", [64, 512], F32)
                for i in range(2):
                    P.add("dve", lambda h, i=i: h.memset(qz[i][:], 0.0), writes=["bqg"])
                P.add("sp", lambda h: h.dma_start(out=kall[:, :, 0:T], in_=kSv[:, :, :]), reads=["kS"], writes=["bkall"], dma="bldk")
                P.add("sp", lambda h: h.dma_start(out=vall[:, 0:NCK, :], in_=vS.rearrange("(b p) d -> p b d", p=128)),
                      reads=["vS"], writes=["bvall"], dma="bldv")
                if R > 1:
                    kcand = sbt("bkcand", [128, 2 * R, 4, 128], BF16)
                    vcand = sbt("bvcand", [128, 2 * R, 256], BF16)
                    for f, c0 in enumerate((CTX, T - 128)):
                        P.add("sp", lambda h, f=f, c0=c0: h.dma_start(out=xkI[f * 512:(f + 1) * 512, :], in_=kS[:, c0:c0 + 128]),
                              reads=["kS"], writes=[("xi", "k")], dma="bxk%d" % f)
                        P.add("sp", lambda h, f=f, c0=c0: h.dma_start(out=xvI[f * 128:(f + 1) * 128, :], in_=vS[c0:c0 + 128, :]),
                              reads=["vS"], writes=[("xi", "v")], dma="bxv%d" % f)
                    allgather("k", xkI, xkO)
                    allgather("v", xvI, xvO)
                    for i in range(R):
                        for f in range(2):
                            P.add("sp", lambda h, i=i, f=f: h.dma_start(
                                out=kcand[:, 2 * i + f, :, :], in_=xkO[i * 1024 + f * 512:i * 1024 + (f + 1) * 512, :].rearrange("(g p) t -> p g t", p=128)),
                                reads=[("xo", "k")], writes=["bkcand"], dma="bck%d" % (2 * i + f))
                            P.add("sp", lambda h, i=i, f=f: h.dma_start(
                                out=vcand[:, 2 * i + f, :], in_=xvO[i * 256 + f * 128:i * 256 + (f + 1) * 128, :]),
                                reads=[("xo", "v")], writes=["bvcand"], dma="bcv%d" % (2 * i + f))
                    for side, (selo, f) in enumerate(((0, 1), (4, 0))):
                        kd = kall[:, :, T + side * 128:T + (side + 1) * 128]
                        vd = vall[:, NCK + side, :]
                        for i in range(R):
                            sc = selt[:, selo + i:selo + i + 1]
                            if i == 0:
                                P.add("dve", lambda h, kd=kd, sc=sc, i=i, f=f: h.tensor_scalar(
                                    out=kd, in0=kcand[:, 2 * i + f, :, :], scalar1=sc, scalar2=None, op0=ALU.mult),
                                    reads=["bkcand", "sel"], writes=["bkall"])
                                P.add("dve", lambda h, vd=vd, sc=sc, i=i, f=f: h.tensor_scalar(
                                    out=vd, in0=vcand[:, 2 * i + f, :], scalar1=sc, scalar2=None, op0=ALU.mult),
                                    reads=["bvcand", "sel"], writes=["bvall"])
                            else:
                                P.add("dve", lambda h, kd=kd, sc=sc, i=i, f=f: h.scalar_tensor_tensor(
                                    out=kd, in0=kcand[:, 2 * i + f, :, :], scalar=sc, in1=kd, op0=ALU.mult, op1=ALU.add),
                                    reads=["bkcand", "sel", "bkall"], writes=["bkall"])
                                P.add("dve", lambda h, vd=vd, sc=sc, i=i, f=f: h.scalar_tensor_tensor(
                                    out=vd, in0=vcand[:, 2 * i + f, :], scalar=sc, in1=vd, op0=ALU.mult, op1=ALU.add),
                                    reads=["bvcand", "sel", "bvall"], writes=["bvall"])
                load_w(wo[:], "bwo", "bwo", b_w_out[j].rearrange("(h d) n -> d h n", d=64))
                P.add("sp", lambda h: h.dma_start(out=esk[:], in_=b_sinks_r[:, :]), writes=["besk"], dma="besk")
                P.add("act", lambda h: h.activation(out=esk[:], in_=esk[:], func=AF.Exp), reads=["besk"], writes=["besk"])
                gl = [gi for gi in range(len(groups)) if ctx_out or gi > 0]
                if DBG == 'b1':
                    gl = []
                for it, gi in enumerate(gl):
                    s0, n = groups[gi]
                    lw = 1 if gi == 0 else 0
                    ht, htok = hts[0], ("b2ht", 0)
                    load_h(ht, htok, "b2ldh0", gi)
                    for i in range(2):
                        P.add("sp", lambda h, s0=s0, n=n, i=i: h.dma_start(out=qz[i][i * 64:(i + 1) * 64, :, :n], in_=qSv[i * 64:(i + 1) * 64, :, s0:s0 + n]),
                              reads=[("qS", gi)], writes=["bqg"], dma="bldq%d" % i)
                    for qb in range(n // 128):
                        c0 = s0 + qb * 128
                        if gi == 0:
                            kbs = [(0, None), (128, None)]
                        else:
                            nbk = (c0 - CTX) // 128
                            kbs = []
                            if nbk > 0:
                                kbs.append((c0 - 128, "L"))
                            elif R > 1:
                                kbs.append((T, "EL"))
                            kbs.append((c0, None))
                            if nbk < NB - 1:
                                kbs.append((c0 + 128, "R"))
                            elif R > 1:
                                kbs.append((T + 128, "ER"))
                            kbs += [(0, None), (128, None)]
                        for g in range(4):
                            for bi, (kc0, mk) in enumerate(kbs):
                                for hh in range(4):
                                    hd = 4 * g + hh
                                    qc, half = hd // 2, hd % 2
                                    p0 = half * 64
                                    P.add("pe", lambda h, bi=bi, hh=hh, g=g, kc0=kc0, qc=qc, half=half, qb=qb: h.matmul(
                                        ps[bi][:, hh * 128:(hh + 1) * 128], kall[:, g, kc0:kc0 + 128],
                                        qz[half][:, qc, qb * 128:(qb + 1) * 128], start=True, stop=True),
                                        reads=["bkall", "bqg"], writes=[PS(bi)])
                                P.add("act", lambda h, bi=bi: h.activation(out=pt[bi][:], in_=ps[bi][:, :], func=AF.Exp, scale=0.125),
                                      reads=[PS(bi)], writes=[("bpt", bi)])
                                if mk is not None and DBG != 'b2':
                                    mo = 512 if mk in ("L", "EL") else 0
                                    mt = mske if mk in ("EL", "ER") else msk
                                    P.add("pool", lambda h, bi=bi, mo=mo, mt=mt: h.tensor_tensor(
                                        out=pt[bi][:], in0=pt[bi][:], in1=mt[:, mo:mo + 512], op=ALU.mult),
                                        reads=[("bpt", bi), "msk", "mske"], writes=[("bpt", bi)])
                            nk = len(kbs)
                            for bi, (kc0, mk) in enumerate(kbs):
                                P.add("pe", lambda h, bi=bi, kc0=kc0, g=g, nk=nk: h.matmul(
                                    ps[5][0:64, :], vall[:, kc0 // 128, g * 64:(g + 1) * 64], pt[bi][:],
                                    start=(bi == 0), stop=(bi == nk - 1)), reads=["bvall", ("bpt", bi)], writes=[PS(5)])
                            for bi, (kc0, mk) in enumerate(kbs):
                                P.add("pe", lambda h, bi=bi, nk=nk: h.matmul(
                                    ps[6][0:64, :], ones_bf[:, 0:64], pt[bi][:],
                                    start=(bi == 0), stop=(bi == nk - 1)), reads=["ones_bf", ("bpt", bi)], writes=[PS(6)])
                            P.add("dve", lambda h, g=g: h.tensor_tensor(
                                out=rd[:, :].rearrange("p (h w) -> p h w", h=4), in0=ps[6][0:64, :].rearrange("p (h w) -> p h w", h=4),
                                in1=esk[0:64, 4 * g:4 * g + 4].unsqueeze(2).to_broadcast([64, 4, 128]), op=ALU.add),
                                reads=[PS(6), "besk"], writes=["brd"])
                            P.add("dve", lambda h: h.reciprocal(out=rd[:], in_=rd[:]), reads=["brd"], writes=["brd"])
                            P.add("dve", lambda h, g=g, qb=qb: h.tensor_tensor(
                                out=oT[:, 4 * g:4 * g + 4, qb * 128:(qb + 1) * 128], in0=ps[5][0:64, :].rearrange("p (h w) -> p h w", h=4),
                                in1=rd[:, :].rearrange("p (h w) -> p h w", h=4), op=ALU.mult),
                                reads=[PS(5), "brd"], writes=["boT"])
                    for oc in range(8):
                        pi = oc % 3
                        for hd in range(16):
                            P.add("pe", lambda h, pi=pi, oc=oc, hd=hd, n=n: h.matmul(
                                ps[pi][:, :n], wo[:, hd, oc * 128:(oc + 1) * 128], oT[:, hd, :n],
                                start=(hd == 0), stop=(hd == 15)), reads=["bwo", "boT"], writes=[PS(pi)])
                        resid(pi, ht, htok, oc, n, mcol(l, 2, lw))
                    store_h(ht, htok, "b2sth0", gi)
                P.barrier(dummy)

        def mlstm_phase(l, j):
            ctx_out = l != 3
            with ExitStack() as st:
                def sbt(name, shape, dt):
                    return st.enter_context(sbuf_u(name, list(shape), dt))
                with ExitStack() as st1:
                    def sb1(name, shape, dt):
                        return st1.enter_context(sbuf_u(name, list(shape), dt))
                    win = sb1("awin", [128, 8, 3072], BF16)
                    wg = sb1("awg", [128, 8, 32], BF16)
                    bgr = sb1("abgr", [128, 32], F32)
                    hts = [sb1("aht%d" % i, [128, 8, 512], F32) for i in range(2)]
                    xn = sb1("axn", [128, 8, 512], BF16)
                    qT = sb1("aqT", [64, 8, 512], BF16)
                    kT = sb1("akT", [64, 8, 512], BF16)
                    ktok = sb1("aktok", [128, 512], BF16)
                    sgo = sb1("asgo", [128, 1024], BF16)
                    gt = sb1("agt", [128, 32], F32)
                    spt = sb1("aspt", [128, 16], F32)
                    At = sb1("aAt", [128, 16], F32)
                    eit = sb1("aeit", [128, 16], F32)
                    ett = sb1("aett", [128, 16], F32)
                    VA = sb1("aVA", [128, 16, 130], BF16)
                    Ssb2 = [sb1("aSsb%d" % i, [128, 128], BF16) for i in range(2)]
                    Sm2 = [[sb1("aSm%d_%d" % (i, d_), [128, 128], BF16) for d_ in range(2)] for i in range(2)]
                    part = sb1("apart", [128, 8, 258], F32)
                    Ut = sb1("aUt", [64, 8, 258], F32)
                    W = dict(sq=sb1("asq", [128, 8, 512], BF16), tmp=sb1("atmp", [128, 8, 512], F32),
                             rstd=sb1("arstd", [128, 512], F32), psn=6)
                    wv_ = a_w_in[j].rearrange("(c p) n -> p c n", p=128)
                    gv_ = a_w_gate[j].rearrange("(c p) n -> p c n", p=128)
                    load_w(win[:], "awin", "awin", wv_)
                    for di, so in enumerate((0, 16, 8, 24)):
                        load_w(wg[:, :, di * 8:(di + 1) * 8], "awg", "awg%d" % di, gv_[:, :, so:so + 8])
                    P.add("sp", lambda h: h.dma_start(out=bgr[:], in_=a_b_gate_r[:, j * 32:(j + 1) * 32]), writes=["abgr"], dma="abgr")
                    for it, gi in enumerate(range(len(groups))):
                        s0, n = groups[gi]
                        lw = 1 if gi == 0 else 0
                        ht, htok = hts[it % 2], ("aht", it % 2)
                        load_h(ht, htok, "aldh%d" % (it % 2), gi)
                        norm_mod(ht, htok, n, xn, "axn", gcol(l, 0, lw), mcol(l, 0, lw), W)
                        for hd in range(8):
                            for qi, (off, dst, dtok, sc) in enumerate(((0, qT, "aqT", 0.125), (512, kT, "akT", 1.0))):
                                pq = (0, 5)[qi]
                                for k in range(KC):
                                    P.add("pe", lambda h, off=off, hd=hd, k=k, n=n, pq=pq: h.matmul(
                                        ps[pq][0:64, :n], win[:, k, off + hd * 64:off + (hd + 1) * 64], xn[:, k, :n],
                                        start=(k == 0), stop=(k == KC - 1)), reads=["awin", "axn"], writes=[PS(pq)])
                                P.add("act", lambda h, dst=dst, hd=hd, sc=sc, n=n, pq=pq: h.activation(
                                    out=dst[:, hd, :n], in_=ps[pq][0:64, :n], func=AF.Copy, scale=sc),
                                    reads=[PS(pq)], writes=[dtok])
                        P.add("sp", lambda h, s0=s0, n=n: h.dma_start(
                            out=aqS.rearrange("p (h t) -> p h t", h=8)[:, :, s0:s0 + n], in_=qT[:, :, :n]),
                            reads=["aqT"], writes=[("aqS", gi)], dma="astq")
                        for tb in range(n // 128):
                            ck = (s0 + tb * 128) // 128
                            need_out = ctx_out or gi > 0
                            tsl = slice(tb * 128, (tb + 1) * 128)
                            def tokproj(pi, c0, ncol, wt=win, wtok="awin", tsl=tsl):
                                for k in range(KC):
                                    P.add("pe", lambda h, k=k: h.matmul(
                                        ps[pi][:, :ncol], xn[:, k, tsl], wt[:, k, c0:c0 + ncol],
                                        start=(k == 0), stop=(k == KC - 1)), reads=[wtok, "axn"], writes=[PS(pi)])
                            tokproj(1, 512, 512)
                            P.add("act", lambda h: h.activation(out=ktok[:], in_=ps[1][:, :], func=AF.Copy),
                                  reads=[PS(1)], writes=["aktok"])
                            tokproj(2, 1024, 512)
                            tokproj(3, 1536, 512)
                            if need_out:
                                for hf in range(2):
                                    po = (4, 0)[hf]
                                    tokproj(po, 2048 + hf * 512, 512)
                                    P.add("act", lambda h, hf=hf, po=po: h.activation(out=sgo[:, hf * 512:(hf + 1) * 512], in_=ps[po][:, :], func=AF.Sigmoid),
                                          reads=[PS(po)], writes=["asgo"])
                                P.add("sp", lambda h, ck=ck: h.dma_start(out=sgS[ck * 128:(ck + 1) * 128, :], in_=sgo[:]),
                                      reads=["asgo"], writes=[("sgS", ck)], dma="astsg")
                            tokproj(5, 0, 32, wg, "awg")
                            P.add("dve", lambda h: h.tensor_tensor(out=gt[:], in0=ps[5][:, 0:32], in1=bgr[:], op=ALU.add),
                                  reads=[PS(5), "abgr"], writes=["agt"])
                            P.add("act", lambda h: h.activation(out=spt[:], in_=gt[:, 16:32], func=AF.Exp, scale=-1.0),
                                  reads=["agt"], writes=["aspt"])
                            P.add("act", lambda h: h.activation(out=spt[:], in_=spt[:], func=AF.Ln, bias=oneb[:, 0:1], scale=1.0),
                                  reads=["aspt", "oneb"], writes=["aspt"])
                            P.add("pe", lambda h: h.matmul(ps[6][:, 0:8], tri[:, 0:128], spt[:, 0:8], start=True, stop=True),
                                  reads=["tri", "aspt"], writes=[PS(6)])
                            P.add("pe", lambda h: h.matmul(ps[6][:, 8:16], tri[:, 128:256], spt[:, 8:16], start=True, stop=True),
                                  reads=["tri", "aspt"], writes=[PS(6)])
                            P.add("pe", lambda h: h.matmul(ps[6][:, 16:32], ones_f[:], spt[:, 0:16], start=True, stop=True),
                                  reads=["ones_f", "aspt"], writes=[PS(6)])
                            P.add("dve", lambda h: h.tensor_tensor(out=At[:], in0=ps[6][:, 0:16], in1=gt[:, 0:16], op=ALU.add),
                                  reads=[PS(6), "agt"], writes=["aAt"])
                            P.add("act", lambda h: h.activation(out=At[:], in_=At[:], func=AF.Exp), reads=["aAt"], writes=["aAt"])
                            P.add("act", lambda h: h.activation(out=eit[:], in_=ps[6][:, 0:16], func=AF.Exp), reads=[PS(6)], writes=["aeit"])
                            P.add("act", lambda h: h.activation(out=ett[:], in_=ps[6][:, 16:32], func=AF.Exp, scale=-1.0), reads=[PS(6)], writes=["aett"])
                            P.add("sp", lambda h, ck=ck: h.dma_start(out=eiS[ck * 128:(ck + 1) * 128, :], in_=eit[:]),
                                  reads=["aeit"], writes=[("eiS", ck)], dma="astei")
                            P.add("sp", lambda h, ck=ck: h.dma_start(out=etS[ck * 64:(ck + 1) * 64, :], in_=ett[0:64, :]),
                                  reads=["aett"], writes=[("etS", ck)], dma="astet")
                            for d in range(2):
                                for bk in range(2):
                                    i4 = d * 8 + bk * 4
                                    P.add("dve", lambda h, i4=i4, bk=bk: h.tensor_tensor(
                                        out=VA[:, i4:i4 + 4, 0:128], in0=ps[2 + bk][:, :].rearrange("p (h w) -> p h w", h=4),
                                        in1=At[:, i4:i4 + 4].unsqueeze(2).to_broadcast([128, 4, 128]), op=ALU.mult),
                                        reads=[PS(2 + bk), "aAt"], writes=[("aVA", i4 + q_) for q_ in range(4)])
                            P.add("dve", lambda h: h.tensor_copy(out=VA[:, :, 128], in_=At[:]),
                                  reads=["aAt"], writes=[("aVA", i) for i in range(16)])
                            def stage_S(hd, tsl=tsl):
                                p = hd % 2
                                pS = (0, 5)[p]
                                P.add("pe", lambda h: h.matmul(ps[pS][:, 0:128], kT[:, hd, tsl], qT[:, hd, tsl], start=True, stop=True),
                                      reads=["akT", "aqT"], writes=[PS(pS)])
                                P.add("act", lambda h: h.activation(out=Ssb2[p][:], in_=ps[pS][:, 0:128], func=AF.Copy),
                                      reads=[PS(pS)], writes=[("aSsb", p)])
                                P.add("dve", lambda h: h.tensor_tensor(out=Sm2[p][0][:], in0=Ssb2[p][:], in1=msk[:, 0:128], op=ALU.mult),
                                      reads=[("aSsb", p), "msk"], writes=[("aSm", p, 0)])
                                P.add("pool", lambda h: h.tensor_tensor(out=Sm2[p][1][:], in0=Ssb2[p][:], in1=msk[:, 512:640], op=ALU.mult),
                                      reads=[("aSsb", p), "msk"], writes=[("aSm", p, 1)])

                            def stage_O(hd):
                                p = hd % 2
                                pO = (1, 6)[p]
                                for d in range(2):
                                    P.add("pe", lambda h, d=d: h.matmul(
                                        ps[pO][:, d * 129:(d + 1) * 129], Sm2[p][d][:], VA[:, d * 8 + hd, 0:129], start=True, stop=True),
                                        reads=[("aSm", p, d), ("aVA", d * 8 + hd)], writes=[PS(pO)])
                                P.add("act", lambda h: h.activation(out=part[:, hd, :], in_=ps[pO][:, 0:258], func=AF.Copy),
                                      reads=[PS(pO)], writes=["apart"])

                            def stage_U(hd):
                                for d in range(2):
                                    P.add("pe", lambda h, d=d: h.matmul(
                                        ps[4][0:64, d * 129:(d + 1) * 129], ktok[:, hd * 64:(hd + 1) * 64], VA[:, d * 8 + hd, 0:129],
                                        start=True, stop=True), reads=["aktok", ("aVA", d * 8 + hd)], writes=[PS(4)])
                                P.add("dve", lambda h: h.tensor_copy(out=Ut[:, hd, :], in_=ps[4][0:64, 0:258]),
                                      reads=[PS(4)], writes=["aUt"])

                            if need_out:
                                stage_S(0)
                            for hd in range(8):
                                if need_out and hd + 1 < 8:
                                    stage_S(hd + 1)
                                stage_U(hd)
                                if need_out:
                                    stage_O(hd)
                            if need_out:
                                P.add("sp", lambda h, ck=ck: h.dma_start(out=opS[ck * 128:(ck + 1) * 128, :], in_=part[:]),
                                      reads=["apart"], writes=[("opS", ck)], dma="astop")
                            P.add("sp", lambda h, ck=ck: h.dma_start(out=usS[ck * 64:(ck + 1) * 64, :], in_=Ut[:]),
                                  reads=["aUt"], writes=[("usS", ck)], dma="astus")
                    P.barrier(dummy)
                if DBG == 'a1':
                    return
                with ExitStack() as st2:
                    def sb2(name, shape, dt):
                        return st2.enter_context(sbuf_u(name, list(shape), dt))
                    nctx = CTX // 128
                    NL = NCK - nctx
                    UBc = sb2("aUBc", [128, nctx, 8, 129], F32)
                    EBc = sb2("aEBc", [128, nctx, 8], F32)
                    UBl = sb2("aUBl", [128, NL, 8, 129], F32)
                    EBl = sb2("aEBl", [128, NL, 8], F32)
                    stt = sb2("astt", [128, 8, 129], F32)
                    stb = [sb2("astb%d" % i, [128, 8, 129], BF16) for i in range(2)]
                    usv = usS.rearrange("(c p) (h w) -> c p h w", p=64, w=258)
                    etv = etS.rearrange("(c p) e -> c p e", p=64)

                    def chunk_of(d, kind, si):
                        if kind == "ctx":
                            return si if d == 0 else nctx - 1 - si
                        return nctx + si if d == 0 else NCK - 1 - si

                    nld = 0
                    for kind, UB, EB, nn in (("ctx", UBc, EBc, nctx), ("lat", UBl, EBl, NL)):
                        for si in range(nn):
                            for d in range(2):
                                ck = chunk_of(d, kind, si)
                                P.add("sp", lambda h, UB=UB, si=si, d=d, ck=ck: h.dma_start(
                                    out=UB[d * 64:(d + 1) * 64, si, :, :], in_=usv[ck, :, :, d * 129:(d + 1) * 129]),
                                    reads=[("usS", ck)], writes=[("aUB", kind, si), ("alduk", nld % 8)], dma="aldu%d" % (nld % 8))
                                P.add("sp", lambda h, EB=EB, si=si, d=d, ck=ck: h.dma_start(
                                    out=EB[d * 64:(d + 1) * 64, si, :], in_=etv[ck, :, d * 8:(d + 1) * 8]),
                                    reads=[("etS", ck)], writes=[("aEB", kind, si), ("aldek", nld % 8)], dma="alde%d" % (nld % 8))
                                nld += 1
                    P.add("dve", lambda h: h.memset(stt[:], 0.0), writes=["astt"])
                    cnt = [0]

                    def step(kind, UB, EB, si, write_cs, pp=None):
                        if write_cs:
                            b = cnt[0] % 2
                            cnt[0] += 1
                            P.add("dve", lambda h: h.tensor_copy(out=stb[b][:], in_=stt[:]), reads=["astt"], writes=[("astb", b)])
                            for d in range(2):
                                ck = chunk_of(d, kind, si)
                                P.add("sp", lambda h, d=d, ck=ck: h.dma_start(
                                    out=csS[(d * NCK + ck) * 64:(d * NCK + ck + 1) * 64, :], in_=stb[b][d * 64:(d + 1) * 64, :, :]),
                                    reads=[("astb", b)], writes=[("csS", d, ck)], dma="astcs%d%d" % (d, b))
                        P.add("dve", lambda h: h.tensor_tensor(out=stt[:], in0=stt[:], in1=UB[:, si, :, :], op=ALU.add),
                              reads=["astt", ("aUB", kind, si)], writes=["astt"])
                        P.add("dve", lambda h: h.tensor_tensor(
                            out=stt[:], in0=stt[:], in1=EB[:, si, :].unsqueeze(2).to_broadcast([128, 8, 129]), op=ALU.mult),
                            reads=["astt", ("aEB", kind, si)], writes=["astt"])
                        if pp is not None:
                            P.add("pool", lambda h: h.tensor_tensor(out=pp[:], in0=pp[:], in1=EB[:, si, :], op=ALU.mult),
                                  reads=["app", ("aEB", kind, si)], writes=["app"])

                    for si in range(nctx):
                        step("ctx", UBc, EBc, si, True)
                    if R == 1:
                        for si in range(NL):
                            step("lat", UBl, EBl, si, True)
                    else:
                        X = sb2("actx", [128, 8, 129], F32)
                        pp = sb2("app", [128, 8], F32)
                        cin = sb2("acin", [128, 8, 129], F32)
                        tmpc = sb2("atmpc", [128, 8, 129], F32)
                        GS = sb2("aGS", [128, R, 8, 129], F32)
                        GP = sb2("aGP", [128, R, 8], F32)
                        P.add("dve", lambda h: h.tensor_copy(out=X[:], in_=stt[:]), reads=["astt"], writes=["actx"])
                        P.add("dve", lambda h: h.memset(stt[:], 0.0), writes=["astt"])
                        P.add("pool", lambda h: h.memset(pp[:], 1.0), writes=["app"])
                        for si in range(NL):
                            step("lat", UBl, EBl, si, False, pp)
                        for d in range(2):
                            P.add("sp", lambda h, d=d: h.dma_start(out=xaI[:, d * 1032:(d + 1) * 1032], in_=stt[d * 64:(d + 1) * 64, :, :]),
                                  reads=["astt"], writes=[("xi", "a")], dma="axs%d" % d)
                            P.add("sp", lambda h, d=d: h.dma_start(out=xaI[:, 2064 + d * 8:2064 + (d + 1) * 8], in_=pp[d * 64:(d + 1) * 64, :]),
                                  reads=["app"], writes=[("xi", "a")], dma="axp%d" % d)
                        allgather("a", xaI, xaO)
                        xav = xaO.rearrange("(i p) w -> i p w", p=64)
                        for n_ in range(R):
                            for d in range(2):
                                i = n_ if d == 0 else R - 1 - n_
                                P.add("sp", lambda h, n_=n_, d=d, i=i: h.dma_start(
                                    out=GS[d * 64:(d + 1) * 64, n_, :, :], in_=xav[i, :, d * 1032:(d + 1) * 1032]),
                                    reads=[("xo", "a")], writes=["aGS"], dma="axg%d" % (2 * n_ + d))
                                P.add("sp", lambda h, n_=n_, d=d, i=i: h.dma_start(
                                    out=GP[d * 64:(d + 1) * 64, n_, :], in_=xav[i, :, 2064 + d * 8:2064 + (d + 1) * 8]),
                                    reads=[("xo", "a")], writes=["aGS"], dma="axh%d" % (2 * n_ + d))
                        for n_ in range(R):
                            oh = selt[:, 12 + n_:13 + n_]
                            if n_ == 0:
                                P.add("dve", lambda h, oh=oh: h.tensor_scalar(out=cin[:], in0=X[:], scalar1=oh, scalar2=None, op0=ALU.mult),
                                      reads=["actx", "sel"], writes=["acin"])
                            else:
                                P.add("dve", lambda h, oh=oh: h.tensor_scalar(out=tmpc[:], in0=X[:], scalar1=oh, scalar2=None, op0=ALU.mult),
                                      reads=["actx", "sel"], writes=["atmpc"])
                                P.add("dve", lambda h: h.tensor_tensor(out=cin[:], in0=cin[:], in1=tmpc[:], op=ALU.add),
                                      reads=["atmpc", "acin"], writes=["acin"])
                            if n_ == R - 1:
                                break
                            P.add("dve", lambda h, n_=n_: h.tensor_tensor(
                                out=X[:], in0=X[:], in1=GP[:, n_, :].unsqueeze(2).to_broadcast([128, 8, 129]), op=ALU.mult),
                                reads=["actx", "aGS"], writes=["actx"])
                            P.add("dve", lambda h, n_=n_: h.tensor_tensor(out=X[:], in0=X[:], in1=GS[:, n_, :, :], op=ALU.add),
                                  reads=["actx", "aGS"], writes=["actx"])
                        P.add("dve", lambda h: h.tensor_copy(out=stt[:], in_=cin[:]), reads=["acin"], writes=["astt"])
                        for si in range(NL):
                            step("lat", UBl, EBl, si, True)
                    P.barrier(dummy)
                if DBG == 'a2':
                    return
                wout = sbt("a2wout", [128, 8, 1024], BF16)
                hgr = sbt("a2hgr", [128, 1024], F32)
                hts = [sbt("a2ht%d" % i, [128, 8, 512], F32) for i in range(2)]
                part2 = [sbt("a2part%d" % i, [128, 8, 258], F32) for i in range(2)]
                qc = [sbt("a2qc%d" % i, [64, 8, 128], BF16) for i in range(2)]
                cst = [sbt("a2cs%d" % i, [64, 2, 8 * 129], BF16) for i in range(2)]
                eic = [sbt("a2ei%d" % i, [128, 16], F32) for i in range(2)]
                sgc = [sbt("a2sg%d" % i, [128, 1024], BF16) for i in range(2)]
                tot = sbt("a2tot", [128, 8, 258], F32)
                rr = sbt("a2rr", [128, 16], F32)
                hs = sbt("a2hs", [128, 8, 128], F32)
                sq2 = sbt("a2sq", [128, 8, 128], F32)
                ss = sbt("a2ss", [128, 8], F32)
                hgo = sbt("a2hgo", [128, 1024], F32)
                hg = sbt("a2hg", [128, 1024], BF16)
                hgT = sbt("a2hgT", [128, 8, 512], BF16)
                load_w(wout[:], "a2wout", "a2wout", a_w_out[j].rearrange("(c p) n -> p c n", p=128))
                P.add("sp", lambda h: h.dma_start(out=hgr[:], in_=a_head_g_r[:, j * D:(j + 1) * D]), writes=["a2hgr"], dma="a2hgr")
                gl = [gi for gi in range(len(groups)) if ctx_out or gi > 0]
                hs2 = [hs, sbt("a2hs1", [128, 8, 128], F32)]
                sqb = sbt("a2sqb", [128, 8, 128], F32)
                sq2b = [sq2, sbt("a2sq1", [128, 8, 128], F32)]
                hgob = [hgo, sbt("a2hgo1", [128, 1024], F32)]
                chunks = []
                for it, gi in enumerate(gl):
                    s0, n = groups[gi]
                    for tb in range(n // 128):
                        chunks.append((it, gi, tb, (s0 + tb * 128) // 128, tb == n // 128 - 1))

                def stage_F(idx):
                    it, gi, tb, ck, last = chunks[idx]
                    b = idx % 2
                    hsb, hstok = hs2[b], ("a2hs", b)
                    if tb == 0:
                        load_h(hts[it % 2], ("a2ht", it % 2), "a2ldh%d" % (it % 2), gi)
                    P.add("sp", lambda h: h.dma_start(out=part2[b][:], in_=opS[ck * 128:(ck + 1) * 128, :]),
                          reads=[("opS", ck)], writes=[("a2part", b)], dma="a2ldp%d" % b)
                    P.add("sp", lambda h: h.dma_start(
                        out=qc[b][:], in_=aqS.rearrange("p (h t) -> p h t", h=8)[:, :, ck * 128:(ck + 1) * 128]),
                        reads=[("aqS", gi)], writes=[("a2qc", b)], dma="a2ldq%d" % b)
                    for d in range(2):
                        P.add("sp", lambda h, d=d: h.dma_start(
                            out=cst[b][:, d, :], in_=csS[(d * NCK + ck) * 64:(d * NCK + ck + 1) * 64, :]),
                            reads=[("csS", d, ck)], writes=[("a2cs", b)], dma="a2ldc%d%d" % (b, d))
                    P.add("sp", lambda h: h.dma_start(out=eic[b][:], in_=eiS[ck * 128:(ck + 1) * 128, :]),
                          reads=[("eiS", ck)], writes=[("a2ei", b)], dma="a2lde%d" % b)
                    P.add("sp", lambda h: h.dma_start(out=sgc[b][:], in_=sgS[ck * 128:(ck + 1) * 128, :]),
                          reads=[("sgS", ck)], writes=[("a2sg", b)], dma="a2lds%d" % b)
                    for hd in range(8):
                        pi = hd % 3
                        for d in range(2):
                            P.add("pe", lambda h, pi=pi, hd=hd, d=d: h.matmul(
                                ps[pi][:, d * 129:(d + 1) * 129], qc[b][:, hd, :], cst[b][:, d, hd * 129:(hd + 1) * 129],
                                start=True, stop=True), reads=[("a2qc", b), ("a2cs", b)], writes=[PS(pi)])
                        P.add("dve", lambda h, pi=pi, hd=hd: h.tensor_tensor(
                            out=tot[:, hd, :], in0=ps[pi][:, 0:258], in1=part2[b][:, hd, :], op=ALU.add),
                            reads=[PS(pi), ("a2part", b)], writes=["a2tot"])
                    for d in range(2):
                        P.add("dve", lambda h, d=d: h.scalar_tensor_tensor(
                            out=rr[:, d * 8:(d + 1) * 8], in0=tot[:, :, d * 129 + 128], scalar=-1.0,
                            in1=tot[:, :, d * 129 + 128], op0=ALU.mult, op1=ALU.max),
                            reads=["a2tot"], writes=["a2rr"])
                        P.add("dve", lambda h, d=d: h.tensor_tensor(
                            out=rr[:, d * 8:(d + 1) * 8], in0=rr[:, d * 8:(d + 1) * 8], in1=eic[b][:, d * 8:(d + 1) * 8], op=ALU.max),
                            reads=["a2rr", ("a2ei", b)], writes=["a2rr"])
                    P.add("dve", lambda h: h.reciprocal(out=rr[:], in_=rr[:]), reads=["a2rr"], writes=["a2rr"])
                    P.add("dve", lambda h: h.tensor_tensor(
                        out=hsb[:], in0=tot[:, :, 0:128], in1=rr[:, 0:8].unsqueeze(2).to_broadcast([128, 8, 128]), op=ALU.mult),
                        reads=["a2tot", "a2rr"], writes=[hstok])
                    P.add("dve", lambda h: h.tensor_tensor(
                        out=sqb[:], in0=tot[:, :, 129:257], in1=rr[:, 8:16].unsqueeze(2).to_broadcast([128, 8, 128]), op=ALU.mult),
                        reads=["a2tot", "a2rr"], writes=["a2sqb"])
                    P.add("dve", lambda h: h.tensor_tensor(out=hsb[:], in0=hsb[:], in1=sqb[:], op=ALU.add),
                          reads=[hstok, "a2sqb"], writes=[hstok])
                    P.add("act", lambda h: h.activation(out=sq2b[b][:], in_=hsb[:], func=AF.Square), reads=[hstok], writes=[("a2sq", b)])
                    P.add("pool", lambda h: h.tensor_tensor(out=hgob[b][:], in0=hgr[:], in1=sgc[b][:], op=ALU.mult),
                          reads=["a2hgr", ("a2sg", b)], writes=[("a2hgo", b)])

                def stage_B(idx):
                    it, gi, tb, ck, last = chunks[idx]
                    b = idx % 2
                    hsb, hstok = hs2[b], ("a2hs", b)
                    P.add("dve", lambda h: h.tensor_reduce(out=ss[:], in_=sq2b[b][:], axis=AX.X, op=ALU.add), reads=[("a2sq", b)], writes=["a2ss"])
                    P.add("act", lambda h: h.activation(out=ss[:], in_=ss[:], func=AF.Sqrt, bias=epsb[:, 0:1], scale=1.0 / 128),
                          reads=["a2ss", "epsb"], writes=["a2ss"])
                    P.add("dve", lambda h: h.reciprocal(out=ss[:], in_=ss[:]), reads=["a2ss"], writes=["a2ss"])
                    P.add("dve", lambda h: h.tensor_tensor(
                        out=hsb[:], in0=hsb[:], in1=ss[:, 0:8].unsqueeze(2).to_broadcast([128, 8, 128]), op=ALU.mult),
                        reads=[hstok, "a2ss"], writes=[hstok])
                    P.add("dve", lambda h: h.tensor_tensor(
                        out=hg[:, :].rearrange("p (h w) -> p h w", h=8), in0=hsb[:], in1=hgob[b][:, :].rearrange("p (h w) -> p h w", h=8), op=ALU.mult),
                        reads=[hstok, ("a2hgo", b)], writes=["a2hg"])
                    for c in range(8):
                        P.add("pe", lambda h, c=c: h.transpose(psb[:, c * 128:(c + 1) * 128], hg[:, c * 128:(c + 1) * 128], ident[:]),
                              reads=["a2hg", "ident"], writes=["psb"])
                    P.add("act", lambda h: h.activation(
                        out=hgT[:, :, tb * 128:(tb + 1) * 128], in_=psb[:, :].rearrange("p (c t) -> p c t", c=8), func=AF.Copy),
                        reads=["psb"], writes=["a2hgT"])
                    if last:
                        s0, n = groups[gi]
                        lw = 1 if gi == 0 else 0
                        ht, htok = hts[it % 2], ("a2ht", it % 2)
                        for oc in range(8):
                            pi = 3 + oc % 3
                            for k in range(KC):
                                P.add("pe", lambda h, pi=pi, oc=oc, k=k: h.matmul(
                                    ps[pi][:, :n], wout[:, k, oc * 128:(oc + 1) * 128], hgT[:, k, :n],
                                    start=(k == 0), stop=(k == KC - 1)), reads=["a2wout", "a2hgT"], writes=[PS(pi)])
                            resid(pi, ht, htok, oc, n, mcol(l, 2, lw))
                        store_h(ht, htok, "a2sth%d" % (it % 2), gi)

                stage_F(0)
                for idx in range(len(chunks)):
                    if idx + 1 < len(chunks):
                        stage_F(idx + 1)
                    stage_B(idx)
                P.barrier(dummy)

        for l in layers:
            kind, j = l % 3, l // 3
            if kind == 0:
                mlstm_phase(l, j)
            elif kind == 1:
                swa_phase(l, j)
            else:
                conv_phase(l, j)
            if not (stop_after_mixer and l == layers[-1]):
                mlp_phase(l)

        with ExitStack() as st:
            def sbt(name, shape, dt):
                return st.enter_context(sbuf_u(name, list(shape), dt))
            hts = [sbt("fht%d" % i, [128, 8, 512], F32) for i in range(2)]
            W = dict(sq=sbt("fsq", [128, 8, 512], BF16), tmp=sbt("ftmp", [128, 8, 512], F32),
                     rstd=sbt("frstd", [128, 512], F32), psn=6)
            if final:
                P.add("dve", lambda h: h.tensor_copy(out=gst_[:, 0:8], in_=fgt[:]), reads=["fgt", "gst"], writes=["gst"])
                for it, gi in enumerate(range(1, len(groups))):
                    s0, n = groups[gi]
                    ht, htok = hts[it % 2], ("fht", it % 2)
                    load_h(ht, htok, "fldh%d" % (it % 2), gi)
                    norm_mod(ht, htok, n, None, None, 0, None, W)
                    P.add("pool", lambda h, s0=s0, n=n: h.dma_start(out=outTv[:, :, s0 - CTX:s0 - CTX + n], in_=W["tmp"][:, :, :n]),
                          reads=[("tmp", c) for c in range(8)], writes=[("out", gi)], dma="fst")
            else:
                for it, gi in enumerate(range(len(groups))):
                    s0, n = groups[gi]
                    ht, htok = hts[it % 2], ("fht", it % 2)
                    load_h(ht, htok, "fldh%d" % (it % 2), gi)
                    P.add("sp", lambda h, s0=s0, n=n, ht=ht: h.dma_start(out=outTv[:, :, s0:s0 + n], in_=ht[:, :, :n]),
                          reads=[htok], writes=[("out", gi)], dma="fst%d" % (it % 2))
        info = P.emit()
    return nc, info


def _fm(v):
    v = np.asarray(v, np.float32)
    lead = v.shape[:-1]
    n = v.shape[-1] // 128
    a = v.reshape(lead + (n, 128))
    a = np.moveaxis(a, -1, 0)
    return np.ascontiguousarray(a.reshape(128, -1))


def _rope_tables(LAT):
    t = np.arange(LAT)
    row = (t // 64).astype(np.float64)
    col = (t % 64).astype(np.float64)
    inv = 10000.0 ** (-np.arange(16, dtype=np.float64) / 16)
    ar = np.float32(row[:, None].astype(np.float32) * inv.astype(np.float32)[None, :]).astype(np.float64)
    ac = np.float32(col[:, None].astype(np.float32) * inv.astype(np.float32)[None, :]).astype(np.float64)
    C = np.zeros((64, LAT)); S = np.zeros((64, LAT))
    C[0:16] = np.cos(ar).T; C[16:32] = np.cos(ar).T; C[32:48] = np.cos(ac).T; C[48:64] = np.cos(ac).T
    S[0:16] = -np.sin(ar).T; S[16:32] = np.sin(ar).T; S[32:48] = -np.sin(ac).T; S[48:64] = np.sin(ac).T
    C = np.concatenate([C, C], 0).astype(np.float32)
    S = np.concatenate([S, S], 0).astype(np.float32)
    return np.ascontiguousarray(C), np.ascontiguousarray(S)


def _swap_cols(w, nheads):
    perm = np.concatenate([np.arange(16, 32), np.arange(0, 16), np.arange(48, 64), np.arange(32, 48)])
    idx = np.concatenate([h * 64 + perm for h in range(nheads)])
    return w[..., idx]


def make_in_maps(inp, R=4):
    x = np.asarray(inp["x"], np.float32)
    B, LAT = x.shape[0], x.shape[1]
    LC = LAT // R
    C, S = _rope_tables(LAT)
    s_ = np.arange(128)
    U = (s_[:, None] <= s_[None, :]).astype(np.float32)
    L = (s_[:, None] >= s_[None, :]).astype(np.float32)
    tri = np.ascontiguousarray(np.concatenate([U, L], 1))
    bf = ml_dtypes.bfloat16
    msk = np.ascontiguousarray(np.concatenate([np.tile(U, (1, 4)), np.tile(L, (1, 4))], 1)).astype(bf)
    ident = np.eye(128, dtype=np.float32).astype(bf)
    f32 = lambda k: np.ascontiguousarray(np.asarray(inp[k], np.float32))
    wqkv = f32("b_w_qkv")
    wsw = np.ascontiguousarray(np.concatenate([_swap_cols(wqkv[..., 0:1024], 16), _swap_cols(wqkv[..., 1024:1280], 4)], -1))
    bg = f32("a_b_gate").reshape(2, 4, 8)[:, [0, 2, 1, 3], :].reshape(1, 64)
    conv = np.concatenate([_fm(f32("c_conv_w")[0, 0]), _fm(f32("c_conv_w")[0, 1]), _fm(f32("c_conv_w")[0, 2]), _fm(f32("c_conv_b")[0])], 1)
    common = dict(
        ada_w=f32("ada_w"), ada_bT=_fm(f32("ada_b")), norm_gT=_fm(f32("norm_g")), final_gT=_fm(f32("final_g")),
        mlp_w1=f32("mlp_w1"), mlp_w2=f32("mlp_w2"), a_w_in=f32("a_w_in"), a_w_gate=f32("a_w_gate"),
        a_b_gate_r=np.ascontiguousarray(np.tile(bg, (128, 1))),
        a_head_g_r=np.ascontiguousarray(np.tile(f32("a_head_g").reshape(1, -1), (128, 1))),
        a_w_out=f32("a_w_out"), b_w_qkv=wqkv, b_w_sw=wsw,
        b_sinks_r=np.ascontiguousarray(np.tile(f32("b_sinks").reshape(1, 16), (128, 1))),
        b_w_out=f32("b_w_out"), c_w_in=f32("c_w_in"), c_convT=np.ascontiguousarray(conv),
        c_w_out=f32("c_w_out"), ident=ident, tri=tri, msk=msk)
    maps = []
    for b in range(B):
        cv = np.ascontiguousarray(np.concatenate([_fm(np.asarray(inp["c"], np.float32)[b]), _fm(f32("c_ctx"))], 1))
        ctxT = np.asarray(inp["ctx"], np.float32)[b].T
        for r in range(R):
            m = dict(common)
            m["xT"] = np.ascontiguousarray(np.concatenate([ctxT, x[b, r * LC:(r + 1) * LC].T], 1))
            m["cvec"] = cv
            m["ropeC"] = np.ascontiguousarray(C[:, r * LC:(r + 1) * LC])
            m["ropeS"] = np.ascontiguousarray(S[:, r * LC:(r + 1) * LC])
            sel = np.zeros((128, 16), np.float32)
            if r > 0:
                sel[:, r - 1] = 1.0
            if r < R - 1:
                sel[:, 4 + r + 1] = 1.0
            sel[:, 8 + r] = 1.0
            sel[0:64, 12 + r] = 1.0
            sel[64:128, 12 + (R - 1 - r)] = 1.0
            m["sel"] = sel
            m["ada_w"] = np.ascontiguousarray(common["ada_w"][r:r + 1]) if R > 1 else common["ada_w"]
            m["ada_bT"] = _fm(f32("ada_b")[r]) if R > 1 else common["ada_bT"]
            vR = 1.0 if r < R - 1 else 0.0
            vL = 1.0 if r > 0 else 0.0
            m["mske"] = np.ascontiguousarray(np.concatenate([np.tile(U, (1, 4)) * vR, np.tile(L, (1, 4)) * vL], 1)).astype(bf)
            maps.append(m)
    return maps


_CACHE = {}
NR = 4


def kernel(**inputs):
    x = np.asarray(inputs["x"])
    B, LAT = int(x.shape[0]), int(x.shape[1])
    LC = LAT // NR
    if LC not in _CACHE:
        _CACHE[LC] = build_program(LC, R=NR)[0]
    nc = _CACHE[LC]
    maps = make_in_maps(inputs, NR)
    res = run_bass_kernel_spmd(nc, maps, core_ids=list(range(len(maps))))
    out = np.empty((B, LAT, D), np.float32)
    for b in range(B):
        for r in range(NR):
            out[b, r * LC:(r + 1) * LC] = res.results[b * NR + r]["outT"].T
    return out
```

```python
import numpy as np
import ml_dtypes
from contextlib import ExitStack
import concourse.bass as bass
import concourse.mybir as mybir
from concourse.bass_utils import run_bass_kernel_spmd

F32 = mybir.dt.float32
BF16 = mybir.dt.bfloat16
AF = mybir.ActivationFunctionType
ALU = mybir.AluOpType
AX = mybir.AxisListType

import os
DBG = os.environ.get('K_DBG', '')
D = 1024
KC = 8
CTX = 256
SEQ = 8192
EPS = 1e-6


class _Op:
    __slots__ = ("eng", "fn", "deps", "key", "tl", "seq", "clock", "signal", "waits", "idx")


class Prog:
    ENG = ("pe", "act", "dve", "pool", "sp")

    def __init__(self, nc):
        self.nc = nc
        self.ops = []
        self.lastw = {}
        self.readers = {}
        self.bar = None
        self.bar_idx = 0
        self.keymap = {}

    def add(self, eng, fn, reads=(), writes=(), dma=None, cc=False):
        if dma is not None:
            cls = "cc" if cc else ("sw" if eng == "pool" else "hw")
            km = self.keymap.setdefault(cls, {})
            dma = (cls, km.setdefault(dma, len(km)))
        op = _Op()
        op.eng = eng
        op.fn = fn
        op.key = dma
        op.signal = dma is not None
        op.idx = len(self.ops)
        deps = set()
        for t in reads:
            w = self.lastw.get(t)
            if w is not None:
                deps.add(w)
        for t in writes:
            w = self.lastw.get(t)
            if w is not None:
                deps.add(w)
            for r in self.readers.get(t, ()):
                deps.add(r)
        for t in writes:
            self.lastw[t] = op
            self.readers[t] = []
        for t in reads:
            self.readers.setdefault(t, []).append(op)
        if self.bar is not None:
            deps.add(self.bar)
        deps.discard(op)
        op.deps = deps
        self.ops.append(op)
        return op

    def barrier(self, dummy):
        last = {}
        for o in self.ops[self.bar_idx:]:
            last[("dma", o.key) if o.key is not None else o.eng] = o
        op = self.add("dve", lambda h: h.memset(dummy[:], 0.0), writes=["__bar"])
        op.deps |= set(last.values())
        op.deps.discard(op)
        self.bar = op
        self.bar_idx = len(self.ops)
        self.keymap = {}

    def _plan(self):
        def skip(d, op):
            return d.key is None and d.eng == "pe" and op.eng == "pe" and op.key is None

        for op in self.ops:
            for d in op.deps:
                if not skip(d, op):
                    d.signal = True
        seqs = {}
        for op in self.ops:
            op.tl = ("dma", op.key) if op.key is not None else op.eng
            if op.signal:
                seqs[op.tl] = seqs.get(op.tl, 0) + 1
            op.seq = seqs.get(op.tl, 0)
        self.final = dict(seqs)
        eng_clock = {e: {} for e in self.ENG}
        for op in self.ops:
            ck = eng_clock[op.eng]
            need = {}
            for d in op.deps:
                if skip(d, op):
                    continue
                if ck.get(d.tl, 0) >= d.seq:
                    continue
                if need.get(d.tl, 0) < d.seq:
                    need[d.tl] = d.seq
            for d in op.deps:
                if skip(d, op):
                    continue
                for tl, s in d.clock.items():
                    if ck.get(tl, 0) < s:
                        ck[tl] = s
            for tl, s in need.items():
                if ck.get(tl, 0) < s:
                    ck[tl] = s
            op.waits = list(need.items())
            c = dict(ck)
            if op.signal:
                c[op.tl] = op.seq
            op.clock = c
            op.deps = None

    def emit(self):
        self._plan()
        nc = self.nc
        with ExitStack() as st:
            sems = {}
            for i, tl in enumerate(self.final):
                sems[tl] = st.enter_context(nc.semaphore("sm%d" % i))
            block = st.enter_context(nc.Block())
            per = {e: [o for o in self.ops if o.eng == e] for e in self.ENG}
            final = self.final

            def val(tl, s):
                return s if (isinstance(tl, str) or tl[1][0] == "cc") else s * 16

            def run(e, h):
                for o in per[e]:
                    for tl, s in o.waits:
                        h.wait_ge(sems[tl], val(tl, s))
                    ins = o.fn(h)
                    if o.signal:
                        if o.key is not None and o.key[0] == "cc":
                            ins.then_inc(sems[o.tl])
                        else:
                            ins.then_inc(sems[o.tl], 16 if o.key is not None else 1)
                if e == "sp":
                    for tl, s in final.items():
                        h.wait_ge(sems[tl], val(tl, s))

            @block.tensor
            def _(h):
                run("pe", h)

            @block.scalar
            def _(h):
                run("act", h)

            @block.vector
            def _(h):
                run("dve", h)

            @block.gpsimd
            def _(h):
                run("pool", h)

            @block.sync
            def _(h):
                run("sp", h)
        return dict(n_ops=len(self.ops), n_sems=len(self.final))


def build_program(LAT=SEQ // 4, layers=(0, 1, 2, 3), final=True, stop_after_mixer=False, R=4):
    T = CTX + LAT
    RG = [list(range(g * R, (g + 1) * R)) for g in range(8 // R)] if R > 1 else None
    NCK = T // 128
    groups = [(0, CTX)] + [(CTX + i * 512, 512) for i in range(LAT // 512)]
    nc = bass.Bass("TRN2", target_bir_lowering=False)

    def din(name, shape, dt=F32):
        return nc.dram_tensor(name, list(shape), dt, kind="ExternalInput").ap()

    def dscr(name, shape, dt):
        return nc.dram_tensor(name, list(shape), dt, kind="Internal").ap()

    xT = din("xT", [D, T])
    cvec = din("cvec", [128, 16])
    ada_w = din("ada_w", [4 if R == 1 else 1, D, 6 * D])
    ada_bT = din("ada_bT", [128, (4 if R == 1 else 1) * 48])
    norm_gT = din("norm_gT", [128, 64])
    final_gT = din("final_gT", [128, 8])
    mlp_w1 = din("mlp_w1", [4, D, 4 * D])
    mlp_w2 = din("mlp_w2", [4, 4 * D, D])
    a_w_in = din("a_w_in", [2, D, 3 * D])
    a_w_gate = din("a_w_gate", [2, D, 32])
    a_b_gate_r = din("a_b_gate_r", [128, 64])
    a_head_g_r = din("a_head_g_r", [128, 2 * D])
    a_w_out = din("a_w_out", [2, D, D])
    b_w_qkv = din("b_w_qkv", [1, D, 1536])
    b_w_sw = din("b_w_sw", [1, D, 1280])
    b_sinks_r = din("b_sinks_r", [128, 16])
    b_w_out = din("b_w_out", [1, D, D])
    ropeC = din("ropeC", [128, LAT])
    ropeS = din("ropeS", [128, LAT])
    c_w_in = din("c_w_in", [1, D, 3 * D])
    c_convT = din("c_convT", [128, 32])
    c_w_out = din("c_w_out", [1, D, D])
    ident_in = din("ident", [128, 128], BF16)
    tri_in = din("tri", [128, 256])
    msk_in = din("msk", [128, 1024], BF16)
    mske_in = din("mske", [128, 1024], BF16)
    sel_in = din("sel", [128, 16])
    if final:
        outT = nc.dram_tensor("outT", [D, LAT], F32, kind="ExternalOutput").ap()
    else:
        outT = nc.dram_tensor("outT", [D, T], F32, kind="ExternalOutput").ap()

    hT = dscr("hT", [D, T], F32)
    xnS = dscr("xnS", [D, T], BF16)
    UW = 1 + CTX + 1 + 1 + LAT + 1
    uS = dscr("uS", [D, UW], F32)
    bgS = dscr("bgS", [D, T], F32)
    qS = dscr("qS", [D, T], BF16)
    kS = dscr("kS", [512, T], BF16)
    vS = dscr("vS", [T, 256], BF16)
    aqS = dscr("aqS", [64, 8 * T], BF16)
    opS = dscr("opS", [T, 8 * 258], F32)
    sgS = dscr("sgS", [T, D], BF16)
    eiS = dscr("eiS", [T, 16], F32)
    etS = dscr("etS", [NCK * 64, 16], F32)
    usS = dscr("usS", [NCK * 64, 8 * 258], F32)
    csS = dscr("csS", [2 * NCK * 64, 8 * 129], BF16)
    RR = max(R, 1)
    xcI = dscr("xcI", [2, D], F32)
    xcO = dscr("xcO", [2 * RR, D], F32)
    xkI = dscr("xkI", [1024, 128], BF16)
    xkO = dscr("xkO", [1024 * RR, 128], BF16)
    xvI = dscr("xvI", [256, 256], BF16)
    xvO = dscr("xvO", [256 * RR, 256], BF16)
    XW = 2 * 8 * 129 + 16
    xaI = dscr("xaI", [64, XW], F32)
    xaO = dscr("xaO", [64 * RR, XW], F32)
    xmI = dscr("xmI", [128, 96], F32)
    xmO = dscr("xmO", [128 * RR, 96], F32)

    hTv = hT.rearrange("(c p) t -> p c t", p=128)
    xTv = xT.rearrange("(c p) t -> p c t", p=128)
    xnSv = xnS.rearrange("(c p) t -> p c t", p=128)
    uSv = uS.rearrange("(c p) t -> p c t", p=128)
    bgSv = bgS.rearrange("(c p) t -> p c t", p=128)
    qSv = qS.rearrange("(c p) t -> p c t", p=128)
    kSv = kS.rearrange("(c p) t -> p c t", p=128)
    outTv = outT.rearrange("(c p) t -> p c t", p=128)

    _cnt = [0]

    def sbuf_u(name, shape, dt):
        _cnt[0] += 1
        return nc.sbuf_tensor("%s_%d" % (name, _cnt[0]), shape, dt)

    P = Prog(nc)
    with ExitStack() as gst:
        def sb(name, shape, dt):
            return gst.enter_context(sbuf_u(name, list(shape), dt))

        ps = [gst.enter_context(nc.psum_tensor("ps%d" % i, [128, 512], F32)) for i in range(7)]
        psb = gst.enter_context(nc.psum_tensor("psb", [128, 1024], BF16))
        dummy = sb("dummyt", [128, 8], F32)
        modt = sb("modt", [128, 4 * 96], F32)
        gst_ = sb("gst", [128, 4 * 32], F32)
        ngt = sb("ngt", [128, 64], F32)
        fgt = sb("fgt", [128, 8], F32)
        adab = sb("adab", [128, 192], F32)
        cs_t = sb("cs_t", [128, 16], F32)
        ones_bf = sb("ones_bf", [128, 128], BF16)
        ones_f = sb("ones_f", [128, 128], F32)
        epsb = sb("epsb", [128, 1], F32)
        oneb = sb("oneb", [128, 1], F32)
        ident = sb("ident_sb", [128, 128], BF16)
        tri = sb("tri_sb", [128, 256], F32)
        msk = sb("msk_sb", [128, 1024], BF16)
        mske = sb("mske_sb", [128, 1024], BF16)
        selt = sb("sel_sb", [128, 16], F32)

        def PS(i):
            return ("ps", i)

        P.add("dve", lambda h: h.memset(ones_bf[:], 1.0), writes=["ones_bf"])
        P.add("dve", lambda h: h.memset(ones_f[:], 1.0), writes=["ones_f"])
        P.add("dve", lambda h: h.memset(epsb[:], EPS), writes=["epsb"])
        P.add("dve", lambda h: h.memset(oneb[:], 1.0), writes=["oneb"])
        for nm, dst, src in (("ngt", ngt, norm_gT), ("fgt", fgt, final_gT), ("adab", adab[:, 0:(192 if R == 1 else 48)], ada_bT),
                             ("cs_t", cs_t, cvec), ("ident", ident, ident_in), ("tri", tri, tri_in),
                             ("msk", msk, msk_in), ("mske", mske, mske_in), ("sel", selt, sel_in)):
            P.add("sp", lambda h, dst=dst, src=src: h.dma_start(out=(dst if nm == "adab" else dst[:]), in_=src[:, :]),
                  writes=[nm], dma="c_" + nm)
        for gi, (s0, n) in enumerate(groups):
            P.add("sp", lambda h, s0=s0, n=n: h.dma_start(out=hT[:, s0:s0 + n], in_=xT[:, s0:s0 + n]),
                  writes=[("hT", gi)], dma="cp%d" % (gi % 4))

        def allgather(name, src, dst):
            P.add("pool", lambda h: h.collective_compute("AllGather", ALU.bypass, replica_groups=RG, ins=[src], outs=[dst]),
                  reads=[("xi", name)], writes=[("xo", name)], dma="cc_" + name, cc=True)

        with ExitStack() as st:
            awt = [st.enter_context(sbuf_u("awt%d" % i, [128, 8, 1024], F32)) for i in range(2)]
            silu = st.enter_context(sbuf_u("silu", [128, 16], F32))
            P.add("act", lambda h: h.activation(out=silu[:], in_=cs_t[:], func=AF.Silu),
                  reads=["cs_t"], writes=["silu"])
            it = 0
            if R > 1:
                modq = st.enter_context(sbuf_u("modq", [128, 96], F32))
                ada_layers, dst_of = [0], (lambda l, c: modq[:, c:c + 8])
            else:
                ada_layers, dst_of = list(layers), (lambda l, c: modt[:, l * 96 + c:l * 96 + c + 8])
            for l in ada_layers:
                for v in range(6):
                    b = it % 2
                    it += 1
                    src = ada_w[l].rearrange("(c p) n -> p c n", p=128)[:, :, v * 1024:(v + 1) * 1024]
                    P.add("sp", lambda h, b=b, src=src: h.dma_start(out=awt[b][:], in_=src),
                          writes=[("awt", b)], dma="awt%d" % b)
                    pa = b
                    for j in range(8):
                        for k in range(8):
                            P.add("pe", lambda h, b=b, j=j, k=k, pa=pa: h.matmul(
                                ps[pa][:, 2 * j:2 * j + 2], awt[b][:, k, j * 128:(j + 1) * 128],
                                silu[:, k:k + 9:8], start=(k == 0), stop=(k == 7)),
                                reads=[("awt", b), "silu"], writes=[PS(pa)])
                    for w in range(2):
                        dst = dst_of(l, (v * 2 + w) * 8)
                        P.add("dve", lambda h, dst=dst, w=w, l=l, v=v, pa=pa: h.tensor_tensor(
                            out=dst, in0=ps[pa][:, w:16:2],
                            in1=adab[:, l * 48 + v * 8:l * 48 + v * 8 + 8], op=ALU.add),
                            reads=[PS(pa), "adab"], writes=["modt"])
            if R > 1:
                P.add("sp", lambda h: h.dma_start(out=xmI[:, :], in_=modq[:]), reads=["modt"], writes=[("xi", "m")], dma="xmi")
                allgather("m", xmI, xmO)
                P.add("sp", lambda h: h.dma_start(out=modt[:, :].rearrange("p (l c) -> p l c", l=4),
                                                  in_=xmO.rearrange("(l p) c -> p l c", p=128)),
                      reads=[("xo", "m")], writes=["modt"], dma="xmo")
            for l in (range(4) if R > 1 else layers):
                for i, v in ((0, 1), (1, 4)):
                    for w in range(2):
                        col = l * 96 + (v * 2 + w) * 8
                        gcol = l * 32 + (i * 2 + w) * 8
                        P.add("dve", lambda h, col=col, gcol=gcol, l=l, i=i: h.scalar_tensor_tensor(
                            out=gst_[:, gcol:gcol + 8], in0=modt[:, col:col + 8], scalar=1.0,
                            in1=ngt[:, l * 16 + i * 8:l * 16 + i * 8 + 8], op0=ALU.add, op1=ALU.mult),
                            reads=["modt", "ngt"], writes=["gst"])
            P.barrier(dummy)

        def mcol(l, v, w):
            return l * 96 + (v * 2 + w) * 8

        def gcol(l, i, w):
            return l * 32 + (i * 2 + w) * 8

        def norm_mod(ht, htok, n, xn, xntok, gs0, sh0, W):
            sq, tmp, rstd, pi = W["sq"], W["tmp"], W["rstd"], W["psn"]
            P.add("act", lambda h: h.activation(out=sq[:, :, :n], in_=ht[:, :, :n], func=AF.Square),
                  reads=[htok], writes=["sq"])
            for c in range(KC):
                P.add("pe", lambda h, c=c: h.matmul(ps[pi][:, :n], ones_bf[:], sq[:, c, :n],
                                                   start=(c == 0), stop=(c == KC - 1)),
                      reads=["sq", "ones_bf"], writes=[PS(pi)])
            P.add("act", lambda h: h.activation(out=rstd[:, :n], in_=ps[pi][:, :n], func=AF.Sqrt,
                                                bias=epsb[:, 0:1], scale=1.0 / D),
                  reads=[PS(pi), "epsb"], writes=["rstd"])
            P.add("dve", lambda h: h.reciprocal(out=rstd[:, :n], in_=rstd[:, :n]),
                  reads=["rstd"], writes=["rstd"])
            for c in range(KC):
                P.add("dve", lambda h, c=c: h.scalar_tensor_tensor(
                    out=tmp[:, c, :n], in0=ht[:, c, :n], scalar=gst_[:, gs0 + c:gs0 + c + 1],
                    in1=rstd[:, :n], op0=ALU.mult, op1=ALU.mult),
                    reads=[htok, "gst", "rstd"], writes=[("tmp", c)])
                if sh0 is None:
                    continue
                P.add("act", lambda h, c=c: h.activation(
                    out=xn[:, c, :n], in_=tmp[:, c, :n], func=AF.Identity,
                    bias=modt[:, sh0 + c:sh0 + c + 1], scale=1.0),
                    reads=[("tmp", c), "modt"], writes=[xntok])

        def load_h(ht, tok, key, gi):
            s0, n = groups[gi]
            P.add("sp", lambda h: h.dma_start(out=ht[:, :, :n], in_=hTv[:, :, s0:s0 + n]),
                  reads=[("hT", gi)], writes=[tok], dma=key)

        def store_h(ht, tok, key, gi, q="sp"):
            s0, n = groups[gi]
            P.add(q, lambda h: h.dma_start(out=hTv[:, :, s0:s0 + n], in_=ht[:, :, :n]),
                  reads=[tok], writes=[("hT", gi)], dma=key)

        def load_w(wt, tok, key, src):
            P.add("pool", lambda h: h.dma_start(out=wt, in_=src), writes=[tok], dma=key)

        def resid(pi, ht, htok, oc, n, gate_col):
            P.add("dve", lambda h: h.scalar_tensor_tensor(
                out=ht[:, oc, :n], in0=ps[pi][:, :n], scalar=modt[:, gate_col + oc:gate_col + oc + 1],
                in1=ht[:, oc, :n], op0=ALU.mult, op1=ALU.add),
                reads=[PS(pi), htok, "modt"], writes=[htok])

        def mlp_phase(l):
            with ExitStack() as st:
                def sbt(name, shape, dt):
                    return st.enter_context(sbuf_u(name, list(shape), dt))
                w1s = [sbt("w1s%d" % i, [128, 8, 1024], BF16) for i in range(2)]
                w2s = [sbt("w2s%d" % i, [128, 8, 1024], BF16) for i in range(2)]
                hts = [sbt("mht%d" % i, [128, 8, 512], F32) for i in range(2)]
                xns = [sbt("mxn%d" % i, [128, 8, 512], BF16) for i in range(2)]
                hid = sbt("mhid", [128, 8, 512], BF16)
                rl = [sbt("mrl%d" % i, [128, 512], BF16) for i in range(2)]
                W = dict(sq=sbt("msq", [128, 8, 512], BF16), tmp=sbt("mtmp", [128, 8, 512], F32),
                         rstd=sbt("mrstd", [128, 512], F32), psn=6)
                gl = [gi for gi in range(len(groups)) if not (l == 3 and gi == 0)]
                w1v = mlp_w1[l].rearrange("(c p) n -> p c n", p=128)
                w2v = mlp_w2[l].rearrange("(c p) n -> p c n", p=128)

                def load_slab(s):
                    b = s % 2
                    load_w(w1s[b][:], ("w1s", b), "w1s%d" % b, w1v[:, :, s * 1024:(s + 1) * 1024])
                    load_w(w2s[b][:], ("w2s", b), "w2s%d" % b, w2v[:, s * 8:(s + 1) * 8, :])

                load_slab(0)
                it = 0
                for s in range(4):
                    b = s % 2
                    if s + 1 < 4:
                        load_slab(s + 1)
                    for gi in gl:
                        s0, n = groups[gi]
                        lw = 1 if gi == 0 else 0
                        ht, xn = hts[it % 2], xns[it % 2]
                        htok, xntok = ("mht", it % 2), ("mxn", it % 2)
                        load_h(ht, htok, "mldh%d" % (it % 2), gi)
                        if s == 0:
                            norm_mod(ht, htok, n, xn, xntok, gcol(l, 1, lw), mcol(l, 3, lw), W)
                            P.add("pool", lambda h, xn=xn, s0=s0, n=n: h.dma_start(
                                out=xnSv[:, :, s0:s0 + n], in_=xn[:, :, :n]),
                                reads=[xntok], writes=[("xnS", gi)], dma="mstx%d" % (it % 2))
                        else:
                            P.add("sp", lambda h, xn=xn, s0=s0, n=n: h.dma_start(
                                out=xn[:, :, :n], in_=xnSv[:, :, s0:s0 + n]),
                                reads=[("xnS", gi)], writes=[xntok], dma="mldx%d" % (it % 2))
                        for hc in range(8):
                            pi = hc % 3
                            for k in range(KC):
                                P.add("pe", lambda h, pi=pi, hc=hc, k=k, xn=xn, b=b, n=n: h.matmul(
                                    ps[pi][:, :n], w1s[b][:, k, hc * 128:(hc + 1) * 128], xn[:, k, :n],
                                    start=(k == 0), stop=(k == KC - 1)),
                                    reads=[("w1s", b), xntok], writes=[PS(pi)])
                            r = rl[hc % 2]
                            P.add("act", lambda h, pi=pi, r=r, n=n: h.activation(
                                out=r[:, :n], in_=ps[pi][:, :n], func=AF.Relu),
                                reads=[PS(pi)], writes=[("mrl", hc % 2)])
                            P.add("dve", lambda h, pi=pi, r=r, hc=hc, n=n: h.tensor_tensor(
                                out=hid[:, hc, :n], in0=ps[pi][:, :n], in1=r[:, :n], op=ALU.mult),
                                reads=[PS(pi), ("mrl", hc % 2)], writes=[("mhid", hc)])
                        for oc in range(8):
                            pi = 3 + oc % 3
                            for hc in range(8):
                                P.add("pe", lambda h, pi=pi, hc=hc, oc=oc, b=b, n=n: h.matmul(
                                    ps[pi][:, :n], w2s[b][:, hc, oc * 128:(oc + 1) * 128], hid[:, hc, :n],
                                    start=(hc == 0), stop=(hc == 7)),
                                    reads=[("w2s", b), ("mhid", hc)], writes=[PS(pi)])
                            resid(pi, ht, htok, oc, n, mcol(l, 5, lw))
                        store_h(ht, htok, "msth%d" % (it % 2), gi, "pool")
                        it += 1
                P.barrier(dummy)

        def conv_phase(l, j):
            ctx_out = l != 3
            with ExitStack() as st:
                def sbt(name, shape, dt):
                    return st.enter_context(sbuf_u(name, list(shape), dt))
                win = sbt("cwin", [128, 8, 3072], BF16)
                wout = sbt("cwout", [128, 8, 1024], BF16)
                cvt = sbt("cvt", [128, 32], F32)
                hts = [sbt("cht%d" % i, [128, 8, 512], F32) for i in range(2)]
                xn = sbt("cxn", [128, 8, 512], BF16)
                ut = sbt("cut", [128, 8, 514], F32)
                bgt = sbt("cbg", [128, 8, 512], F32)
                cgt = sbt("ccg", [128, 512], F32)
                zt = sbt("czt", [128, 8, 512], BF16)
                acc = sbt("cacc", [128, 512], F32)
                zero = sbt("czero", [128, 8], F32)
                W = dict(sq=sbt("csq", [128, 8, 512], BF16), tmp=sbt("ctmp", [128, 8, 512], F32),
                         rstd=sbt("crstd", [128, 512], F32), psn=6)
                load_w(win[:], "cwin", "cwin", c_w_in[j].rearrange("(c p) n -> p c n", p=128))
                load_w(wout[:], "cwout", "cwout", c_w_out[j].rearrange("(c p) n -> p c n", p=128))
                P.add("sp", lambda h: h.dma_start(out=cvt[:], in_=c_convT[:, :]), writes=["cvt"], dma="cvt")
                P.add("dve", lambda h: h.memset(zero[:], 0.0), writes=["czero"])
                gl = [gi for gi in range(len(groups)) if ctx_out or gi > 0]

                def ucol(gi):
                    s0, n = groups[gi]
                    return (1 + s0) if gi == 0 else (3 + s0)
                pads = (0, 1 + CTX, 2 + CTX, UW - 1) if R == 1 else (0, 1 + CTX)
                for ci, col in enumerate(pads):
                    P.add("sp", lambda h, col=col: h.dma_start(out=uSv[:, :, col:col + 1], in_=zero[:, :].rearrange("p (c o) -> p c o", o=1), allow_slow_non_contiguous=True),
                          reads=["czero"], writes=[("uSpad", ci)], dma="cpad%d" % ci)
                edge = [sbt("cedge%d" % i, [128, 8], F32) for i in range(2)]
                xg = sbt("cxg", [128, 2 * RR, 8], F32)
                halo = [sbt("chalo%d" % i, [128, 8], F32) for i in range(2)]
                for it, gi in enumerate(gl):
                    s0, n = groups[gi]
                    lw = 1 if gi == 0 else 0
                    ht, htok = hts[it % 2], ("cht", it % 2)
                    load_h(ht, htok, "cldh%d" % (it % 2), gi)
                    norm_mod(ht, htok, n, xn, "cxn", gcol(l, 0, lw), mcol(l, 0, lw), W)
                    for c in range(8):
                        for which, off, pi in (("bg", 0, 0), ("cg", 1024, 1), ("xt", 2048, 2)):
                            for k in range(KC):
                                P.add("pe", lambda h, pi=pi, off=off, c=c, k=k, n=n: h.matmul(
                                    ps[pi][:, :n], win[:, k, off + c * 128:off + (c + 1) * 128], xn[:, k, :n],
                                    start=(k == 0), stop=(k == KC - 1)),
                                    reads=["cwin", "cxn"], writes=[PS(pi)])
                        P.add("act", lambda h, c=c, n=n: h.activation(out=bgt[:, c, :n], in_=ps[0][:, :n], func=AF.Copy),
                              reads=[PS(0)], writes=["cbg"])
                        P.add("act", lambda h, n=n: h.activation(out=cgt[:, :n], in_=ps[1][:, :n], func=AF.Copy),
                              reads=[PS(1)], writes=["ccg"])
                        P.add("dve", lambda h, c=c, n=n: h.tensor_tensor(out=ut[:, c, :n], in0=ps[2][:, :n], in1=cgt[:, :n], op=ALU.mult),
                              reads=[PS(2), "ccg"], writes=["cut"])
                    uc = ucol(gi)
                    if R > 1 and gi == 1:
                        P.add("dve", lambda h: h.tensor_copy(out=edge[0][:], in_=ut[:, :, 0]), reads=["cut"], writes=[("cedge", 0)])
                    if R > 1 and gi == len(groups) - 1:
                        P.add("dve", lambda h, n=n: h.tensor_copy(out=edge[1][:], in_=ut[:, :, n - 1]), reads=["cut"], writes=[("cedge", 1)])
                    P.add("pool", lambda h, uc=uc, n=n: h.dma_start(out=uSv[:, :, uc:uc + n], in_=ut[:, :, :n]),
                          reads=["cut"], writes=[("uS", gi)], dma="cstu")
                    P.add("pool", lambda h, s0=s0, n=n: h.dma_start(out=bgSv[:, :, s0:s0 + n], in_=bgt[:, :, :n]),
                          reads=["cbg"], writes=[("bgS", gi)], dma="cstb")
                if R > 1:
                    for i in range(2):
                        P.add("sp", lambda h, i=i: h.dma_start(out=xcI[i, :].rearrange("(p c) -> p c", c=8), in_=edge[i][:]),
                              reads=[("cedge", i)], writes=[("xi", "c")], dma="cxi%d" % i)
                    allgather("c", xcI, xcO)
                    P.add("sp", lambda h: h.dma_start(out=xg[:], in_=xcO.rearrange("r (p c) -> p r c", c=8)),
                          reads=[("xo", "c")], writes=["cxg"], dma="cxo")
                    for side, (selo, rowo) in enumerate(((0, 1), (4, 0))):
                        for i in range(R):
                            if i == 0:
                                P.add("dve", lambda h, side=side, selo=selo, rowo=rowo, i=i: h.tensor_scalar(
                                    out=halo[side][:], in0=xg[:, 2 * i + rowo, :], scalar1=selt[:, selo + i:selo + i + 1], scalar2=None, op0=ALU.mult),
                                    reads=["cxg", "sel"], writes=[("chalo", side)])
                            else:
                                P.add("dve", lambda h, side=side, selo=selo, rowo=rowo, i=i: h.scalar_tensor_tensor(
                                    out=halo[side][:], in0=xg[:, 2 * i + rowo, :], scalar=selt[:, selo + i:selo + i + 1], in1=halo[side][:],
                                    op0=ALU.mult, op1=ALU.add), reads=["cxg", "sel", ("chalo", side)], writes=[("chalo", side)])
                        col = (2 + CTX, UW - 1)[side]
                        P.add("sp", lambda h, side=side, col=col: h.dma_start(
                            out=uSv[:, :, col:col + 1], in_=halo[side][:, :].rearrange("p (c o) -> p c o", o=1), allow_slow_non_contiguous=True),
                            reads=[("chalo", side)], writes=[("uSpad", 2 + side)], dma="cpad%d" % (2 + side))
                for it, gi in enumerate(gl):
                    s0, n = groups[gi]
                    lw = 1 if gi == 0 else 0
                    ht, htok = hts[it % 2], ("cht", it % 2)
                    load_h(ht, htok, "cldh%d" % (it % 2), gi)
                    uc = ucol(gi)
                    rd = [("uS", g2) for g2 in gl] + [("uSpad", i) for i in range(4)]
                    P.add("sp", lambda h, uc=uc, n=n: h.dma_start(out=ut[:, :, :n + 2], in_=uSv[:, :, uc - 1:uc + n + 1]),
                          reads=rd, writes=["cut"], dma="cldu")
                    P.add("sp", lambda h, s0=s0, n=n: h.dma_start(out=bgt[:, :, :n], in_=bgSv[:, :, s0:s0 + n]),
                          reads=[("bgS", gi)], writes=["cbg"], dma="cldb")
                    for c in range(8):
                        P.add("dve", lambda h, c=c, n=n: h.tensor_scalar(
                            out=acc[:, :n], in0=ut[:, c, 1:n + 1], scalar1=cvt[:, 8 + c:9 + c], scalar2=cvt[:, 24 + c:25 + c],
                            op0=ALU.mult, op1=ALU.add), reads=["cut", "cvt"], writes=["cacc"])
                        P.add("dve", lambda h, c=c, n=n: h.scalar_tensor_tensor(
                            out=acc[:, :n], in0=ut[:, c, 0:n], scalar=cvt[:, c:c + 1], in1=acc[:, :n],
                            op0=ALU.mult, op1=ALU.add), reads=["cut", "cvt", "cacc"], writes=["cacc"])
                        P.add("dve", lambda h, c=c, n=n: h.scalar_tensor_tensor(
                            out=acc[:, :n], in0=ut[:, c, 2:n + 2], scalar=cvt[:, 16 + c:17 + c], in1=acc[:, :n],
                            op0=ALU.mult, op1=ALU.add), reads=["cut", "cvt", "cacc"], writes=["cacc"])
                        P.add("dve", lambda h, c=c, n=n: h.tensor_tensor(
                            out=zt[:, c, :n], in0=acc[:, :n], in1=bgt[:, c, :n], op=ALU.mult),
                            reads=["cacc", "cbg"], writes=["czt"])
                    for oc in range(8):
                        pi = oc % 3
                        for k in range(KC):
                            P.add("pe", lambda h, pi=pi, oc=oc, k=k, n=n: h.matmul(
                                ps[pi][:, :n], wout[:, k, oc * 128:(oc + 1) * 128], zt[:, k, :n],
                                start=(k == 0), stop=(k == KC - 1)), reads=["cwout", "czt"], writes=[PS(pi)])
                        resid(pi, ht, htok, oc, n, mcol(l, 2, lw))
                    store_h(ht, htok, "csth%d" % (it % 2), gi, "pool")
                P.barrier(dummy)

        def swa_phase(l, j):
            ctx_out = l != 3
            NB = LAT // 128
            with ExitStack() as st:
                def sbt(name, shape, dt):
                    return st.enter_context(sbuf_u(name, list(shape), dt))
                with ExitStack() as st1:
                    def sb1(name, shape, dt):
                        return st1.enter_context(sbuf_u(name, list(shape), dt))
                    wq = sb1("bwq", [128, 8, 1024], BF16)
                    wqs = sb1("bwqs", [128, 8, 1024], BF16)
                    wk = sb1("bwk", [128, 8, 512], BF16)
                    wks = sb1("bwks", [128, 8, 512], BF16)
                    wv = sb1("bwv", [128, 8, 256], BF16)
                    rc = sb1("brc", [128, 512], F32)
                    rs = sb1("brs", [128, 512], F32)
                    hts = [sb1("bht%d" % i, [128, 8, 512], F32) for i in range(2)]
                    xn = sb1("bxn", [128, 8, 512], BF16)
                    qt = sb1("bqt", [128, 8, 512], BF16)
                    kt = sb1("bkt", [128, 4, 512], BF16)
                    vt = sb1("bvt", [128, 4, 256], BF16)
                    t1 = sb1("bt1", [128, 512], F32)
                    t2 = sb1("bt2", [128, 512], F32)
                    W = dict(sq=sb1("bsq", [128, 8, 512], BF16), tmp=sb1("btmp", [128, 8, 512], F32),
                             rstd=sb1("brstd", [128, 512], F32), psn=6)
                    qv = b_w_qkv[j].rearrange("(c p) n -> p c n", p=128)
                    sv = b_w_sw[j].rearrange("(c p) n -> p c n", p=128)
                    load_w(wq[:], "bwq", "bwq", qv[:, :, 0:1024])
                    load_w(wqs[:], "bwqs", "bwqs", sv[:, :, 0:1024])
                    for g in range(4):
                        for half in range(2):
                            load_w(wk[:, :, g * 128 + half * 64:g * 128 + half * 64 + 64], "bwk", "bwk%d" % (g * 2 + half),
                                   qv[:, :, 1024 + g * 64:1024 + (g + 1) * 64])
                            load_w(wks[:, :, g * 128 + half * 64:g * 128 + half * 64 + 64], "bwks", "bwks%d" % (g * 2 + half),
                                   sv[:, :, 1024 + g * 64:1024 + (g + 1) * 64])
                    load_w(wv[:], "bwv", "bwv", qv[:, :, 1280:1536])
                    for it, gi in enumerate(range(len(groups))):
                        s0, n = groups[gi]
                        lw = 1 if gi == 0 else 0
                        ht, htok = hts[it % 2], ("bht", it % 2)
                        load_h(ht, htok, "bldh%d" % (it % 2), gi)
                        norm_mod(ht, htok, n, xn, "bxn", gcol(l, 0, lw), mcol(l, 0, lw), W)
                        rope = gi > 0
                        if rope:
                            lp = s0 - CTX
                            P.add("sp", lambda h, lp=lp: h.dma_start(out=rc[:], in_=ropeC[:, lp:lp + 512]), writes=["brc"], dma="brc")
                            P.add("sp", lambda h, lp=lp: h.dma_start(out=rs[:], in_=ropeS[:, lp:lp + 512]), writes=["brs"], dma="brs")
                        for (wa, wb, nch, dst, dtok, cw) in ((wq, wqs, 8, qt, "bqt", 1024), (wk, wks, 4, kt, "bkt", 512)):
                            watok = "bwq" if nch == 8 else "bwk"
                            wbtok = "bwqs" if nch == 8 else "bwks"
                            for c in range(nch):
                                for k in range(KC):
                                    P.add("pe", lambda h, wa=wa, c=c, k=k, n=n: h.matmul(
                                        ps[0][:, :n], wa[:, k, c * 128:(c + 1) * 128], xn[:, k, :n],
                                        start=(k == 0), stop=(k == KC - 1)), reads=[watok, "bxn"], writes=[PS(0)])
                                if rope:
                                    for k in range(KC):
                                        P.add("pe", lambda h, wb=wb, c=c, k=k, n=n: h.matmul(
                                            ps[1][:, :n], wb[:, k, c * 128:(c + 1) * 128], xn[:, k, :n],
                                            start=(k == 0), stop=(k == KC - 1)), reads=[wbtok, "bxn"], writes=[PS(1)])
                                    P.add("dve", lambda h, n=n: h.tensor_tensor(out=t1[:, :n], in0=ps[0][:, :n], in1=rc[:, :n], op=ALU.mult),
                                          reads=[PS(0), "brc"], writes=["bt1"])
                                    P.add("dve", lambda h, n=n: h.tensor_tensor(out=t2[:, :n], in0=ps[1][:, :n], in1=rs[:, :n], op=ALU.mult),
                                          reads=[PS(1), "brs"], writes=["bt2"])
                                    P.add("pool", lambda h, dst=dst, c=c, n=n: h.tensor_tensor(out=dst[:, c, :n], in0=t1[:, :n], in1=t2[:, :n], op=ALU.add),
                                          reads=["bt1", "bt2"], writes=[dtok])
                                else:
                                    P.add("act", lambda h, dst=dst, c=c, n=n: h.activation(out=dst[:, c, :n], in_=ps[0][:, :n], func=AF.Copy),
                                          reads=[PS(0)], writes=[dtok])
                        for tb in range(n // 128):
                            for k in range(KC):
                                P.add("pe", lambda h, tb=tb, k=k: h.matmul(
                                    ps[2][:, :256], xn[:, k, tb * 128:(tb + 1) * 128], wv[:, k, :],
                                    start=(k == 0), stop=(k == KC - 1)), reads=["bwv", "bxn"], writes=[PS(2)])
                            P.add("act", lambda h, tb=tb: h.activation(out=vt[:, tb, :], in_=ps[2][:, :256], func=AF.Copy),
                                  reads=[PS(2)], writes=["bvt"])
                        P.add("pool", lambda h, s0=s0, n=n: h.dma_start(out=qSv[:, :, s0:s0 + n], in_=qt[:, :, :n]),
                              reads=["bqt"], writes=[("qS", gi)], dma="bstq")
                        P.add("pool", lambda h, s0=s0, n=n: h.dma_start(out=kSv[:, :, s0:s0 + n], in_=kt[:, :, :n]),
                              reads=["bkt"], writes=["kS"], dma="bstk")
                        nb = n // 128
                        P.add("pool", lambda h, s0=s0, n=n, nb=nb: h.dma_start(
                            out=vS[s0:s0 + n, :].rearrange("(b p) d -> p b d", p=128), in_=vt[:, :nb, :]),
                            reads=["bvt"], writes=["vS"], dma="bstv")
                    P.barrier(dummy)
                kall = sbt("bkall", [128, 4, T + 256], BF16)
                vall = sbt("bvall", [128, NCK + 2, 256], BF16)
                wo = sbt("bwo", [64, 16, 1024], BF16)
                esk = sbt("besk", [128, 16], F32)
                qz = [sbt("bqz%d" % i, [128, 8, 512], BF16) for i in range(2)]
                hts = [sbt("b2ht0", [128, 8, 512], F32)] * 2
                pt = [sbt("bpt%d" % i, [128, 512], BF16) for i in range(5)]
                oT = sbt("boT", [64, 16, 512], BF16)
                rd = sbt("brd", [64, 512], F32)
                for i in range(2):
                    P.add("dve", lambda h, i=i: h.memset(qz[i][:], 0.0), writes=["bqg"])
                P.add("sp", lambda h: h.dma_start(out=kall[:, :, 0:T], in_=kSv[:, :, :]), reads=["kS"], writes=["bkall"], dma="bldk")
                P.add("sp", lambda h: h.dma_start(out=vall[:, 0:NCK, :], in_=vS.rearrange("(b p) d -> p b d", p=128)),
                      reads=["vS"], writes=["bvall"], dma="bldv")
                if R > 1:
                    kcand = sbt("bkcand", [128, 2 * R, 4, 128], BF16)
                    vcand = sbt("bvcand", [128, 2 * R, 256], BF16)
                    for f, c0 in enumerate((CTX, T - 128)):
                        P.add("sp", lambda h, f=f, c0=c0: h.dma_start(out=xkI[f * 512:(f + 1) * 512, :], in_=kS[:, c0:c0 + 128]),
                              reads=["kS"], writes=[("xi", "k")], dma="bxk%d" % f)
                        P.add("sp", lambda h, f=f, c0=c0: h.dma_start(out=xvI[f * 128:(f + 1) * 128, :], in_=vS[c0:c0 + 128, :]),
                              reads=["vS"], writes=[("xi", "v")], dma="bxv%d" % f)
                    allgather("k", xkI, xkO)
                    allgather("v", xvI, xvO)
                    for i in range(R):
                        for f in range(2):
                            P.add("sp", lambda h, i=i, f=f: h.dma_start(
                                out=kcand[:, 2 * i + f, :, :], in_=xkO[i * 1024 + f * 512:i * 1024 + (f + 1) * 512, :].rearrange("(g p) t -> p g t", p=128)),
                                reads=[("xo", "k")], writes=["bkcand"], dma="bck%d" % (2 * i + f))
                            P.add("sp", lambda h, i=i, f=f: h.dma_start(
                                out=vcand[:, 2 * i + f, :], in_=xvO[i * 256 + f * 128:i * 256 + (f + 1) * 128, :]),
                                reads=[("xo", "v")], writes=["bvcand"], dma="bcv%d" % (2 * i + f))
                    for side, (selo, f) in enumerate(((0, 1), (4, 0))):
                        kd = kall[:, :, T + side * 128:T + (side + 1) * 128]
                        vd = vall[:, NCK + side, :]
                        for i in range(R):
                            sc = selt[:, selo + i:selo + i + 1]
                            if i == 0:
                                P.add("dve", lambda h, kd=kd, sc=sc, i=i, f=f: h.tensor_scalar(
                                    out=kd, in0=kcand[:, 2 * i + f, :, :], scalar1=sc, scalar2=None, op0=ALU.mult),
                                    reads=["bkcand", "sel"], writes=["bkall"])
                                P.add("dve", lambda h, vd=vd, sc=sc, i=i, f=f: h.tensor_scalar(
                                    out=vd, in0=vcand[:, 2 * i + f, :], scalar1=sc, scalar2=None, op0=ALU.mult),
                                    reads=["bvcand", "sel"], writes=["bvall"])
                            else:
                                P.add("dve", lambda h, kd=kd, sc=sc, i=i, f=f: h.scalar_tensor_tensor(
                                    out=kd, in0=kcand[:, 2 * i + f, :, :], scalar=sc, in1=kd, op0=ALU.mult, op1=ALU.add),
                                    reads=["bkcand", "sel", "bkall"], writes=["bkall"])
                                P.add("dve", lambda h, vd=vd, sc=sc, i=i, f=f: h.scalar_tensor_tensor(
                                    out=vd, in0=vcand[:, 2 * i + f, :], scalar=sc, in1=vd, op0=ALU.mult, op1=ALU.add),
                                    reads=["bvcand", "sel", "bvall"], writes=["bvall"])
                load_w(wo[:], "bwo", "bwo", b_w_out[j].rearrange("(h d) n -> d h n", d=64))
                P.add("sp", lambda h: h.dma_start(out=esk[:], in_=b_sinks_r[:, :]), writes=["besk"], dma="besk")
                P.add("act", lambda h: h.activation(out=esk[:], in_=esk[:], func=AF.Exp), reads=["besk"], writes=["besk"])
                gl = [gi for gi in range(len(groups)) if ctx_out or gi > 0]
                if DBG == 'b1':
                    gl = []
                for it, gi in enumerate(gl):
                    s0, n = groups[gi]
                    lw = 1 if gi == 0 else 0
                    ht, htok = hts[0], ("b2ht", 0)
                    load_h(ht, htok, "b2ldh0", gi)
                    for i in range(2):
                        P.add("sp", lambda h, s0=s0, n=n, i=i: h.dma_start(out=qz[i][i * 64:(i + 1) * 64, :, :n], in_=qSv[i * 64:(i + 1) * 64, :, s0:s0 + n]),
                              reads=[("qS", gi)], writes=["bqg"], dma="bldq%d" % i)
                    for qb in range(n // 128):
                        c0 = s0 + qb * 128
                        if gi == 0:
                            kbs = [(0, None), (128, None)]
                        else:
                            nbk = (c0 - CTX) // 128
                            kbs = []
                            if nbk > 0:
                                kbs.append((c0 - 128, "L"))
                            elif R > 1:
                                kbs.append((T, "EL"))
                            kbs.append((c0, None))
                            if nbk < NB - 1:
                                kbs.append((c0 + 128, "R"))
                            elif R > 1:
                                kbs.append((T + 128, "ER"))
                            kbs += [(0, None), (128, None)]
                        for g in range(4):
                            for bi, (kc0, mk) in enumerate(kbs):
                                for hh in range(4):
                                    hd = 4 * g + hh
                                    qc, half = hd // 2, hd % 2
                                    p0 = half * 64
                                    P.add("pe", lambda h, bi=bi, hh=hh, g=g, kc0=kc0, qc=qc, half=half, qb=qb: h.matmul(
                                        ps[bi][:, hh * 128:(hh + 1) * 128], kall[:, g, kc0:kc0 + 128],
                                        qz[half][:, qc, qb * 128:(qb + 1) * 128], start=True, stop=True),
                                        reads=["bkall", "bqg"], writes=[PS(bi)])
                                P.add("act", lambda h, bi=bi: h.activation(out=pt[bi][:], in_=ps[bi][:, :], func=AF.Exp, scale=0.125),
                                      reads=[PS(bi)], writes=[("bpt", bi)])
                                if mk is not None and DBG != 'b2':
                                    mo = 512 if mk in ("L", "EL") else 0
                                    mt = mske if mk in ("EL", "ER") else msk
                                    P.add("pool", lambda h, bi=bi, mo=mo, mt=mt: h.tensor_tensor(
                                        out=pt[bi][:], in0=pt[bi][:], in1=mt[:, mo:mo + 512], op=ALU.mult),
                                        reads=[("bpt", bi), "msk", "mske"], writes=[("bpt", bi)])
                            nk = len(kbs)
                            for bi, (kc0, mk) in enumerate(kbs):
                                P.add("pe", lambda h, bi=bi, kc0=kc0, g=g, nk=nk: h.matmul(
                                    ps[5][0:64, :], vall[:, kc0 // 128, g * 64:(g + 1) * 64], pt[bi][:],
                                    start=(bi == 0), stop=(bi == nk - 1)), reads=["bvall", ("bpt", bi)], writes=[PS(5)])
                            for bi, (kc0, mk) in enumerate(kbs):
                                P.add("pe", lambda h, bi=bi, nk=nk: h.matmul(
                                    ps[6][0:64, :], ones_bf[:, 0:64], pt[bi][:],
                                    start=(bi == 0), stop=(bi == nk - 1)), reads=["ones_bf", ("bpt", bi)], writes=[PS(6)])
                            P.add("dve", lambda h, g=g: h.tensor_tensor(
                                out=rd[:, :].rearrange("p (h w) -> p h w", h=4), in0=ps[6][0:64, :].rearrange("p (h w) -> p h w", h=4),
                                in1=esk[0:64, 4 * g:4 * g + 4].unsqueeze(2).to_broadcast([64, 4, 128]), op=ALU.add),
                                reads=[PS(6), "besk"], writes=["brd"])
                            P.add("dve", lambda h: h.reciprocal(out=rd[:], in_=rd[:]), reads=["brd"], writes=["brd"])
                            P.add("dve", lambda h, g=g, qb=qb: h.tensor_tensor(
                                out=oT[:, 4 * g:4 * g + 4, qb * 128:(qb + 1) * 128], in0=ps[5][0:64, :].rearrange("p (h w) -> p h w", h=4),
                                in1=rd[:, :].rearrange("p (h w) -> p h w", h=4), op=ALU.mult),
                                reads=[PS(5), "brd"], writes=["boT"])
                    for oc in range(8):
                        pi = oc % 3
                        for hd in range(16):
                            P.add("pe", lambda h, pi=pi, oc=oc, hd=hd, n=n: h.matmul(
                                ps[pi][:, :n], wo[:, hd, oc * 128:(oc + 1) * 128], oT[:, hd, :n],
                                start=(hd == 0), stop=(hd == 15)), reads=["bwo", "boT"], writes=[PS(pi)])
                        resid(pi, ht, htok, oc, n, mcol(l, 2, lw))
                    store_h(ht, htok, "b2sth0", gi)
                P.barrier(dummy)

        def mlstm_phase(l, j):
            ctx_out = l != 3
            with ExitStack() as st:
                def sbt(name, shape, dt):
                    return st.enter_context(sbuf_u(name, list(shape), dt))
                with ExitStack() as st1:
                    def sb1(name, shape, dt):
                        return st1.enter_context(sbuf_u(name, list(shape), dt))
                    win = sb1("awin", [128, 8, 3072], BF16)
                    wg = sb1("awg", [128, 8, 32], BF16)
                    bgr = sb1("abgr", [128, 32], F32)
                    hts = [sb1("aht%d" % i, [128, 8, 512], F32) for i in range(2)]
                    xn = sb1("axn", [128, 8, 512], BF16)
                    qT = sb1("aqT", [64, 8, 512], BF16)
                    kT = sb1("akT", [64, 8, 512], BF16)
                    ktok = sb1("aktok", [128, 512], BF16)
                    sgo = sb1("asgo", [128, 1024], BF16)
                    gt = sb1("agt", [128, 32], F32)
                    spt = sb1("aspt", [128, 16], F32)
                    At = sb1("aAt", [128, 16], F32)
                    eit = sb1("aeit", [128, 16], F32)
                    ett = sb1("aett", [128, 16], F32)
                    VA = sb1("aVA", [128, 16, 130], BF16)
                    Ssb2 = [sb1("aSsb%d" % i, [128, 128], BF16) for i in range(2)]
                    Sm2 = [[sb1("aSm%d_%d" % (i, d_), [128, 128], BF16) for d_ in range(2)] for i in range(2)]
                    part = sb1("apart", [128, 8, 258], F32)
                    Ut = sb1("aUt", [64, 8, 258], F32)
                    W = dict(sq=sb1("asq", [128, 8, 512], BF16), tmp=sb1("atmp", [128, 8, 512], F32),
                             rstd=sb1("arstd", [128, 512], F32), psn=6)
                    wv_ = a_w_in[j].rearrange("(c p) n -> p c n", p=128)
                    gv_ = a_w_gate[j].rearrange("(c p) n -> p c n", p=128)
                    load_w(win[:], "awin", "awin", wv_)
                    for di, so in enumerate((0, 16, 8, 24)):
                        load_w(wg[:, :, di * 8:(di + 1) * 8], "awg", "awg%d" % di, gv_[:, :, so:so + 8])
                    P.add("sp", lambda h: h.dma_start(out=bgr[:], in_=a_b_gate_r[:, j * 32:(j + 1) * 32]), writes=["abgr"], dma="abgr")
                    for it, gi in enumerate(range(len(groups))):
                        s0, n = groups[gi]
                        lw = 1 if gi == 0 else 0
                        ht, htok = hts[it % 2], ("aht", it % 2)
                        load_h(ht, htok, "aldh%d" % (it % 2), gi)
                        norm_mod(ht, htok, n, xn, "axn", gcol(l, 0, lw), mcol(l, 0, lw), W)
                        for hd in range(8):
                            for qi, (off, dst, dtok, sc) in enumerate(((0, qT, "aqT", 0.125), (512, kT, "akT", 1.0))):
                                pq = (0, 5)[qi]
                                for k in range(KC):
                                    P.add("pe", lambda h, off=off, hd=hd, k=k, n=n, pq=pq: h.matmul(
                                        ps[pq][0:64, :n], win[:, k, off + hd * 64:off + (hd + 1) * 64], xn[:, k, :n],
                                        start=(k == 0), stop=(k == KC - 1)), reads=["awin", "axn"], writes=[PS(pq)])
                                P.add("act", lambda h, dst=dst, hd=hd, sc=sc, n=n, pq=pq: h.activation(
                                    out=dst[:, hd, :n], in_=ps[pq][0:64, :n], func=AF.Copy, scale=sc),
                                    reads=[PS(pq)], writes=[dtok])
                        P.add("pool", lambda h, s0=s0, n=n: h.dma_start(
                            out=aqS.rearrange("p (h t) -> p h t", h=8)[:, :, s0:s0 + n], in_=qT[:, :, :n]),
                            reads=["aqT"], writes=[("aqS", gi)], dma="astq")
                        for tb in range(n // 128):
                            ck = (s0 + tb * 128) // 128
                            need_out = ctx_out or gi > 0
                            tsl = slice(tb * 128, (tb + 1) * 128)
                            def tokproj(pi, c0, ncol, wt=win, wtok="awin", tsl=tsl):
                                for k in range(KC):
                                    P.add("pe", lambda h, k=k: h.matmul(
                                        ps[pi][:, :ncol], xn[:, k, tsl], wt[:, k, c0:c0 + ncol],
                                        start=(k == 0), stop=(k == KC - 1)), reads=[wtok, "axn"], writes=[PS(pi)])
                            tokproj(1, 512, 512)
                            P.add("act", lambda h: h.activation(out=ktok[:], in_=ps[1][:, :], func=AF.Copy),
                                  reads=[PS(1)], writes=["aktok"])
                            tokproj(2, 1024, 512)
                            tokproj(3, 1536, 512)
                            if need_out:
                                for hf in range(2):
                                    po = (4, 0)[hf]
                                    tokproj(po, 2048 + hf * 512, 512)
                                    P.add("act", lambda h, hf=hf, po=po: h.activation(out=sgo[:, hf * 512:(hf + 1) * 512], in_=ps[po][:, :], func=AF.Sigmoid),
                                          reads=[PS(po)], writes=["asgo"])
                                P.add("pool", lambda h, ck=ck: h.dma_start(out=sgS[ck * 128:(ck + 1) * 128, :], in_=sgo[:]),
                                      reads=["asgo"], writes=[("sgS", ck)], dma="astsg")
                            tokproj(5, 0, 32, wg, "awg")
                            P.add("dve", lambda h: h.tensor_tensor(out=gt[:], in0=ps[5][:, 0:32], in1=bgr[:], op=ALU.add),
                                  reads=[PS(5), "abgr"], writes=["agt"])
                            P.add("act", lambda h: h.activation(out=spt[:], in_=gt[:, 16:32], func=AF.Exp, scale=-1.0),
                                  reads=["agt"], writes=["aspt"])
                            P.add("act", lambda h: h.activation(out=spt[:], in_=spt[:], func=AF.Ln, bias=oneb[:, 0:1], scale=1.0),
                                  reads=["aspt", "oneb"], writes=["aspt"])
                            P.add("pe", lambda h: h.matmul(ps[6][:, 0:8], tri[:, 0:128], spt[:, 0:8], start=True, stop=True),
                                  reads=["tri", "aspt"], writes=[PS(6)])
                            P.add("pe", lambda h: h.matmul(ps[6][:, 8:16], tri[:, 128:256], spt[:, 8:16], start=True, stop=True),
                                  reads=["tri", "aspt"], writes=[PS(6)])
                            P.add("pe", lambda h: h.matmul(ps[6][:, 16:32], ones_f[:], spt[:, 0:16], start=True, stop=True),
                                  reads=["ones_f", "aspt"], writes=[PS(6)])
                            P.add("dve", lambda h: h.tensor_tensor(out=At[:], in0=ps[6][:, 0:16], in1=gt[:, 0:16], op=ALU.add),
                                  reads=[PS(6), "agt"], writes=["aAt"])
                            P.add("act", lambda h: h.activation(out=At[:], in_=At[:], func=AF.Exp), reads=["aAt"], writes=["aAt"])
                            P.add("act", lambda h: h.activation(out=eit[:], in_=ps[6][:, 0:16], func=AF.Exp), reads=[PS(6)], writes=["aeit"])
                            P.add("act", lambda h: h.activation(out=ett[:], in_=ps[6][:, 16:32], func=AF.Exp, scale=-1.0), reads=[PS(6)], writes=["aett"])
                            P.add("pool", lambda h, ck=ck: h.dma_start(out=eiS[ck * 128:(ck + 1) * 128, :], in_=eit[:]),
                                  reads=["aeit"], writes=[("eiS", ck)], dma="astei")
                            P.add("pool", lambda h, ck=ck: h.dma_start(out=etS[ck * 64:(ck + 1) * 64, :], in_=ett[0:64, :]),
                                  reads=["aett"], writes=[("etS", ck)], dma="astet")
                            for d in range(2):
                                for bk in range(2):
                                    i4 = d * 8 + bk * 4
                                    P.add("dve", lambda h, i4=i4, bk=bk: h.tensor_tensor(
                                        out=VA[:, i4:i4 + 4, 0:128], in0=ps[2 + bk][:, :].rearrange("p (h w) -> p h w", h=4),
                                        in1=At[:, i4:i4 + 4].unsqueeze(2).to_broadcast([128, 4, 128]), op=ALU.mult),
                                        reads=[PS(2 + bk), "aAt"], writes=[("aVA", i4 + q_) for q_ in range(4)])
                            P.add("dve", lambda h: h.tensor_copy(out=VA[:, :, 128], in_=At[:]),
                                  reads=["aAt"], writes=[("aVA", i) for i in range(16)])
                            def stage_S(hd, tsl=tsl):
                                p = hd % 2
                                pS = (0, 5)[p]
                                P.add("pe", lambda h: h.matmul(ps[pS][:, 0:128], kT[:, hd, tsl], qT[:, hd, tsl], start=True, stop=True),
                                      reads=["akT", "aqT"], writes=[PS(pS)])
                                P.add("act", lambda h: h.activation(out=Ssb2[p][:], in_=ps[pS][:, 0:128], func=AF.Copy),
                                      reads=[PS(pS)], writes=[("aSsb", p)])
                                P.add("dve", lambda h: h.tensor_tensor(out=Sm2[p][0][:], in0=Ssb2[p][:], in1=msk[:, 0:128], op=ALU.mult),
                                      reads=[("aSsb", p), "msk"], writes=[("aSm", p, 0)])
                                P.add("pool", lambda h: h.tensor_tensor(out=Sm2[p][1][:], in0=Ssb2[p][:], in1=msk[:, 512:640], op=ALU.mult),
                                      reads=[("aSsb", p), "msk"], writes=[("aSm", p, 1)])

                            def stage_O(hd):
                                p = hd % 2
                                pO = (1, 6)[p]
                                for d in range(2):
                                    P.add("pe", lambda h, d=d: h.matmul(
                                        ps[pO][:, d * 129:(d + 1) * 129], Sm2[p][d][:], VA[:, d * 8 + hd, 0:129], start=True, stop=True),
                                        reads=[("aSm", p, d), ("aVA", d * 8 + hd)], writes=[PS(pO)])
                                P.add("act", lambda h: h.activation(out=part[:, hd, :], in_=ps[pO][:, 0:258], func=AF.Copy),
                                      reads=[PS(pO)], writes=["apart"])

                            def stage_U(hd):
                                for d in range(2):
                                    P.add("pe", lambda h, d=d: h.matmul(
                                        ps[4][0:64, d * 129:(d + 1) * 129], ktok[:, hd * 64:(hd + 1) * 64], VA[:, d * 8 + hd, 0:129],
                                        start=True, stop=True), reads=["aktok", ("aVA", d * 8 + hd)], writes=[PS(4)])
                                P.add("dve", lambda h: h.tensor_copy(out=Ut[:, hd, :], in_=ps[4][0:64, 0:258]),
                                      reads=[PS(4)], writes=["aUt"])

                            if need_out:
                                stage_S(0)
                            for hd in range(8):
                                if need_out and hd + 1 < 8:
                                    stage_S(hd + 1)
                                stage_U(hd)
                                if need_out:
                                    stage_O(hd)
                            if need_out:
                                P.add("pool", lambda h, ck=ck: h.dma_start(out=opS[ck * 128:(ck + 1) * 128, :], in_=part[:]),
                                      reads=["apart"], writes=[("opS", ck)], dma="astop")
                            P.add("pool", lambda h, ck=ck: h.dma_start(out=usS[ck * 64:(ck + 1) * 64, :], in_=Ut[:]),
                                  reads=["aUt"], writes=[("usS", ck)], dma="astus")
                    P.barrier(dummy)
                if DBG == 'a1':
                    return
                with ExitStack() as st2:
                    def sb2(name, shape, dt):
                        return st2.enter_context(sbuf_u(name, list(shape), dt))
                    nctx = CTX // 128
                    NL = NCK - nctx
                    UBc = sb2("aUBc", [128, nctx, 8, 129], F32)
                    EBc = sb2("aEBc", [128, nctx, 8], F32)
                    UBl = sb2("aUBl", [128, NL, 8, 129], F32)
                    EBl = sb2("aEBl", [128, NL, 8], F32)
                    stt = sb2("astt", [128, 8, 129], F32)
                    stb = [sb2("astb%d" % i, [128, 8, 129], BF16) for i in range(2)]
                    usv = usS.rearrange("(c p) (h w) -> c p h w", p=64, w=258)
                    etv = etS.rearrange("(c p) e -> c p e", p=64)

                    def chunk_of(d, kind, si):
                        if kind == "ctx":
                            return si if d == 0 else nctx - 1 - si
                        return nctx + si if d == 0 else NCK - 1 - si

                    nld = 0
                    for kind, UB, EB, nn in (("ctx", UBc, EBc, nctx), ("lat", UBl, EBl, NL)):
                        for si in range(nn):
                            for d in range(2):
                                ck = chunk_of(d, kind, si)
                                P.add("sp", lambda h, UB=UB, si=si, d=d, ck=ck: h.dma_start(
                                    out=UB[d * 64:(d + 1) * 64, si, :, :], in_=usv[ck, :, :, d * 129:(d + 1) * 129]),
                                    reads=[("usS", ck)], writes=[("aUB", kind, si), ("alduk", nld % 8)], dma="aldu%d" % (nld % 8))
                                P.add("sp", lambda h, EB=EB, si=si, d=d, ck=ck: h.dma_start(
                                    out=EB[d * 64:(d + 1) * 64, si, :], in_=etv[ck, :, d * 8:(d + 1) * 8]),
                                    reads=[("etS", ck)], writes=[("aEB", kind, si), ("aldek", nld % 8)], dma="alde%d" % (nld % 8))
                                nld += 1
                    P.add("dve", lambda h: h.memset(stt[:], 0.0), writes=["astt"])
                    cnt = [0]

                    def step(kind, UB, EB, si, write_cs, pp=None):
                        if write_cs:
                            b = cnt[0] % 2
                            cnt[0] += 1
                            P.add("dve", lambda h: h.tensor_copy(out=stb[b][:], in_=stt[:]), reads=["astt"], writes=[("astb", b)])
                            for d in range(2):
                                ck = chunk_of(d, kind, si)
                                P.add("sp", lambda h, d=d, ck=ck: h.dma_start(
                                    out=csS[(d * NCK + ck) * 64:(d * NCK + ck + 1) * 64, :], in_=stb[b][d * 64:(d + 1) * 64, :, :]),
                                    reads=[("astb", b)], writes=[("csS", d, ck)], dma="astcs%d%d" % (d, b))
                        P.add("dve", lambda h: h.tensor_tensor(out=stt[:], in0=stt[:], in1=UB[:, si, :, :], op=ALU.add),
                              reads=["astt", ("aUB", kind, si)], writes=["astt"])
                        P.add("dve", lambda h: h.tensor_tensor(
                            out=stt[:], in0=stt[:], in1=EB[:, si, :].unsqueeze(2).to_broadcast([128, 8, 129]), op=ALU.mult),
                            reads=["astt", ("aEB", kind, si)], writes=["astt"])
                        if pp is not None:
                            P.add("pool", lambda h: h.tensor_tensor(out=pp[:], in0=pp[:], in1=EB[:, si, :], op=ALU.mult),
                                  reads=["app", ("aEB", kind, si)], writes=["app"])

                    for si in range(nctx):
                        step("ctx", UBc, EBc, si, True)
                    if R == 1:
                        for si in range(NL):
                            step("lat", UBl, EBl, si, True)
                    else:
                        X = sb2("actx", [128, 8, 129], F32)
                        pp = sb2("app", [128, 8], F32)
                        cin = sb2("acin", [128, 8, 129], F32)
                        tmpc = sb2("atmpc", [128, 8, 129], F32)
                        GS = sb2("aGS", [128, R, 8, 129], F32)
                        GP = sb2("aGP", [128, R, 8], F32)
                        P.add("dve", lambda h: h.tensor_copy(out=X[:], in_=stt[:]), reads=["astt"], writes=["actx"])
                        P.add("dve", lambda h: h.memset(stt[:], 0.0), writes=["astt"])
                        P.add("pool", lambda h: h.memset(pp[:], 1.0), writes=["app"])
                        for si in range(NL):
                            step("lat", UBl, EBl, si, False, pp)
                        for d in range(2):
                            P.add("sp", lambda h, d=d: h.dma_start(out=xaI[:, d * 1032:(d + 1) * 1032], in_=stt[d * 64:(d + 1) * 64, :, :]),
                                  reads=["astt"], writes=[("xi", "a")], dma="axs%d" % d)
                            P.add("sp", lambda h, d=d: h.dma_start(out=xaI[:, 2064 + d * 8:2064 + (d + 1) * 8], in_=pp[d * 64:(d + 1) * 64, :]),
                                  reads=["app"], writes=[("xi", "a")], dma="axp%d" % d)
                        allgather("a", xaI, xaO)
                        xav = xaO.rearrange("(i p) w -> i p w", p=64)
                        for n_ in range(R):
                            for d in range(2):
                                i = n_ if d == 0 else R - 1 - n_
                                P.add("sp", lambda h, n_=n_, d=d, i=i: h.dma_start(
                                    out=GS[d * 64:(d + 1) * 64, n_, :, :], in_=xav[i, :, d * 1032:(d + 1) * 1032]),
                                    reads=[("xo", "a")], writes=["aGS"], dma="axg%d" % (2 * n_ + d))
                                P.add("sp", lambda h, n_=n_, d=d, i=i: h.dma_start(
                                    out=GP[d * 64:(d + 1) * 64, n_, :], in_=xav[i, :, 2064 + d * 8:2064 + (d + 1) * 8]),
                                    reads=[("xo", "a")], writes=["aGS"], dma="axh%d" % (2 * n_ + d))
                        for n_ in range(R):
                            oh = selt[:, 12 + n_:13 + n_]
                            if n_ == 0:
                                P.add("dve", lambda h, oh=oh: h.tensor_scalar(out=cin[:], in0=X[:], scalar1=oh, scalar2=None, op0=ALU.mult),
                                      reads=["actx", "sel"], writes=["acin"])
                            else:
                                P.add("dve", lambda h, oh=oh: h.tensor_scalar(out=tmpc[:], in0=X[:], scalar1=oh, scalar2=None, op0=ALU.mult),
                                      reads=["actx", "sel"], writes=["atmpc"])
                                P.add("dve", lambda h: h.tensor_tensor(out=cin[:], in0=cin[:], in1=tmpc[:], op=ALU.add),
                                      reads=["atmpc", "acin"], writes=["acin"])
                            if n_ == R - 1:
                                break
                            P.add("dve", lambda h, n_=n_: h.tensor_tensor(
                                out=X[:], in0=X[:], in1=GP[:, n_, :].unsqueeze(2).to_broadcast([128, 8, 129]), op=ALU.mult),
                                reads=["actx", "aGS"], writes=["actx"])
                            P.add("dve", lambda h, n_=n_: h.tensor_tensor(out=X[:], in0=X[:], in1=GS[:, n_, :, :], op=ALU.add),
                                  reads=["actx", "aGS"], writes=["actx"])
                        P.add("dve", lambda h: h.tensor_copy(out=stt[:], in_=cin[:]), reads=["acin"], writes=["astt"])
                        for si in range(NL):
                            step("lat", UBl, EBl, si, True)
                    P.barrier(dummy)
                if DBG == 'a2':
                    return
                wout = sbt("a2wout", [128, 8, 1024], BF16)
                hgr = sbt("a2hgr", [128, 1024], F32)
                hts = [sbt("a2ht%d" % i, [128, 8, 512], F32) for i in range(2)]
                part2 = [sbt("a2part%d" % i, [128, 8, 258], F32) for i in range(2)]
                qc = [sbt("a2qc%d" % i, [64, 8, 128], BF16) for i in range(2)]
                cst = [sbt("a2cs%d" % i, [64, 2, 8 * 129], BF16) for i in range(2)]
                eic = [sbt("a2ei%d" % i, [128, 16], F32) for i in range(2)]
                sgc = [sbt("a2sg%d" % i, [128, 1024], BF16) for i in range(2)]
                tot = sbt("a2tot", [128, 8, 258], F32)
                rr = sbt("a2rr", [128, 16], F32)
                hs = sbt("a2hs", [128, 8, 128], F32)
                sq2 = sbt("a2sq", [128, 8, 128], F32)
                ss = sbt("a2ss", [128, 8], F32)
                hgo = sbt("a2hgo", [128, 1024], F32)
                hg = sbt("a2hg", [128, 1024], BF16)
                hgT = sbt("a2hgT", [128, 8, 512], BF16)
                load_w(wout[:], "a2wout", "a2wout", a_w_out[j].rearrange("(c p) n -> p c n", p=128))
                P.add("sp", lambda h: h.dma_start(out=hgr[:], in_=a_head_g_r[:, j * D:(j + 1) * D]), writes=["a2hgr"], dma="a2hgr")
                gl = [gi for gi in range(len(groups)) if ctx_out or gi > 0]
                hs2 = [hs, sbt("a2hs1", [128, 8, 128], F32)]
                sqb = sbt("a2sqb", [128, 8, 128], F32)
                sq2b = [sq2, sbt("a2sq1", [128, 8, 128], F32)]
                hgob = [hgo, sbt("a2hgo1", [128, 1024], F32)]
                chunks = []
                for it, gi in enumerate(gl):
                    s0, n = groups[gi]
                    for tb in range(n // 128):
                        chunks.append((it, gi, tb, (s0 + tb * 128) // 128, tb == n // 128 - 1))

                def stage_F(idx):
                    it, gi, tb, ck, last = chunks[idx]
                    b = idx % 2
                    hsb, hstok = hs2[b], ("a2hs", b)
                    if tb == 0:
                        load_h(hts[it % 2], ("a2ht", it % 2), "a2ldh%d" % (it % 2), gi)
                    P.add("sp", lambda h: h.dma_start(out=part2[b][:], in_=opS[ck * 128:(ck + 1) * 128, :]),
                          reads=[("opS", ck)], writes=[("a2part", b)], dma="a2ldp%d" % b)
                    P.add("sp", lambda h: h.dma_start(
                        out=qc[b][:], in_=aqS.rearrange("p (h t) -> p h t", h=8)[:, :, ck * 128:(ck + 1) * 128]),
                        reads=[("aqS", gi)], writes=[("a2qc", b)], dma="a2ldq%d" % b)
                    for d in range(2):
                        P.add("sp", lambda h, d=d: h.dma_start(
                            out=cst[b][:, d, :], in_=csS[(d * NCK + ck) * 64:(d * NCK + ck + 1) * 64, :]),
                            reads=[("csS", d, ck)], writes=[("a2cs", b)], dma="a2ldc%d%d" % (b, d))
                    P.add("sp", lambda h: h.dma_start(out=eic[b][:], in_=eiS[ck * 128:(ck + 1) * 128, :]),
                          reads=[("eiS", ck)], writes=[("a2ei", b)], dma="a2lde%d" % b)
                    P.add("sp", lambda h: h.dma_start(out=sgc[b][:], in_=sgS[ck * 128:(ck + 1) * 128, :]),
                          reads=[("sgS", ck)], writes=[("a2sg", b)], dma="a2lds%d" % b)
                    for hd in range(8):
                        pi = hd % 3
                        for d in range(2):
                            P.add("pe", lambda h, pi=pi, hd=hd, d=d: h.matmul(
                                ps[pi][:, d * 129:(d + 1) * 129], qc[b][:, hd, :], cst[b][:, d, hd * 129:(hd + 1) * 129],
                                start=True, stop=True), reads=[("a2qc", b), ("a2cs", b)], writes=[PS(pi)])
                        P.add("dve", lambda h, pi=pi, hd=hd: h.tensor_tensor(
                            out=tot[:, hd, :], in0=ps[pi][:, 0:258], in1=part2[b][:, hd, :], op=ALU.add),
                            reads=[PS(pi), ("a2part", b)], writes=["a2tot"])
                    for d in range(2):
                        P.add("dve", lambda h, d=d: h.scalar_tensor_tensor(
                            out=rr[:, d * 8:(d + 1) * 8], in0=tot[:, :, d * 129 + 128], scalar=-1.0,
                            in1=tot[:, :, d * 129 + 128], op0=ALU.mult, op1=ALU.max),
                            reads=["a2tot"], writes=["a2rr"])
                        P.add("dve", lambda h, d=d: h.tensor_tensor(
                            out=rr[:, d * 8:(d + 1) * 8], in0=rr[:, d * 8:(d + 1) * 8], in1=eic[b][:, d * 8:(d + 1) * 8], op=ALU.max),
                            reads=["a2rr", ("a2ei", b)], writes=["a2rr"])
                    P.add("dve", lambda h: h.reciprocal(out=rr[:], in_=rr[:]), reads=["a2rr"], writes=["a2rr"])
                    P.add("dve", lambda h: h.tensor_tensor(
                        out=hsb[:], in0=tot[:, :, 0:128], in1=rr[:, 0:8].unsqueeze(2).to_broadcast([128, 8, 128]), op=ALU.mult),
                        reads=["a2tot", "a2rr"], writes=[hstok])
                    P.add("dve", lambda h: h.tensor_tensor(
                        out=sqb[:], in0=tot[:, :, 129:257], in1=rr[:, 8:16].unsqueeze(2).to_broadcast([128, 8, 128]), op=ALU.mult),
                        reads=["a2tot", "a2rr"], writes=["a2sqb"])
                    P.add("dve", lambda h: h.tensor_tensor(out=hsb[:], in0=hsb[:], in1=sqb[:], op=ALU.add),
                          reads=[hstok, "a2sqb"], writes=[hstok])
                    P.add("act", lambda h: h.activation(out=sq2b[b][:], in_=hsb[:], func=AF.Square), reads=[hstok], writes=[("a2sq", b)])
                    P.add("pool", lambda h: h.tensor_tensor(out=hgob[b][:], in0=hgr[:], in1=sgc[b][:], op=ALU.mult),
                          reads=["a2hgr", ("a2sg", b)], writes=[("a2hgo", b)])

                def stage_B(idx):
                    it, gi, tb, ck, last = chunks[idx]
                    b = idx % 2
                    hsb, hstok = hs2[b], ("a2hs", b)
                    P.add("dve", lambda h: h.tensor_reduce(out=ss[:], in_=sq2b[b][:], axis=AX.X, op=ALU.add), reads=[("a2sq", b)], writes=["a2ss"])
                    P.add("act", lambda h: h.activation(out=ss[:], in_=ss[:], func=AF.Sqrt, bias=epsb[:, 0:1], scale=1.0 / 128),
                          reads=["a2ss", "epsb"], writes=["a2ss"])
                    P.add("dve", lambda h: h.reciprocal(out=ss[:], in_=ss[:]), reads=["a2ss"], writes=["a2ss"])
                    P.add("dve", lambda h: h.tensor_tensor(
                        out=hsb[:], in0=hsb[:], in1=ss[:, 0:8].unsqueeze(2).to_broadcast([128, 8, 128]), op=ALU.mult),
                        reads=[hstok, "a2ss"], writes=[hstok])
                    P.add("dve", lambda h: h.tensor_tensor(
                        out=hg[:, :].rearrange("p (h w) -> p h w", h=8), in0=hsb[:], in1=hgob[b][:, :].rearrange("p (h w) -> p h w", h=8), op=ALU.mult),
                        reads=[hstok, ("a2hgo", b)], writes=["a2hg"])
                    for c in range(8):
                        P.add("pe", lambda h, c=c: h.transpose(psb[:, c * 128:(c + 1) * 128], hg[:, c * 128:(c + 1) * 128], ident[:]),
                              reads=["a2hg", "ident"], writes=["psb"])
                    P.add("act", lambda h: h.activation(
                        out=hgT[:, :, tb * 128:(tb + 1) * 128], in_=psb[:, :].rearrange("p (c t) -> p c t", c=8), func=AF.Copy),
                        reads=["psb"], writes=["a2hgT"])
                    if last:
                        s0, n = groups[gi]
                        lw = 1 if gi == 0 else 0
                        ht, htok = hts[it % 2], ("a2ht", it % 2)
                        for oc in range(8):
                            pi = 3 + oc % 3
                            for k in range(KC):
                                P.add("pe", lambda h, pi=pi, oc=oc, k=k: h.matmul(
                                    ps[pi][:, :n], wout[:, k, oc * 128:(oc + 1) * 128], hgT[:, k, :n],
                                    start=(k == 0), stop=(k == KC - 1)), reads=["a2wout", "a2hgT"], writes=[PS(pi)])
                            resid(pi, ht, htok, oc, n, mcol(l, 2, lw))
                        store_h(ht, htok, "a2sth%d" % (it % 2), gi)

                stage_F(0)
                for idx in range(len(chunks)):
                    if idx + 1 < len(chunks):
                        stage_F(idx + 1)
                    stage_B(idx)
                P.barrier(dummy)

        for l in layers:
            kind, j = l % 3, l // 3
            if kind == 0:
                mlstm_phase(l, j)
            elif kind == 1:
                swa_phase(l, j)
            else:
                conv_phase(l, j)
            if not (stop_after_mixer and l == layers[-1]):
                mlp_phase(l)

        with ExitStack() as st:
            def sbt(name, shape, dt):
                return st.enter_context(sbuf_u(name, list(shape), dt))
            hts = [sbt("fht%d" % i, [128, 8, 512], F32) for i in range(2)]
            W = dict(sq=sbt("fsq", [128, 8, 512], BF16), tmp=sbt("ftmp", [128, 8, 512], F32),
                     rstd=sbt("frstd", [128, 512], F32), psn=6)
            if final:
                P.add("dve", lambda h: h.tensor_copy(out=gst_[:, 0:8], in_=fgt[:]), reads=["fgt", "gst"], writes=["gst"])
                for it, gi in enumerate(range(1, len(groups))):
                    s0, n = groups[gi]
                    ht, htok = hts[it % 2], ("fht", it % 2)
                    load_h(ht, htok, "fldh%d" % (it % 2), gi)
                    norm_mod(ht, htok, n, None, None, 0, None, W)
                    P.add("pool", lambda h, s0=s0, n=n: h.dma_start(out=outTv[:, :, s0 - CTX:s0 - CTX + n], in_=W["tmp"][:, :, :n]),
                          reads=[("tmp", c) for c in range(8)], writes=[("out", gi)], dma="fst")
            else:
                for it, gi in enumerate(range(len(groups))):
                    s0, n = groups[gi]
                    ht, htok = hts[it % 2], ("fht", it % 2)
                    load_h(ht, htok, "fldh%d" % (it % 2), gi)
                    P.add("sp", lambda h, s0=s0, n=n, ht=ht: h.dma_start(out=outTv[:, :, s0:s0 + n], in_=ht[:, :, :n]),
                          reads=[htok], writes=[("out", gi)], dma="fst%d" % (it % 2))
        info = P.emit()
    return nc, info


def _fm(v):
    v = np.asarray(v, np.float32)
    lead = v.shape[:-1]
    n = v.shape[-1] // 128
    a = v.reshape(lead + (n, 128))
    a = np.moveaxis(a, -1, 0)
    return np.ascontiguousarray(a.reshape(128, -1))


def _rope_tables(LAT):
    t = np.arange(LAT)
    row = (t // 64).astype(np.float64)
    col = (t % 64).astype(np.float64)
    inv = 10000.0 ** (-np.arange(16, dtype=np.float64) / 16)
    ar = np.float32(row[:, None].astype(np.float32) * inv.astype(np.float32)[None, :]).astype(np.float64)
    ac = np.float32(col[:, None].astype(np.float32) * inv.astype(np.float32)[None, :]).astype(np.float64)
    C = np.zeros((64, LAT)); S = np.zeros((64, LAT))
    C[0:16] = np.cos(ar).T; C[16:32] = np.cos(ar).T; C[32:48] = np.cos(ac).T; C[48:64] = np.cos(ac).T
    S[0:16] = -np.sin(ar).T; S[16:32] = np.sin(ar).T; S[32:48] = -np.sin(ac).T; S[48:64] = np.sin(ac).T
    C = np.concatenate([C, C], 0).astype(np.float32)
    S = np.concatenate([S, S], 0).astype(np.float32)
    return np.ascontiguousarray(C), np.ascontiguousarray(S)


def _swap_cols(w, nheads):
    perm = np.concatenate([np.arange(16, 32), np.arange(0, 16), np.arange(48, 64), np.arange(32, 48)])
    idx = np.concatenate([h * 64 + perm for h in range(nheads)])
    return w[..., idx]


def make_in_maps(inp, R=4):
    x = np.asarray(inp["x"], np.float32)
    B, LAT = x.shape[0], x.shape[1]
    LC = LAT // R
    C, S = _rope_tables(LAT)
    s_ = np.arange(128)
    U = (s_[:, None] <= s_[None, :]).astype(np.float32)
    L = (s_[:, None] >= s_[None, :]).astype(np.float32)
    tri = np.ascontiguousarray(np.concatenate([U, L], 1))
    bf = ml_dtypes.bfloat16
    msk = np.ascontiguousarray(np.concatenate([np.tile(U, (1, 4)), np.tile(L, (1, 4))], 1)).astype(bf)
    ident = np.eye(128, dtype=np.float32).astype(bf)
    f32 = lambda k: np.ascontiguousarray(np.asarray(inp[k], np.float32))
    wqkv = f32("b_w_qkv")
    wsw = np.ascontiguousarray(np.concatenate([_swap_cols(wqkv[..., 0:1024], 16), _swap_cols(wqkv[..., 1024:1280], 4)], -1))
    bg = f32("a_b_gate").reshape(2, 4, 8)[:, [0, 2, 1, 3], :].reshape(1, 64)
    conv = np.concatenate([_fm(f32("c_conv_w")[0, 0]), _fm(f32("c_conv_w")[0, 1]), _fm(f32("c_conv_w")[0, 2]), _fm(f32("c_conv_b")[0])], 1)
    common = dict(
        ada_w=f32("ada_w"), ada_bT=_fm(f32("ada_b")), norm_gT=_fm(f32("norm_g")), final_gT=_fm(f32("final_g")),
        mlp_w1=f32("mlp_w1"), mlp_w2=f32("mlp_w2"), a_w_in=f32("a_w_in"), a_w_gate=f32("a_w_gate"),
        a_b_gate_r=np.ascontiguousarray(np.tile(bg, (128, 1))),
        a_head_g_r=np.ascontiguousarray(np.tile(f32("a_head_g").reshape(1, -1), (128, 1))),
        a_w_out=f32("a_w_out"), b_w_qkv=wqkv, b_w_sw=wsw,
        b_sinks_r=np.ascontiguousarray(np.tile(f32("b_sinks").reshape(1, 16), (128, 1))),
        b_w_out=f32("b_w_out"), c_w_in=f32("c_w_in"), c_convT=np.ascontiguousarray(conv),
        c_w_out=f32("c_w_out"), ident=ident, tri=tri, msk=msk)
    maps = []
    for b in range(B):
        cv = np.ascontiguousarray(np.concatenate([_fm(np.asarray(inp["c"], np.float32)[b]), _fm(f32("c_ctx"))], 1))
        ctxT = np.asarray(inp["ctx"], np.float32)[b].T
        for r in range(R):
            m = dict(common)
            m["xT"] = np.ascontiguousarray(np.concatenate([ctxT, x[b, r * LC:(r + 1) * LC].T], 1))
            m["cvec"] = cv
            m["ropeC"] = np.ascontiguousarray(C[:, r * LC:(r + 1) * LC])
            m["ropeS"] = np.ascontiguousarray(S[:, r * LC:(r + 1) * LC])
            sel = np.zeros((128, 16), np.float32)
            if r > 0:
                sel[:, r - 1] = 1.0
            if r < R - 1:
                sel[:, 4 + r + 1] = 1.0
            sel[:, 8 + r] = 1.0
            sel[0:64, 12 + r] = 1.0
            sel[64:128, 12 + (R - 1 - r)] = 1.0
            m["sel"] = sel
            m["ada_w"] = np.ascontiguousarray(common["ada_w"][r:r + 1]) if R > 1 else common["ada_w"]
            m["ada_bT"] = _fm(f32("ada_b")[r]) if R > 1 else common["ada_bT"]
            vR = 1.0 if r < R - 1 else 0.0
            vL = 1.0 if r > 0 else 0.0
            m["mske"] = np.ascontiguousarray(np.concatenate([np.tile(U, (1, 4)) * vR, np.tile(L, (1, 4)) * vL], 1)).astype(bf)
            maps.append(m)
    return maps


_CACHE = {}
NR = 4


def kernel(**inputs):
    x = np.asarray(inputs["x"])
    B, LAT = int(x.shape[0]), int(x.shape[1])
    LC = LAT // NR
    if LC not in _CACHE:
        _CACHE[LC] = build_program(LC, R=NR)[0]
    nc = _CACHE[LC]
    maps = make_in_maps(inputs, NR)
    res = run_bass_kernel_spmd(nc, maps, core_ids=list(range(len(maps))))
    out = np.empty((B, LAT, D), np.float32)
    for b in range(B):
        for r in range(NR):
            out[b, r * LC:(r + 1) * LC] = res.results[b * NR + r]["outT"].T
    return out
```

```python
import numpy as np
import ml_dtypes
from contextlib import ExitStack
import concourse.bass as bass
import concourse.mybir as mybir
from concourse.bass_utils import run_bass_kernel_spmd

F32 = mybir.dt.float32
BF16 = mybir.dt.bfloat16
AF = mybir.ActivationFunctionType
ALU = mybir.AluOpType
AX = mybir.AxisListType

import os
DBG = os.environ.get('K_DBG', '')
D = 1024
KC = 8
CTX = 256
SEQ = 8192
EPS = 1e-6


class _Op:
    __slots__ = ("eng", "fn", "deps", "key", "tl", "seq", "clock", "signal", "waits", "idx")


class Prog:
    ENG = ("pe", "act", "dve", "pool", "sp")

    def __init__(self, nc):
        self.nc = nc
        self.ops = []
        self.lastw = {}
        self.readers = {}
        self.bar = None
        self.bar_idx = 0
        self.keymap = {}

    def add(self, eng, fn, reads=(), writes=(), dma=None, cc=False):
        if dma is not None:
            cls = "cc" if cc else ("sw" if eng == "pool" else "hw")
            km = self.keymap.setdefault(cls, {})
            dma = (cls, km.setdefault(dma, len(km)))
        op = _Op()
        op.eng = eng
        op.fn = fn
        op.key = dma
        op.signal = dma is not None
        op.idx = len(self.ops)
        deps = set()
        for t in reads:
            w = self.lastw.get(t)
            if w is not None:
                deps.add(w)
        for t in writes:
            w = self.lastw.get(t)
            if w is not None:
                deps.add(w)
            for r in self.readers.get(t, ()):
                deps.add(r)
        for t in writes:
            self.lastw[t] = op
            self.readers[t] = []
        for t in reads:
            self.readers.setdefault(t, []).append(op)
        if self.bar is not None:
            deps.add(self.bar)
        deps.discard(op)
        op.deps = deps
        self.ops.append(op)
        return op

    def barrier(self, dummy):
        last = {}
        for o in self.ops[self.bar_idx:]:
            last[("dma", o.key) if o.key is not None else o.eng] = o
        op = self.add("dve", lambda h: h.memset(dummy[:], 0.0), writes=["__bar"])
        op.deps |= set(last.values())
        op.deps.discard(op)
        self.bar = op
        self.bar_idx = len(self.ops)
        self.keymap = {}

    def _plan(self):
        def skip(d, op):
            return d.key is None and d.eng == "pe" and op.eng == "pe" and op.key is None

        for op in self.ops:
            for d in op.deps:
                if not skip(d, op):
                    d.signal = True
        seqs = {}
        for op in self.ops:
            op.tl = ("dma", op.key) if op.key is not None else op.eng
            if op.signal:
                seqs[op.tl] = seqs.get(op.tl, 0) + 1
            op.seq = seqs.get(op.tl, 0)
        self.final = dict(seqs)
        eng_clock = {e: {} for e in self.ENG}
        for op in self.ops:
            ck = eng_clock[op.eng]
            need = {}
            for d in op.deps:
                if skip(d, op):
                    continue
                if ck.get(d.tl, 0) >= d.seq:
                    continue
                if need.get(d.tl, 0) < d.seq:
                    need[d.tl] = d.seq
            for d in op.deps:
                if skip(d, op):
                    continue
                for tl, s in d.clock.items():
                    if ck.get(tl, 0) < s:
                        ck[tl] = s
            for tl, s in need.items():
                if ck.get(tl, 0) < s:
                    ck[tl] = s
            op.waits = list(need.items())
            c = dict(ck)
            if op.signal:
                c[op.tl] = op.seq
            op.clock = c
            op.deps = None

    def emit(self):
        self._plan()
        nc = self.nc
        with ExitStack() as st:
            sems = {}
            for i, tl in enumerate(self.final):
                sems[tl] = st.enter_context(nc.semaphore("sm%d" % i))
            block = st.enter_context(nc.Block())
            per = {e: [o for o in self.ops if o.eng == e] for e in self.ENG}
            final = self.final

            def val(tl, s):
                return s if (isinstance(tl, str) or tl[1][0] == "cc") else s * 16

            def run(e, h):
                for o in per[e]:
                    for tl, s in o.waits:
                        h.wait_ge(sems[tl], val(tl, s))
                    ins = o.fn(h)
                    if o.signal:
                        if o.key is not None and o.key[0] == "cc":
                            ins.then_inc(sems[o.tl])
                        else:
                            ins.then_inc(sems[o.tl], 16 if o.key is not None else 1)
                if e == "sp":
                    for tl, s in final.items():
                        h.wait_ge(sems[tl], val(tl, s))

            @block.tensor
            def _(h):
                run("pe", h)

            @block.scalar
            def _(h):
                run("act", h)

            @block.vector
            def _(h):
                run("dve", h)

            @block.gpsimd
            def _(h):
                run("pool", h)

            @block.sync
            def _(h):
                run("sp", h)
        return dict(n_ops=len(self.ops), n_sems=len(self.final))


def build_program(LAT=SEQ // 4, layers=(0, 1, 2, 3), final=True, stop_after_mixer=False, R=4):
    T = CTX + LAT
    RG = [list(range(g * R, (g + 1) * R)) for g in range(8 // R)] if R > 1 else None
    NCK = T // 128
    groups = [(0, CTX)] + [(CTX + i * 512, 512) for i in range(LAT // 512)]
    nc = bass.Bass("TRN2", target_bir_lowering=False)

    def din(name, shape, dt=F32):
        return nc.dram_tensor(name, list(shape), dt, kind="ExternalInput").ap()

    def dscr(name, shape, dt):
        return nc.dram_tensor(name, list(shape), dt, kind="Internal").ap()

    xT = din("xT", [D, T])
    cvec = din("cvec", [128, 16])
    ada_w = din("ada_w", [4 if R == 1 else 1, D, 6 * D])
    ada_bT = din("ada_bT", [128, (4 if R == 1 else 1) * 48])
    norm_gT = din("norm_gT", [128, 64])
    final_gT = din("final_gT", [128, 8])
    mlp_w1 = din("mlp_w1", [4, D, 4 * D])
    mlp_w2 = din("mlp_w2", [4, 4 * D, D])
    a_w_in = din("a_w_in", [2, D, 3 * D])
    a_w_gate = din("a_w_gate", [2, D, 32])
    a_b_gate_r = din("a_b_gate_r", [128, 64])
    a_head_g_r = din("a_head_g_r", [128, 2 * D])
    a_w_out = din("a_w_out", [2, D, D])
    b_w_qkv = din("b_w_qkv", [1, D, 1536])
    b_w_sw = din("b_w_sw", [1, D, 1280])
    b_sinks_r = din("b_sinks_r", [128, 16])
    b_w_out = din("b_w_out", [1, D, D])
    ropeC = din("ropeC", [128, LAT])
    ropeS = din("ropeS", [128, LAT])
    c_w_in = din("c_w_in", [1, D, 3 * D])
    c_convT = din("c_convT", [128, 32])
    c_w_out = din("c_w_out", [1, D, D])
    ident_in = din("ident", [128, 128], BF16)
    tri_in = din("tri", [128, 256])
    msk_in = din("msk", [128, 1024], BF16)
    mske_in = din("mske", [128, 1024], BF16)
    sel_in = din("sel", [128, 16])
    if final:
        outT = nc.dram_tensor("outT", [D, LAT], F32, kind="ExternalOutput").ap()
    else:
        outT = nc.dram_tensor("outT", [D, T], F32, kind="ExternalOutput").ap()

    hT = dscr("hT", [D, T], F32)
    xnS = dscr("xnS", [D, T], BF16)
    UW = 1 + CTX + 1 + 1 + LAT + 1
    uS = dscr("uS", [D, UW], F32)
    bgS = dscr("bgS", [D, T], F32)
    qS = dscr("qS", [D, T], BF16)
    kS = dscr("kS", [512, T], BF16)
    vS = dscr("vS", [T, 256], BF16)
    aqS = dscr("aqS", [64, 8 * T], BF16)
    opS = dscr("opS", [T, 8 * 258], F32)
    sgS = dscr("sgS", [T, D], BF16)
    eiS = dscr("eiS", [T, 16], F32)
    etS = dscr("etS", [NCK * 64, 16], F32)
    usS = dscr("usS", [NCK * 64, 8 * 258], F32)
    csS = dscr("csS", [2 * NCK * 64, 8 * 129], BF16)
    RR = max(R, 1)
    xcI = dscr("xcI", [2, D], F32)
    xcO = dscr("xcO", [2 * RR, D], F32)
    xkI = dscr("xkI", [1024, 128], BF16)
    xkO = dscr("xkO", [1024 * RR, 128], BF16)
    xvI = dscr("xvI", [256, 256], BF16)
    xvO = dscr("xvO", [256 * RR, 256], BF16)
    XW = 2 * 8 * 129 + 16
    xaI = dscr("xaI", [64, XW], F32)
    xaO = dscr("xaO", [64 * RR, XW], F32)
    xmI = dscr("xmI", [128, 96], F32)
    xmO = dscr("xmO", [128 * RR, 96], F32)

    hTv = hT.rearrange("(c p) t -> p c t", p=128)
    xTv = xT.rearrange("(c p) t -> p c t", p=128)
    xnSv = xnS.rearrange("(c p) t -> p c t", p=128)
    uSv = uS.rearrange("(c p) t -> p c t", p=128)
    bgSv = bgS.rearrange("(c p) t -> p c t", p=128)
    qSv = qS.rearrange("(c p) t -> p c t", p=128)
    kSv = kS.rearrange("(c p) t -> p c t", p=128)
    outTv = outT.rearrange("(c p) t -> p c t", p=128)

    _cnt = [0]

    def sbuf_u(name, shape, dt):
        _cnt[0] += 1
        return nc.sbuf_tensor("%s_%d" % (name, _cnt[0]), shape, dt)

    P = Prog(nc)
    with ExitStack() as gst:
        def sb(name, shape, dt):
            return gst.enter_context(sbuf_u(name, list(shape), dt))

        ps = [gst.enter_context(nc.psum_tensor("ps%d" % i, [128, 512], F32)) for i in range(7)]
        psb = gst.enter_context(nc.psum_tensor("psb", [128, 1024], BF16))
        dummy = sb("dummyt", [128, 8], F32)
        modt = sb("modt", [128, 4 * 96], F32)
        gst_ = sb("gst", [128, 4 * 32], F32)
        ngt = sb("ngt", [128, 64], F32)
        fgt = sb("fgt", [128, 8], F32)
        adab = sb("adab", [128, 192], F32)
        cs_t = sb("cs_t", [128, 16], F32)
        ones_bf = sb("ones_bf", [128, 128], BF16)
        ones_f = sb("ones_f", [128, 128], F32)
        epsb = sb("epsb", [128, 1], F32)
        oneb = sb("oneb", [128, 1], F32)
        ident = sb("ident_sb", [128, 128], BF16)
        tri = sb("tri_sb", [128, 256], F32)
        msk = sb("msk_sb", [128, 1024], BF16)
        mske = sb("mske_sb", [128, 1024], BF16)
        selt = sb("sel_sb", [128, 16], F32)

        def PS(i):
            return ("ps", i)

        P.add("dve", lambda h: h.memset(ones_bf[:], 1.0), writes=["ones_bf"])
        P.add("dve", lambda h: h.memset(ones_f[:], 1.0), writes=["ones_f"])
        P.add("dve", lambda h: h.memset(epsb[:], EPS), writes=["epsb"])
        P.add("dve", lambda h: h.memset(oneb[:], 1.0), writes=["oneb"])
        for nm, dst, src in (("ngt", ngt, norm_gT), ("fgt", fgt, final_gT), ("adab", adab[:, 0:(192 if R == 1 else 48)], ada_bT),
                             ("cs_t", cs_t, cvec), ("ident", ident, ident_in), ("tri", tri, tri_in),
                             ("msk", msk, msk_in), ("mske", mske, mske_in), ("sel", selt, sel_in)):
            P.add("sp", lambda h, dst=dst, src=src: h.dma_start(out=(dst if nm == "adab" else dst[:]), in_=src[:, :]),
                  writes=[nm], dma="c_" + nm)
        for gi, (s0, n) in enumerate(groups):
            P.add("sp", lambda h, s0=s0, n=n: h.dma_start(out=hT[:, s0:s0 + n], in_=xT[:, s0:s0 + n]),
                  writes=[("hT", gi)], dma="cp%d" % (gi % 4))

        def allgather(name, src, dst):
            P.add("pool", lambda h: h.collective_compute("AllGather", ALU.bypass, replica_groups=RG, ins=[src], outs=[dst]),
                  reads=[("xi", name)], writes=[("xo", name)], dma="cc_" + name, cc=True)

        with ExitStack() as st:
            awt = [st.enter_context(sbuf_u("awt%d" % i, [128, 8, 1024], F32)) for i in range(2)]
            silu = st.enter_context(sbuf_u("silu", [128, 16], F32))
            P.add("act", lambda h: h.activation(out=silu[:], in_=cs_t[:], func=AF.Silu),
                  reads=["cs_t"], writes=["silu"])
            it = 0
            if R > 1:
                modq = st.enter_context(sbuf_u("modq", [128, 96], F32))
                ada_layers, dst_of = [0], (lambda l, c: modq[:, c:c + 8])
            else:
                ada_layers, dst_of = list(layers), (lambda l, c: modt[:, l * 96 + c:l * 96 + c + 8])
            for l in ada_layers:
                for v in range(6):
                    b = it % 2
                    it += 1
                    src = ada_w[l].rearrange("(c p) n -> p c n", p=128)[:, :, v * 1024:(v + 1) * 1024]
                    P.add("sp", lambda h, b=b, src=src: h.dma_start(out=awt[b][:], in_=src),
                          writes=[("awt", b)], dma="awt%d" % b)
                    pa = b
                    for j in range(8):
                        for k in range(8):
                            P.add("pe", lambda h, b=b, j=j, k=k, pa=pa: h.matmul(
                                ps[pa][:, 2 * j:2 * j + 2], awt[b][:, k, j * 128:(j + 1) * 128],
                                silu[:, k:k + 9:8], start=(k == 0), stop=(k == 7)),
                                reads=[("awt", b), "silu"], writes=[PS(pa)])
                    for w in range(2):
                        dst = dst_of(l, (v * 2 + w) * 8)
                        P.add("dve", lambda h, dst=dst, w=w, l=l, v=v, pa=pa: h.tensor_tensor(
                            out=dst, in0=ps[pa][:, w:16:2],
                            in1=adab[:, l * 48 + v * 8:l * 48 + v * 8 + 8], op=ALU.add),
                            reads=[PS(pa), "adab"], writes=["modt"])
            if R > 1:
                P.add("sp", lambda h: h.dma_start(out=xmI[:, :], in_=modq[:]), reads=["modt"], writes=[("xi", "m")], dma="xmi")
                allgather("m", xmI, xmO)
                P.add("sp", lambda h: h.dma_start(out=modt[:, :].rearrange("p (l c) -> p l c", l=4),
                                                  in_=xmO.rearrange("(l p) c -> p l c", p=128)),
                      reads=[("xo", "m")], writes=["modt"], dma="xmo")
            for l in (range(4) if R > 1 else layers):
                for i, v in ((0, 1), (1, 4)):
                    for w in range(2):
                        col = l * 96 + (v * 2 + w) * 8
                        gcol = l * 32 + (i * 2 + w) * 8
                        P.add("dve", lambda h, col=col, gcol=gcol, l=l, i=i: h.scalar_tensor_tensor(
                            out=gst_[:, gcol:gcol + 8], in0=modt[:, col:col + 8], scalar=1.0,
                            in1=ngt[:, l * 16 + i * 8:l * 16 + i * 8 + 8], op0=ALU.add, op1=ALU.mult),
                            reads=["modt", "ngt"], writes=["gst"])
            P.barrier(dummy)

        def mcol(l, v, w):
            return l * 96 + (v * 2 + w) * 8

        def gcol(l, i, w):
            return l * 32 + (i * 2 + w) * 8

        def norm_mod(ht, htok, n, xn, xntok, gs0, sh0, W):
            sq, tmp, rstd, pi = W["sq"], W["tmp"], W["rstd"], W["psn"]
            P.add("act", lambda h: h.activation(out=sq[:, :, :n], in_=ht[:, :, :n], func=AF.Square),
                  reads=[htok], writes=["sq"])
            for c in range(KC):
                P.add("pe", lambda h, c=c: h.matmul(ps[pi][:, :n], ones_bf[:], sq[:, c, :n],
                                                   start=(c == 0), stop=(c == KC - 1)),
                      reads=["sq", "ones_bf"], writes=[PS(pi)])
            P.add("act", lambda h: h.activation(out=rstd[:, :n], in_=ps[pi][:, :n], func=AF.Sqrt,
                                                bias=epsb[:, 0:1], scale=1.0 / D),
                  reads=[PS(pi), "epsb"], writes=["rstd"])
            P.add("dve", lambda h: h.reciprocal(out=rstd[:, :n], in_=rstd[:, :n]),
                  reads=["rstd"], writes=["rstd"])
            for c in range(KC):
                P.add("dve", lambda h, c=c: h.scalar_tensor_tensor(
                    out=tmp[:, c, :n], in0=ht[:, c, :n], scalar=gst_[:, gs0 + c:gs0 + c + 1],
                    in1=rstd[:, :n], op0=ALU.mult, op1=ALU.mult),
                    reads=[htok, "gst", "rstd"], writes=[("tmp", c)])
                if sh0 is None:
                    continue
                P.add("act", lambda h, c=c: h.activation(
                    out=xn[:, c, :n], in_=tmp[:, c, :n], func=AF.Identity,
                    bias=modt[:, sh0 + c:sh0 + c + 1], scale=1.0),
                    reads=[("tmp", c), "modt"], writes=[xntok])

        def load_h(ht, tok, key, gi):
            s0, n = groups[gi]
            P.add("sp", lambda h: h.dma_start(out=ht[:, :, :n], in_=hTv[:, :, s0:s0 + n]),
                  reads=[("hT", gi)], writes=[tok], dma=key)

        def store_h(ht, tok, key, gi, q="sp"):
            s0, n = groups[gi]
            P.add(q, lambda h: h.dma_start(out=hTv[:, :, s0:s0 + n], in_=ht[:, :, :n]),
                  reads=[tok], writes=[("hT", gi)], dma=key)

        def load_w(wt, tok, key, src):
            P.add("pool", lambda h: h.dma_start(out=wt, in_=src), writes=[tok], dma=key)

        def resid(pi, ht, htok, oc, n, gate_col):
            P.add("dve", lambda h: h.scalar_tensor_tensor(
                out=ht[:, oc, :n], in0=ps[pi][:, :n], scalar=modt[:, gate_col + oc:gate_col + oc + 1],
                in1=ht[:, oc, :n], op0=ALU.mult, op1=ALU.add),
                reads=[PS(pi), htok, "modt"], writes=[htok])

        def mlp_phase(l):
            with ExitStack() as st:
                def sbt(name, shape, dt):
                    return st.enter_context(sbuf_u(name, list(shape), dt))
                w1s = [sbt("w1s%d" % i, [128, 8, 1024], BF16) for i in range(2)]
                w2s = [sbt("w2s%d" % i, [128, 8, 1024], BF16) for i in range(2)]
                hts = [sbt("mht%d" % i, [128, 8, 512], F32) for i in range(2)]
                xns = [sbt("mxn%d" % i, [128, 8, 512], BF16) for i in range(2)]
                hid = sbt("mhid", [128, 8, 512], BF16)
                rl = [sbt("mrl%d" % i, [128, 512], BF16) for i in range(2)]
                W = dict(sq=sbt("msq", [128, 8, 512], BF16), tmp=sbt("mtmp", [128, 8, 512], F32),
                         rstd=sbt("mrstd", [128, 512], F32), psn=6)
                gl = [gi for gi in range(len(groups)) if not (l == 3 and gi == 0)]
                w1v = mlp_w1[l].rearrange("(c p) n -> p c n", p=128)
                w2v = mlp_w2[l].rearrange("(c p) n -> p c n", p=128)

                def load_slab(s):
                    b = s % 2
                    load_w(w1s[b][:], ("w1s", b), "w1s%d" % b, w1v[:, :, s * 1024:(s + 1) * 1024])
                    load_w(w2s[b][:], ("w2s", b), "w2s%d" % b, w2v[:, s * 8:(s + 1) * 8, :])

                load_slab(0)
                it = 0
                for s in range(4):
                    b = s % 2
                    if s + 1 < 4:
                        load_slab(s + 1)
                    for gi in gl:
                        s0, n = groups[gi]
                        lw = 1 if gi == 0 else 0
                        ht, xn = hts[it % 2], xns[it % 2]
                        htok, xntok = ("mht", it % 2), ("mxn", it % 2)
                        load_h(ht, htok, "mldh%d" % (it % 2), gi)
                        if s == 0:
                            norm_mod(ht, htok, n, xn, xntok, gcol(l, 1, lw), mcol(l, 3, lw), W)
                            P.add("pool", lambda h, xn=xn, s0=s0, n=n: h.dma_start(
                                out=xnSv[:, :, s0:s0 + n], in_=xn[:, :, :n]),
                                reads=[xntok], writes=[("xnS", gi)], dma="mstx%d" % (it % 2))
                        else:
                            P.add("sp", lambda h, xn=xn, s0=s0, n=n: h.dma_start(
                                out=xn[:, :, :n], in_=xnSv[:, :, s0:s0 + n]),
                                reads=[("xnS", gi)], writes=[xntok], dma="mldx%d" % (it % 2))
                        for hc in range(8):
                            pi = hc % 3
                            for k in range(KC):
                                P.add("pe", lambda h, pi=pi, hc=hc, k=k, xn=xn, b=b, n=n: h.matmul(
                                    ps[pi][:, :n], w1s[b][:, k, hc * 128:(hc + 1) * 128], xn[:, k, :n],
                                    start=(k == 0), stop=(k == KC - 1)),
                                    reads=[("w1s", b), xntok], writes=[PS(pi)])
                            r = rl[hc % 2]
                            P.add("act", lambda h, pi=pi, r=r, n=n: h.activation(
                                out=r[:, :n], in_=ps[pi][:, :n], func=AF.Relu),
                                reads=[PS(pi)], writes=[("mrl", hc % 2)])
                            P.add("dve", lambda h, pi=pi, r=r, hc=hc, n=n: h.tensor_tensor(
                                out=hid[:, hc, :n], in0=ps[pi][:, :n], in1=r[:, :n], op=ALU.mult),
                                reads=[PS(pi), ("mrl", hc % 2)], writes=[("mhid", hc)])
                        for oc in range(8):
                            pi = 3 + oc % 3
                            for hc in range(8):
                                P.add("pe", lambda h, pi=pi, hc=hc, oc=oc, b=b, n=n: h.matmul(
                                    ps[pi][:, :n], w2s[b][:, hc, oc * 128:(oc + 1) * 128], hid[:, hc, :n],
                                    start=(hc == 0), stop=(hc == 7)),
                                    reads=[("w2s", b), ("mhid", hc)], writes=[PS(pi)])
                            resid(pi, ht, htok, oc, n, mcol(l, 5, lw))
                        store_h(ht, htok, "msth%d" % (it % 2), gi, "pool")
                        it += 1
                P.barrier(dummy)

        def conv_phase(l, j):
            ctx_out = l != 3
            with ExitStack() as st:
                def sbt(name, shape, dt):
                    return st.enter_context(sbuf_u(name, list(shape), dt))
                win = sbt("cwin", [128, 8, 3072], BF16)
                wout = sbt("cwout", [128, 8, 1024], BF16)
                cvt = sbt("cvt", [128, 32], F32)
                hts = [sbt("cht%d" % i, [128, 8, 512], F32) for i in range(2)]
                xn = sbt("cxn", [128, 8, 512], BF16)
                ut = sbt("cut", [128, 8, 514], F32)
                bgt = sbt("cbg", [128, 8, 512], F32)
                cgt = sbt("ccg", [128, 512], F32)
                zt = sbt("czt", [128, 8, 512], BF16)
                acc = sbt("cacc", [128, 512], F32)
                zero = sbt("czero", [128, 8], F32)
                W = dict(sq=sbt("csq", [128, 8, 512], BF16), tmp=sbt("ctmp", [128, 8, 512], F32),
                         rstd=sbt("crstd", [128, 512], F32), psn=6)
                load_w(win[:], "cwin", "cwin", c_w_in[j].rearrange("(c p) n -> p c n", p=128))
                load_w(wout[:], "cwout", "cwout", c_w_out[j].rearrange("(c p) n -> p c n", p=128))
                P.add("sp", lambda h: h.dma_start(out=cvt[:], in_=c_convT[:, :]), writes=["cvt"], dma="cvt")
                P.add("dve", lambda h: h.memset(zero[:], 0.0), writes=["czero"])
                gl = [gi for gi in range(len(groups)) if ctx_out or gi > 0]

                def ucol(gi):
                    s0, n = groups[gi]
                    return (1 + s0) if gi == 0 else (3 + s0)
                pads = (0, 1 + CTX, 2 + CTX, UW - 1) if R == 1 else (0, 1 + CTX)
                for ci, col in enumerate(pads):
                    P.add("sp", lambda h, col=col: h.dma_start(out=uSv[:, :, col:col + 1], in_=zero[:, :].rearrange("p (c o) -> p c o", o=1), allow_slow_non_contiguous=True),
                          reads=["czero"], writes=[("uSpad", ci)], dma="cpad%d" % ci)
                edge = [sbt("cedge%d" % i, [128, 8], F32) for i in range(2)]
                xg = sbt("cxg", [128, 2 * RR, 8], F32)
                halo = [sbt("chalo%d" % i, [128, 8], F32) for i in range(2)]
                for it, gi in enumerate(gl):
                    s0, n = groups[gi]
                    lw = 1 if gi == 0 else 0
                    ht, htok = hts[it % 2], ("cht", it % 2)
                    load_h(ht, htok, "cldh%d" % (it % 2), gi)
                    norm_mod(ht, htok, n, xn, "cxn", gcol(l, 0, lw), mcol(l, 0, lw), W)
                    for c in range(8):
                        for which, off, pi in (("bg", 0, 0), ("cg", 1024, 1), ("xt", 2048, 2)):
                            for k in range(KC):
                                P.add("pe", lambda h, pi=pi, off=off, c=c, k=k, n=n: h.matmul(
                                    ps[pi][:, :n], win[:, k, off + c * 128:off + (c + 1) * 128], xn[:, k, :n],
                                    start=(k == 0), stop=(k == KC - 1)),
                                    reads=["cwin", "cxn"], writes=[PS(pi)])
                        P.add("act", lambda h, c=c, n=n: h.activation(out=bgt[:, c, :n], in_=ps[0][:, :n], func=AF.Copy),
                              reads=[PS(0)], writes=["cbg"])
                        P.add("act", lambda h, n=n: h.activation(out=cgt[:, :n], in_=ps[1][:, :n], func=AF.Copy),
                              reads=[PS(1)], writes=["ccg"])
                        P.add("dve", lambda h, c=c, n=n: h.tensor_tensor(out=ut[:, c, :n], in0=ps[2][:, :n], in1=cgt[:, :n], op=ALU.mult),
                              reads=[PS(2), "ccg"], writes=["cut"])
                    uc = ucol(gi)
                    if R > 1 and gi == 1:
                        P.add("dve", lambda h: h.tensor_copy(out=edge[0][:], in_=ut[:, :, 0]), reads=["cut"], writes=[("cedge", 0)])
                    if R > 1 and gi == len(groups) - 1:
                        P.add("dve", lambda h, n=n: h.tensor_copy(out=edge[1][:], in_=ut[:, :, n - 1]), reads=["cut"], writes=[("cedge", 1)])
                    P.add("pool", lambda h, uc=uc, n=n: h.dma_start(out=uSv[:, :, uc:uc + n], in_=ut[:, :, :n]),
                          reads=["cut"], writes=[("uS", gi)], dma="cstu")
                    P.add("pool", lambda h, s0=s0, n=n: h.dma_start(out=bgSv[:, :, s0:s0 + n], in_=bgt[:, :, :n]),
                          reads=["cbg"], writes=[("bgS", gi)], dma="cstb")
                if R > 1:
                    for i in range(2):
                        P.add("sp", lambda h, i=i: h.dma_start(out=xcI[i, :].rearrange("(p c) -> p c", c=8), in_=edge[i][:]),
                              reads=[("cedge", i)], writes=[("xi", "c")], dma="cxi%d" % i)
                    allgather("c", xcI, xcO)
                    P.add("sp", lambda h: h.dma_start(out=xg[:], in_=xcO.rearrange("r (p c) -> p r c", c=8)),
                          reads=[("xo", "c")], writes=["cxg"], dma="cxo")
                    for side, (selo, rowo) in enumerate(((0, 1), (4, 0))):
                        for i in range(R):
                            if i == 0:
                                P.add("dve", lambda h, side=side, selo=selo, rowo=rowo, i=i: h.tensor_scalar(
                                    out=halo[side][:], in0=xg[:, 2 * i + rowo, :], scalar1=selt[:, selo + i:selo + i + 1], scalar2=None, op0=ALU.mult),
                                    reads=["cxg", "sel"], writes=[("chalo", side)])
                            else:
                                P.add("dve", lambda h, side=side, selo=selo, rowo=rowo, i=i: h.scalar_tensor_tensor(
                                    out=halo[side][:], in0=xg[:, 2 * i + rowo, :], scalar=selt[:, selo + i:selo + i + 1], in1=halo[side][:],
                                    op0=ALU.mult, op1=ALU.add), reads=["cxg", "sel", ("chalo", side)], writes=[("chalo", side)])
                        col = (2 + CTX, UW - 1)[side]
                        P.add("sp", lambda h, side=side, col=col: h.dma_start(
                            out=uSv[:, :, col:col + 1], in_=halo[side][:, :].rearrange("p (c o) -> p c o", o=1), allow_slow_non_contiguous=True),
                            reads=[("chalo", side)], writes=[("uSpad", 2 + side)], dma="cpad%d" % (2 + side))
                for it, gi in enumerate(gl):
                    s0, n = groups[gi]
                    lw = 1 if gi == 0 else 0
                    ht, htok = hts[it % 2], ("cht", it % 2)
                    load_h(ht, htok, "cldh%d" % (it % 2), gi)
                    uc = ucol(gi)
                    rd = [("uS", g2) for g2 in gl] + [("uSpad", i) for i in range(4)]
                    P.add("sp", lambda h, uc=uc, n=n: h.dma_start(out=ut[:, :, :n + 2], in_=uSv[:, :, uc - 1:uc + n + 1]),
                          reads=rd, writes=["cut"], dma="cldu")
                    P.add("sp", lambda h, s0=s0, n=n: h.dma_start(out=bgt[:, :, :n], in_=bgSv[:, :, s0:s0 + n]),
                          reads=[("bgS", gi)], writes=["cbg"], dma="cldb")
                    for c in range(8):
                        P.add("dve", lambda h, c=c, n=n: h.tensor_scalar(
                            out=acc[:, :n], in0=ut[:, c, 1:n + 1], scalar1=cvt[:, 8 + c:9 + c], scalar2=cvt[:, 24 + c:25 + c],
                            op0=ALU.mult, op1=ALU.add), reads=["cut", "cvt"], writes=["cacc"])
                        P.add("dve", lambda h, c=c, n=n: h.scalar_tensor_tensor(
                            out=acc[:, :n], in0=ut[:, c, 0:n], scalar=cvt[:, c:c + 1], in1=acc[:, :n],
                            op0=ALU.mult, op1=ALU.add), reads=["cut", "cvt", "cacc"], writes=["cacc"])
                        P.add("dve", lambda h, c=c, n=n: h.scalar_tensor_tensor(
                            out=acc[:, :n], in0=ut[:, c, 2:n + 2], scalar=cvt[:, 16 + c:17 + c], in1=acc[:, :n],
                            op0=ALU.mult, op1=ALU.add), reads=["cut", "cvt", "cacc"], writes=["cacc"])
                        P.add("dve", lambda h, c=c, n=n: h.tensor_tensor(
                            out=zt[:, c, :n], in0=acc[:, :n], in1=bgt[:, c, :n], op=ALU.mult),
                            reads=["cacc", "cbg"], writes=["czt"])
                    for oc in range(8):
                        pi = oc % 3
                        for k in range(KC):
                            P.add("pe", lambda h, pi=pi, oc=oc, k=k, n=n: h.matmul(
                                ps[pi][:, :n], wout[:, k, oc * 128:(oc + 1) * 128], zt[:, k, :n],
                                start=(k == 0), stop=(k == KC - 1)), reads=["cwout", "czt"], writes=[PS(pi)])
                        resid(pi, ht, htok, oc, n, mcol(l, 2, lw))
                    store_h(ht, htok, "csth%d" % (it % 2), gi, "pool")
                P.barrier(dummy)

        def swa_phase(l, j):
            ctx_out = l != 3
            NB = LAT // 128
            with ExitStack() as st:
                def sbt(name, shape, dt):
                    return st.enter_context(sbuf_u(name, list(shape), dt))
                with ExitStack() as st1:
                    def sb1(name, shape, dt):
                        return st1.enter_context(sbuf_u(name, list(shape), dt))
                    wq = sb1("bwq", [128, 8, 1024], BF16)
                    wqs = sb1("bwqs", [128, 8, 1024], BF16)
                    wk = sb1("bwk", [128, 8, 512], BF16)
                    wks = sb1("bwks", [128, 8, 512], BF16)
                    wv = sb1("bwv", [128, 8, 256], BF16)
                    rc = sb1("brc", [128, 512], F32)
                    rs = sb1("brs", [128, 512], F32)
                    hts = [sb1("bht%d" % i, [128, 8, 512], F32) for i in range(2)]
                    xn = sb1("bxn", [128, 8, 512], BF16)
                    qt = sb1("bqt", [128, 8, 512], BF16)
                    kt = sb1("bkt", [128, 4, 512], BF16)
                    vt = sb1("bvt", [128, 4, 256], BF16)
                    t1 = sb1("bt1", [128, 512], F32)
                    t2 = sb1("bt2", [128, 512], F32)
                    W = dict(sq=sb1("bsq", [128, 8, 512], BF16), tmp=sb1("btmp", [128, 8, 512], F32),
                             rstd=sb1("brstd", [128, 512], F32), psn=6)
                    qv = b_w_qkv[j].rearrange("(c p) n -> p c n", p=128)
                    sv = b_w_sw[j].rearrange("(c p) n -> p c n", p=128)
                    load_w(wq[:], "bwq", "bwq", qv[:, :, 0:1024])
                    load_w(wqs[:], "bwqs", "bwqs", sv[:, :, 0:1024])
                    for g in range(4):
                        for half in range(2):
                            load_w(wk[:, :, g * 128 + half * 64:g * 128 + half * 64 + 64], "bwk", "bwk%d" % (g * 2 + half),
                                   qv[:, :, 1024 + g * 64:1024 + (g + 1) * 64])
                            load_w(wks[:, :, g * 128 + half * 64:g * 128 + half * 64 + 64], "bwks", "bwks%d" % (g * 2 + half),
                                   sv[:, :, 1024 + g * 64:1024 + (g + 1) * 64])
                    load_w(wv[:], "bwv", "bwv", qv[:, :, 1280:1536])
                    for it, gi in enumerate(range(len(groups))):
                        s0, n = groups[gi]
                        lw = 1 if gi == 0 else 0
                        ht, htok = hts[it % 2], ("bht", it % 2)
                        load_h(ht, htok, "bldh%d" % (it % 2), gi)
                        norm_mod(ht, htok, n, xn, "bxn", gcol(l, 0, lw), mcol(l, 0, lw), W)
                        rope = gi > 0
                        if rope:
                            lp = s0 - CTX
                            P.add("sp", lambda h, lp=lp: h.dma_start(out=rc[:], in_=ropeC[:, lp:lp + 512]), writes=["brc"], dma="brc")
                            P.add("sp", lambda h, lp=lp: h.dma_start(out=rs[:], in_=ropeS[:, lp:lp + 512]), writes=["brs"], dma="brs")
                        for (wa, wb, nch, dst, dtok, cw) in ((wq, wqs, 8, qt, "bqt", 1024), (wk, wks, 4, kt, "bkt", 512)):
                            watok = "bwq" if nch == 8 else "bwk"
                            wbtok = "bwqs" if nch == 8 else "bwks"
                            for c in range(nch):
                                for k in range(KC):
                                    P.add("pe", lambda h, wa=wa, c=c, k=k, n=n: h.matmul(
                                        ps[0][:, :n], wa[:, k, c * 128:(c + 1) * 128], xn[:, k, :n],
                                        start=(k == 0), stop=(k == KC - 1)), reads=[watok, "bxn"], writes=[PS(0)])
                                if rope:
                                    for k in range(KC):
                                        P.add("pe", lambda h, wb=wb, c=c, k=k, n=n: h.matmul(
                                            ps[1][:, :n], wb[:, k, c * 128:(c + 1) * 128], xn[:, k, :n],
                                            start=(k == 0), stop=(k == KC - 1)), reads=[wbtok, "bxn"], writes=[PS(1)])
                                    P.add("dve", lambda h, n=n: h.tensor_tensor(out=t1[:, :n], in0=ps[0][:, :n], in1=rc[:, :n], op=ALU.mult),
                                          reads=[PS(0), "brc"], writes=["bt1"])
                                    P.add("dve", lambda h, n=n: h.tensor_tensor(out=t2[:, :n], in0=ps[1][:, :n], in1=rs[:, :n], op=ALU.mult),
                                          reads=[PS(1), "brs"], writes=["bt2"])
                                    P.add("pool", lambda h, dst=dst, c=c, n=n: h.tensor_tensor(out=dst[:, c, :n], in0=t1[:, :n], in1=t2[:, :n], op=ALU.add),
                                          reads=["bt1", "bt2"], writes=[dtok])
                                else:
                                    P.add("act", lambda h, dst=dst, c=c, n=n: h.activation(out=dst[:, c, :n], in_=ps[0][:, :n], func=AF.Copy),
                                          reads=[PS(0)], writes=[dtok])
                        for tb in range(n // 128):
                            for k in range(KC):
                                P.add("pe", lambda h, tb=tb, k=k: h.matmul(
                                    ps[2][:, :256], xn[:, k, tb * 128:(tb + 1) * 128], wv[:, k, :],
                                    start=(k == 0), stop=(k == KC - 1)), reads=["bwv", "bxn"], writes=[PS(2)])
                            P.add("act", lambda h, tb=tb: h.activation(out=vt[:, tb, :], in_=ps[2][:, :256], func=AF.Copy),
                                  reads=[PS(2)], writes=["bvt"])
                        P.add("pool", lambda h, s0=s0, n=n: h.dma_start(out=qSv[:, :, s0:s0 + n], in_=qt[:, :, :n]),
                              reads=["bqt"], writes=[("qS", gi)], dma="bstq")
                        P.add("pool", lambda h, s0=s0, n=n: h.dma_start(out=kSv[:, :, s0:s0 + n], in_=kt[:, :, :n]),
                              reads=["bkt"], writes=["kS"], dma="bstk")
                        nb = n // 128
                        P.add("pool", lambda h, s0=s0, n=n, nb=nb: h.dma_start(
                            out=vS[s0:s0 + n, :].rearrange("(b p) d -> p b d", p=128), in_=vt[:, :nb, :]),
                            reads=["bvt"], writes=["vS"], dma="bstv")
                    P.barrier(dummy)
                kall = sbt("bkall", [128, 4, T + 256], BF16)
                vall = sbt("bvall", [128, NCK + 2, 256], BF16)
                wo = sbt("bwo", [64, 16, 1024], BF16)
                esk = sbt("besk", [128, 16], F32)
                qz = [sbt("bqz%d" % i, [128, 8, 512], BF16) for i in range(2)]
                hts = [sbt("b2ht%d" % i, [128, 8, 512], F32) for i in range(2)]
                pt = [sbt("bpt%d" % i, [128, 512], BF16) for i in range(5)]
                oT = sbt("boT", [64, 16, 512], BF16)
                rd = sbt("brd", [64, 512], F32)
                for i in range(2):
                    P.add("dve", lambda h, i=i: h.memset(qz[i][:], 0.0), writes=["bqg"])
                P.add("sp", lambda h: h.dma_start(out=kall[:, :, 0:T], in_=kSv[:, :, :]), reads=["kS"], writes=["bkall"], dma="bldk")
                P.add("sp", lambda h: h.dma_start(out=vall[:, 0:NCK, :], in_=vS.rearrange("(b p) d -> p b d", p=128)),
                      reads=["vS"], writes=["bvall"], dma="bldv")
                if R > 1:
                    kcand = sbt("bkcand", [128, 2 * R, 4, 128], BF16)
                    vcand = sbt("bvcand", [128, 2 * R, 256], BF16)
                    for f, c0 in enumerate((CTX, T - 128)):
                        P.add("sp", lambda h, f=f, c0=c0: h.dma_start(out=xkI[f * 512:(f + 1) * 512, :], in_=kS[:, c0:c0 + 128]),
                              reads=["kS"], writes=[("xi", "k")], dma="bxk%d" % f)
                        P.add("sp", lambda h, f=f, c0=c0: h.dma_start(out=xvI[f * 128:(f + 1) * 128, :], in_=vS[c0:c0 + 128, :]),
                              reads=["vS"], writes=[("xi", "v")], dma="bxv%d" % f)
                    allgather("k", xkI, xkO)
                    allgather("v", xvI, xvO)
                    for i in range(R):
                        for f in range(2):
                            P.add("sp", lambda h, i=i, f=f: h.dma_start(
                                out=kcand[:, 2 * i + f, :, :], in_=xkO[i * 1024 + f * 512:i * 1024 + (f + 1) * 512, :].rearrange("(g p) t -> p g t", p=128)),
                                reads=[("xo", "k")], writes=["bkcand"], dma="bck%d" % (2 * i + f))
                            P.add("sp", lambda h, i=i, f=f: h.dma_start(
                                out=vcand[:, 2 * i + f, :], in_=xvO[i * 256 + f * 128:i * 256 + (f + 1) * 128, :]),
                                reads=[("xo", "v")], writes=["bvcand"], dma="bcv%d" % (2 * i + f))
                    for side, (selo, f) in enumerate(((0, 1), (4, 0))):
                        kd = kall[:, :, T + side * 128:T + (side + 1) * 128]
                        vd = vall[:, NCK + side, :]
                        for i in range(R):
                            sc = selt[:, selo + i:selo + i + 1]
                            if i == 0:
                                P.add("dve", lambda h, kd=kd, sc=sc, i=i, f=f: h.tensor_scalar(
                                    out=kd, in0=kcand[:, 2 * i + f, :, :], scalar1=sc, scalar2=None, op0=ALU.mult),
                                    reads=["bkcand", "sel"], writes=["bkall"])
                                P.add("dve", lambda h, vd=vd, sc=sc, i=i, f=f: h.tensor_scalar(
                                    out=vd, in0=vcand[:, 2 * i + f, :], scalar1=sc, scalar2=None, op0=ALU.mult),
                                    reads=["bvcand", "sel"], writes=["bvall"])
                            else:
                                P.add("dve", lambda h, kd=kd, sc=sc, i=i, f=f: h.scalar_tensor_tensor(
                                    out=kd, in0=kcand[:, 2 * i + f, :, :], scalar=sc, in1=kd, op0=ALU.mult, op1=ALU.add),
                                    reads=["bkcand", "sel", "bkall"], writes=["bkall"])
                                P.add("dve", lambda h, vd=vd, sc=sc, i=i, f=f: h.scalar_tensor_tensor(
                                    out=vd, in0=vcand[:, 2 * i + f, :], scalar=sc, in1=vd, op0=ALU.mult, op1=ALU.add),
                                    reads=["bvcand", "sel", "bvall"], writes=["bvall"])
                load_w(wo[:], "bwo", "bwo", b_w_out[j].rearrange("(h d) n -> d h n", d=64))
                P.add("sp", lambda h: h.dma_start(out=esk[:], in_=b_sinks_r[:, :]), writes=["besk"], dma="besk")
                P.add("act", lambda h: h.activation(out=esk[:], in_=esk[:], func=AF.Exp), reads=["besk"], writes=["besk"])
                gl = [gi for gi in range(len(groups)) if ctx_out or gi > 0]
                if DBG == 'b1':
                    gl = []
                for it, gi in enumerate(gl):
                    s0, n = groups[gi]
                    lw = 1 if gi == 0 else 0
                    ht, htok = hts[it % 2], ("b2ht", it % 2)
                    load_h(ht, htok, "b2ldh%d" % (it % 2), gi)
                    for i in range(2):
                        P.add("sp", lambda h, s0=s0, n=n, i=i: h.dma_start(out=qz[i][i * 64:(i + 1) * 64, :, :n], in_=qSv[i * 64:(i + 1) * 64, :, s0:s0 + n]),
                              reads=[("qS", gi)], writes=["bqg"], dma="bldq%d" % i)
                    for qb in range(n // 128):
                        c0 = s0 + qb * 128
                        if gi == 0:
                            kbs = [(0, None), (128, None)]
                        else:
                            nbk = (c0 - CTX) // 128
                            kbs = []
                            if nbk > 0:
                                kbs.append((c0 - 128, "L"))
                            elif R > 1:
                                kbs.append((T, "EL"))
                            kbs.append((c0, None))
                            if nbk < NB - 1:
                                kbs.append((c0 + 128, "R"))
                            elif R > 1:
                                kbs.append((T + 128, "ER"))
                            kbs += [(0, None), (128, None)]
                        for g in range(4):
                            for bi, (kc0, mk) in enumerate(kbs):
                                for hh in range(4):
                                    hd = 4 * g + hh
                                    qc, half = hd // 2, hd % 2
                                    p0 = half * 64
                                    P.add("pe", lambda h, bi=bi, hh=hh, g=g, kc0=kc0, qc=qc, half=half, qb=qb: h.matmul(
                                        ps[bi][:, hh * 128:(hh + 1) * 128], kall[:, g, kc0:kc0 + 128],
                                        qz[half][:, qc, qb * 128:(qb + 1) * 128], start=True, stop=True),
                                        reads=["bkall", "bqg"], writes=[PS(bi)])
                                P.add("act", lambda h, bi=bi: h.activation(out=pt[bi][:], in_=ps[bi][:, :], func=AF.Exp, scale=0.125),
                                      reads=[PS(bi)], writes=[("bpt", bi)])
                                if mk is not None and DBG != 'b2':
                                    mo = 512 if mk in ("L", "EL") else 0
                                    mt = mske if mk in ("EL", "ER") else msk
                                    P.add("pool", lambda h, bi=bi, mo=mo, mt=mt: h.tensor_tensor(
                                        out=pt[bi][:], in0=pt[bi][:], in1=mt[:, mo:mo + 512], op=ALU.mult),
                                        reads=[("bpt", bi), "msk", "mske"], writes=[("bpt", bi)])
                            nk = len(kbs)
                            for bi, (kc0, mk) in enumerate(kbs):
                                P.add("pe", lambda h, bi=bi, kc0=kc0, g=g, nk=nk: h.matmul(
                                    ps[5][0:64, :], vall[:, kc0 // 128, g * 64:(g + 1) * 64], pt[bi][:],
                                    start=(bi == 0), stop=(bi == nk - 1)), reads=["bvall", ("bpt", bi)], writes=[PS(5)])
                            for bi, (kc0, mk) in enumerate(kbs):
                                P.add("pe", lambda h, bi=bi, nk=nk: h.matmul(
                                    ps[6][0:64, :], ones_bf[:, 0:64], pt[bi][:],
                                    start=(bi == 0), stop=(bi == nk - 1)), reads=["ones_bf", ("bpt", bi)], writes=[PS(6)])
                            P.add("dve", lambda h, g=g: h.tensor_tensor(
                                out=rd[:, :].rearrange("p (h w) -> p h w", h=4), in0=ps[6][0:64, :].rearrange("p (h w) -> p h w", h=4),
                                in1=esk[0:64, 4 * g:4 * g + 4].unsqueeze(2).to_broadcast([64, 4, 128]), op=ALU.add),
                                reads=[PS(6), "besk"], writes=["brd"])
                            P.add("dve", lambda h: h.reciprocal(out=rd[:], in_=rd[:]), reads=["brd"], writes=["brd"])
                            P.add("dve", lambda h, g=g, qb=qb: h.tensor_tensor(
                                out=oT[:, 4 * g:4 * g + 4, qb * 128:(qb + 1) * 128], in0=ps[5][0:64, :].rearrange("p (h w) -> p h w", h=4),
                                in1=rd[:, :].rearrange("p (h w) -> p h w", h=4), op=ALU.mult),
                                reads=[PS(5), "brd"], writes=["boT"])
                    for oc in range(8):
                        pi = oc % 3
                        for hd in range(16):
                            P.add("pe", lambda h, pi=pi, oc=oc, hd=hd, n=n: h.matmul(
                                ps[pi][:, :n], wo[:, hd, oc * 128:(oc + 1) * 128], oT[:, hd, :n],
                                start=(hd == 0), stop=(hd == 15)), reads=["bwo", "boT"], writes=[PS(pi)])
                        resid(pi, ht, htok, oc, n, mcol(l, 2, lw))
                    store_h(ht, htok, "b2sth%d" % (it % 2), gi, "pool")
                P.barrier(dummy)

        def mlstm_phase(l, j):
            ctx_out = l != 3
            with ExitStack() as st:
                def sbt(name, shape, dt):
                    return st.enter_context(sbuf_u(name, list(shape), dt))
                with ExitStack() as st1:
                    def sb1(name, shape, dt):
                        return st1.enter_context(sbuf_u(name, list(shape), dt))
                    win = sb1("awin", [128, 8, 3072], BF16)
                    wg = sb1("awg", [128, 8, 32], BF16)
                    bgr = sb1("abgr", [128, 32], F32)
                    hts = [sb1("aht%d" % i, [128, 8, 512], F32) for i in range(2)]
                    xn = sb1("axn", [128, 8, 512], BF16)
                    qT = sb1("aqT", [64, 8, 512], BF16)
                    kT = sb1("akT", [64, 8, 512], BF16)
                    ktok = sb1("aktok", [128, 512], BF16)
                    sgo = sb1("asgo", [128, 1024], BF16)
                    gt = sb1("agt", [128, 32], F32)
                    spt = sb1("aspt", [128, 16], F32)
                    At = sb1("aAt", [128, 16], F32)
                    eit = sb1("aeit", [128, 16], F32)
                    ett = sb1("aett", [128, 16], F32)
                    VA = sb1("aVA", [128, 16, 130], BF16)
                    Ssb2 = [sb1("aSsb%d" % i, [128, 128], BF16) for i in range(2)]
                    Sm2 = [[sb1("aSm%d_%d" % (i, d_), [128, 128], BF16) for d_ in range(2)] for i in range(2)]
                    part = sb1("apart", [128, 8, 258], F32)
                    Ut = sb1("aUt", [64, 8, 258], F32)
                    W = dict(sq=sb1("asq", [128, 8, 512], BF16), tmp=sb1("atmp", [128, 8, 512], F32),
                             rstd=sb1("arstd", [128, 512], F32), psn=6)
                    wv_ = a_w_in[j].rearrange("(c p) n -> p c n", p=128)
                    gv_ = a_w_gate[j].rearrange("(c p) n -> p c n", p=128)
                    load_w(win[:], "awin", "awin", wv_)
                    for di, so in enumerate((0, 16, 8, 24)):
                        load_w(wg[:, :, di * 8:(di + 1) * 8], "awg", "awg%d" % di, gv_[:, :, so:so + 8])
                    P.add("sp", lambda h: h.dma_start(out=bgr[:], in_=a_b_gate_r[:, j * 32:(j + 1) * 32]), writes=["abgr"], dma="abgr")
                    for it, gi in enumerate(range(len(groups))):
                        s0, n = groups[gi]
                        lw = 1 if gi == 0 else 0
                        ht, htok = hts[it % 2], ("aht", it % 2)
                        load_h(ht, htok, "aldh%d" % (it % 2), gi)
                        norm_mod(ht, htok, n, xn, "axn", gcol(l, 0, lw), mcol(l, 0, lw), W)
                        for hd in range(8):
                            for qi, (off, dst, dtok, sc) in enumerate(((0, qT, "aqT", 0.125), (512, kT, "akT", 1.0))):
                                pq = (0, 5)[qi]
                                for k in range(KC):
                                    P.add("pe", lambda h, off=off, hd=hd, k=k, n=n, pq=pq: h.matmul(
                                        ps[pq][0:64, :n], win[:, k, off + hd * 64:off + (hd + 1) * 64], xn[:, k, :n],
                                        start=(k == 0), stop=(k == KC - 1)), reads=["awin", "axn"], writes=[PS(pq)])
                                P.add("act", lambda h, dst=dst, hd=hd, sc=sc, n=n, pq=pq: h.activation(
                                    out=dst[:, hd, :n], in_=ps[pq][0:64, :n], func=AF.Copy, scale=sc),
                                    reads=[PS(pq)], writes=[dtok])
                        P.add("pool", lambda h, s0=s0, n=n: h.dma_start(
                            out=aqS.rearrange("p (h t) -> p h t", h=8)[:, :, s0:s0 + n], in_=qT[:, :, :n]),
                            reads=["aqT"], writes=[("aqS", gi)], dma="astq")
                        for tb in range(n // 128):
                            ck = (s0 + tb * 128) // 128
                            need_out = ctx_out or gi > 0
                            tsl = slice(tb * 128, (tb + 1) * 128)
                            def tokproj(pi, c0, ncol, wt=win, wtok="awin", tsl=tsl):
                                for k in range(KC):
                                    P.add("pe", lambda h, k=k: h.matmul(
                                        ps[pi][:, :ncol], xn[:, k, tsl], wt[:, k, c0:c0 + ncol],
                                        start=(k == 0), stop=(k == KC - 1)), reads=[wtok, "axn"], writes=[PS(pi)])
                            tokproj(1, 512, 512)
                            P.add("act", lambda h: h.activation(out=ktok[:], in_=ps[1][:, :], func=AF.Copy),
                                  reads=[PS(1)], writes=["aktok"])
                            tokproj(2, 1024, 512)
                            tokproj(3, 1536, 512)
                            if need_out:
                                for hf in range(2):
                                    po = (4, 0)[hf]
                                    tokproj(po, 2048 + hf * 512, 512)
                                    P.add("act", lambda h, hf=hf, po=po: h.activation(out=sgo[:, hf * 512:(hf + 1) * 512], in_=ps[po][:, :], func=AF.Sigmoid),
                                          reads=[PS(po)], writes=["asgo"])
                                P.add("pool", lambda h, ck=ck: h.dma_start(out=sgS[ck * 128:(ck + 1) * 128, :], in_=sgo[:]),
                                      reads=["asgo"], writes=[("sgS", ck)], dma="astsg")
                            tokproj(5, 0, 32, wg, "awg")
                            P.add("dve", lambda h: h.tensor_tensor(out=gt[:], in0=ps[5][:, 0:32], in1=bgr[:], op=ALU.add),
                                  reads=[PS(5), "abgr"], writes=["agt"])
                            P.add("act", lambda h: h.activation(out=spt[:], in_=gt[:, 16:32], func=AF.Exp, scale=-1.0),
                                  reads=["agt"], writes=["aspt"])
                            P.add("act", lambda h: h.activation(out=spt[:], in_=spt[:], func=AF.Ln, bias=oneb[:, 0:1], scale=1.0),
                                  reads=["aspt", "oneb"], writes=["aspt"])
                            P.add("pe", lambda h: h.matmul(ps[6][:, 0:8], tri[:, 0:128], spt[:, 0:8], start=True, stop=True),
                                  reads=["tri", "aspt"], writes=[PS(6)])
                            P.add("pe", lambda h: h.matmul(ps[6][:, 8:16], tri[:, 128:256], spt[:, 8:16], start=True, stop=True),
                                  reads=["tri", "aspt"], writes=[PS(6)])
                            P.add("pe", lambda h: h.matmul(ps[6][:, 16:32], ones_f[:], spt[:, 0:16], start=True, stop=True),
                                  reads=["ones_f", "aspt"], writes=[PS(6)])
                            P.add("dve", lambda h: h.tensor_tensor(out=At[:], in0=ps[6][:, 0:16], in1=gt[:, 0:16], op=ALU.add),
                                  reads=[PS(6), "agt"], writes=["aAt"])
                            P.add("act", lambda h: h.activation(out=At[:], in_=At[:], func=AF.Exp), reads=["aAt"], writes=["aAt"])
                            P.add("act", lambda h: h.activation(out=eit[:], in_=ps[6][:, 0:16], func=AF.Exp), reads=[PS(6)], writes=["aeit"])
                            P.add("act", lambda h: h.activation(out=ett[:], in_=ps[6][:, 16:32], func=AF.Exp, scale=-1.0), reads=[PS(6)], writes=["aett"])
                            P.add("pool", lambda h, ck=ck: h.dma_start(out=eiS[ck * 128:(ck + 1) * 128, :], in_=eit[:]),
                                  reads=["aeit"], writes=[("eiS", ck)], dma="astei")
                            P.add("pool", lambda h, ck=ck: h.dma_start(out=etS[ck * 64:(ck + 1) * 64, :], in_=ett[0:64, :]),
                                  reads=["aett"], writes=[("etS", ck)], dma="astet")
                            for d in range(2):
                                for bk in range(2):
                                    i4 = d * 8 + bk * 4
                                    P.add("dve", lambda h, i4=i4, bk=bk: h.tensor_tensor(
                                        out=VA[:, i4:i4 + 4, 0:128], in0=ps[2 + bk][:, :].rearrange("p (h w) -> p h w", h=4),
                                        in1=At[:, i4:i4 + 4].unsqueeze(2).to_broadcast([128, 4, 128]), op=ALU.mult),
                                        reads=[PS(2 + bk), "aAt"], writes=[("aVA", i4 + q_) for q_ in range(4)])
                            P.add("dve", lambda h: h.tensor_copy(out=VA[:, :, 128], in_=At[:]),
                                  reads=["aAt"], writes=[("aVA", i) for i in range(16)])
                            def stage_S(hd, tsl=tsl):
                                p = hd % 2
                                pS = (0, 5)[p]
                                P.add("pe", lambda h: h.matmul(ps[pS][:, 0:128], kT[:, hd, tsl], qT[:, hd, tsl], start=True, stop=True),
                                      reads=["akT", "aqT"], writes=[PS(pS)])
                                P.add("act", lambda h: h.activation(out=Ssb2[p][:], in_=ps[pS][:, 0:128], func=AF.Copy),
                                      reads=[PS(pS)], writes=[("aSsb", p)])
                                P.add("dve", lambda h: h.tensor_tensor(out=Sm2[p][0][:], in0=Ssb2[p][:], in1=msk[:, 0:128], op=ALU.mult),
                                      reads=[("aSsb", p), "msk"], writes=[("aSm", p, 0)])
                                P.add("pool", lambda h: h.tensor_tensor(out=Sm2[p][1][:], in0=Ssb2[p][:], in1=msk[:, 512:640], op=ALU.mult),
                                      reads=[("aSsb", p), "msk"], writes=[("aSm", p, 1)])

                            def stage_O(hd):
                                p = hd % 2
                                pO = (1, 6)[p]
                                for d in range(2):
                                    P.add("pe", lambda h, d=d: h.matmul(
                                        ps[pO][:, d * 129:(d + 1) * 129], Sm2[p][d][:], VA[:, d * 8 + hd, 0:129], start=True, stop=True),
                                        reads=[("aSm", p, d), ("aVA", d * 8 + hd)], writes=[PS(pO)])
                                P.add("act", lambda h: h.activation(out=part[:, hd, :], in_=ps[pO][:, 0:258], func=AF.Copy),
                                      reads=[PS(pO)], writes=["apart"])

                            def stage_U(hd):
                                for d in range(2):
                                    P.add("pe", lambda h, d=d: h.matmul(
                                        ps[4][0:64, d * 129:(d + 1) * 129], ktok[:, hd * 64:(hd + 1) * 64], VA[:, d * 8 + hd, 0:129],
                                        start=True, stop=True), reads=["aktok", ("aVA", d * 8 + hd)], writes=[PS(4)])
                                P.add("dve", lambda h: h.tensor_copy(out=Ut[:, hd, :], in_=ps[4][0:64, 0:258]),
                                      reads=[PS(4)], writes=["aUt"])

                            if need_out:
                                stage_S(0)
                            for hd in range(8):
                                if need_out and hd + 1 < 8:
                                    stage_S(hd + 1)
                                stage_U(hd)
                                if need_out:
                                    stage_O(hd)
                            if need_out:
                                P.add("pool", lambda h, ck=ck: h.dma_start(out=opS[ck * 128:(ck + 1) * 128, :], in_=part[:]),
                                      reads=["apart"], writes=[("opS", ck)], dma="astop")
                            P.add("pool", lambda h, ck=ck: h.dma_start(out=usS[ck * 64:(ck + 1) * 64, :], in_=Ut[:]),
                                  reads=["aUt"], writes=[("usS", ck)], dma="astus")
                    P.barrier(dummy)
                if DBG == 'a1':
                    return
                with ExitStack() as st2:
                    def sb2(name, shape, dt):
                        return st2.enter_context(sbuf_u(name, list(shape), dt))
                    nctx = CTX // 128
                    NL = NCK - nctx
                    UBc = sb2("aUBc", [128, nctx, 8, 129], F32)
                    EBc = sb2("aEBc", [128, nctx, 8], F32)
                    UBl = sb2("aUBl", [128, NL, 8, 129], F32)
                    EBl = sb2("aEBl", [128, NL, 8], F32)
                    stt = sb2("astt", [128, 8, 129], F32)
                    stb = [sb2("astb%d" % i, [128, 8, 129], BF16) for i in range(2)]
                    usv = usS.rearrange("(c p) (h w) -> c p h w", p=64, w=258)
                    etv = etS.rearrange("(c p) e -> c p e", p=64)

                    def chunk_of(d, kind, si):
                        if kind == "ctx":
                            return si if d == 0 else nctx - 1 - si
                        return nctx + si if d == 0 else NCK - 1 - si

                    nld = 0
                    for kind, UB, EB, nn in (("ctx", UBc, EBc, nctx), ("lat", UBl, EBl, NL)):
                        for si in range(nn):
                            for d in range(2):
                                ck = chunk_of(d, kind, si)
                                P.add("sp", lambda h, UB=UB, si=si, d=d, ck=ck: h.dma_start(
                                    out=UB[d * 64:(d + 1) * 64, si, :, :], in_=usv[ck, :, :, d * 129:(d + 1) * 129]),
                                    reads=[("usS", ck)], writes=[("aUB", kind, si), ("alduk", nld % 8)], dma="aldu%d" % (nld % 8))
                                P.add("sp", lambda h, EB=EB, si=si, d=d, ck=ck: h.dma_start(
                                    out=EB[d * 64:(d + 1) * 64, si, :], in_=etv[ck, :, d * 8:(d + 1) * 8]),
                                    reads=[("etS", ck)], writes=[("aEB", kind, si), ("aldek", nld % 8)], dma="alde%d" % (nld % 8))
                                nld += 1
                    P.add("dve", lambda h: h.memset(stt[:], 0.0), writes=["astt"])
                    cnt = [0]

                    def step(kind, UB, EB, si, write_cs, pp=None):
                        if write_cs:
                            b = cnt[0] % 2
                            cnt[0] += 1
                            P.add("dve", lambda h: h.tensor_copy(out=stb[b][:], in_=stt[:]), reads=["astt"], writes=[("astb", b)])
                            for d in range(2):
                                ck = chunk_of(d, kind, si)
                                P.add("sp", lambda h, d=d, ck=ck: h.dma_start(
                                    out=csS[(d * NCK + ck) * 64:(d * NCK + ck + 1) * 64, :], in_=stb[b][d * 64:(d + 1) * 64, :, :]),
                                    reads=[("astb", b)], writes=[("csS", d, ck)], dma="astcs%d%d" % (d, b))
                        P.add("dve", lambda h: h.tensor_tensor(out=stt[:], in0=stt[:], in1=UB[:, si, :, :], op=ALU.add),
                              reads=["astt", ("aUB", kind, si)], writes=["astt"])
                        P.add("dve", lambda h: h.tensor_tensor(
                            out=stt[:], in0=stt[:], in1=EB[:, si, :].unsqueeze(2).to_broadcast([128, 8, 129]), op=ALU.mult),
                            reads=["astt", ("aEB", kind, si)], writes=["astt"])
                        if pp is not None:
                            P.add("pool", lambda h: h.tensor_tensor(out=pp[:], in0=pp[:], in1=EB[:, si, :], op=ALU.mult),
                                  reads=["app", ("aEB", kind, si)], writes=["app"])

                    for si in range(nctx):
                        step("ctx", UBc, EBc, si, True)
                    if R == 1:
                        for si in range(NL):
                            step("lat", UBl, EBl, si, True)
                    else:
                        X = sb2("actx", [128, 8, 129], F32)
                        pp = sb2("app", [128, 8], F32)
                        cin = sb2("acin", [128, 8, 129], F32)
                        tmpc = sb2("atmpc", [128, 8, 129], F32)
                        GS = sb2("aGS", [128, R, 8, 129], F32)
                        GP = sb2("aGP", [128, R, 8], F32)
                        P.add("dve", lambda h: h.tensor_copy(out=X[:], in_=stt[:]), reads=["astt"], writes=["actx"])
                        P.add("dve", lambda h: h.memset(stt[:], 0.0), writes=["astt"])
                        P.add("pool", lambda h: h.memset(pp[:], 1.0), writes=["app"])
                        for si in range(NL):
                            step("lat", UBl, EBl, si, False, pp)
                        for d in range(2):
                            P.add("sp", lambda h, d=d: h.dma_start(out=xaI[:, d * 1032:(d + 1) * 1032], in_=stt[d * 64:(d + 1) * 64, :, :]),
                                  reads=["astt"], writes=[("xi", "a")], dma="axs%d" % d)
                            P.add("sp", lambda h, d=d: h.dma_start(out=xaI[:, 2064 + d * 8:2064 + (d + 1) * 8], in_=pp[d * 64:(d + 1) * 64, :]),
                                  reads=["app"], writes=[("xi", "a")], dma="axp%d" % d)
                        allgather("a", xaI, xaO)
                        xav = xaO.rearrange("(i p) w -> i p w", p=64)
                        for n_ in range(R):
                            for d in range(2):
                                i = n_ if d == 0 else R - 1 - n_
                                P.add("sp", lambda h, n_=n_, d=d, i=i: h.dma_start(
                                    out=GS[d * 64:(d + 1) * 64, n_, :, :], in_=xav[i, :, d * 1032:(d + 1) * 1032]),
                                    reads=[("xo", "a")], writes=["aGS"], dma="axg%d" % (2 * n_ + d))
                                P.add("sp", lambda h, n_=n_, d=d, i=i: h.dma_start(
                                    out=GP[d * 64:(d + 1) * 64, n_, :], in_=xav[i, :, 2064 + d * 8:2064 + (d + 1) * 8]),
                                    reads=[("xo", "a")], writes=["aGS"], dma="axh%d" % (2 * n_ + d))
                        for n_ in range(R):
                            oh = selt[:, 12 + n_:13 + n_]
                            if n_ == 0:
                                P.add("dve", lambda h, oh=oh: h.tensor_scalar(out=cin[:], in0=X[:], scalar1=oh, scalar2=None, op0=ALU.mult),
                                      reads=["actx", "sel"], writes=["acin"])
                            else:
                                P.add("dve", lambda h, oh=oh: h.tensor_scalar(out=tmpc[:], in0=X[:], scalar1=oh, scalar2=None, op0=ALU.mult),
                                      reads=["actx", "sel"], writes=["atmpc"])
                                P.add("dve", lambda h: h.tensor_tensor(out=cin[:], in0=cin[:], in1=tmpc[:], op=ALU.add),
                                      reads=["atmpc", "acin"], writes=["acin"])
                            if n_ == R - 1:
                                break
                            P.add("dve", lambda h, n_=n_: h.tensor_tensor(
                                out=X[:], in0=X[:], in1=GP[:, n_, :].unsqueeze(2).to_broadcast([128, 8, 129]), op=ALU.mult),
                                reads=["actx", "aGS"], writes=["actx"])
                            P.add("dve", lambda h, n_=n_: h.tensor_tensor(out=X[:], in0=X[:], in1=GS[:, n_, :, :], op=ALU.add),
                                  reads=["actx", "aGS"], writes=["actx"])
                        P.add("dve", lambda h: h.tensor_copy(out=stt[:], in_=cin[:]), reads=["acin"], writes=["astt"])
                        for si in range(NL):
                            step("lat", UBl, EBl, si, True)
                    P.barrier(dummy)
                if DBG == 'a2':
                    return
                wout = sbt("a2wout", [128, 8, 1024], BF16)
                hgr = sbt("a2hgr", [128, 1024], F32)
                hts = [sbt("a2ht%d" % i, [128, 8, 512], F32) for i in range(2)]
                part2 = [sbt("a2part%d" % i, [128, 8, 258], F32) for i in range(2)]
                qc = [sbt("a2qc%d" % i, [64, 8, 128], BF16) for i in range(2)]
                cst = [sbt("a2cs%d" % i, [64, 2, 8 * 129], BF16) for i in range(2)]
                eic = [sbt("a2ei%d" % i, [128, 16], F32) for i in range(2)]
                sgc = [sbt("a2sg%d" % i, [128, 1024], BF16) for i in range(2)]
                tot = sbt("a2tot", [128, 8, 258], F32)
                rr = sbt("a2rr", [128, 16], F32)
                hs = sbt("a2hs", [128, 8, 128], F32)
                sq2 = sbt("a2sq", [128, 8, 128], F32)
                ss = sbt("a2ss", [128, 8], F32)
                hgo = sbt("a2hgo", [128, 1024], F32)
                hg = sbt("a2hg", [128, 1024], BF16)
                hgT = sbt("a2hgT", [128, 8, 512], BF16)
                load_w(wout[:], "a2wout", "a2wout", a_w_out[j].rearrange("(c p) n -> p c n", p=128))
                P.add("sp", lambda h: h.dma_start(out=hgr[:], in_=a_head_g_r[:, j * D:(j + 1) * D]), writes=["a2hgr"], dma="a2hgr")
                gl = [gi for gi in range(len(groups)) if ctx_out or gi > 0]
                hs2 = [hs, sbt("a2hs1", [128, 8, 128], F32)]
                sqb = sbt("a2sqb", [128, 8, 128], F32)
                sq2b = [sq2, sbt("a2sq1", [128, 8, 128], F32)]
                hgob = [hgo, sbt("a2hgo1", [128, 1024], F32)]
                chunks = []
                for it, gi in enumerate(gl):
                    s0, n = groups[gi]
                    for tb in range(n // 128):
                        chunks.append((it, gi, tb, (s0 + tb * 128) // 128, tb == n // 128 - 1))

                def stage_F(idx):
                    it, gi, tb, ck, last = chunks[idx]
                    b = idx % 2
                    hsb, hstok = hs2[b], ("a2hs", b)
                    if tb == 0:
                        load_h(hts[it % 2], ("a2ht", it % 2), "a2ldh%d" % (it % 2), gi)
                    P.add("sp", lambda h: h.dma_start(out=part2[b][:], in_=opS[ck * 128:(ck + 1) * 128, :]),
                          reads=[("opS", ck)], writes=[("a2part", b)], dma="a2ldp%d" % b)
                    P.add("sp", lambda h: h.dma_start(
                        out=qc[b][:], in_=aqS.rearrange("p (h t) -> p h t", h=8)[:, :, ck * 128:(ck + 1) * 128]),
                        reads=[("aqS", gi)], writes=[("a2qc", b)], dma="a2ldq%d" % b)
                    for d in range(2):
                        P.add("sp", lambda h, d=d: h.dma_start(
                            out=cst[b][:, d, :], in_=csS[(d * NCK + ck) * 64:(d * NCK + ck + 1) * 64, :]),
                            reads=[("csS", d, ck)], writes=[("a2cs", b)], dma="a2ldc%d%d" % (b, d))
                    P.add("sp", lambda h: h.dma_start(out=eic[b][:], in_=eiS[ck * 128:(ck + 1) * 128, :]),
                          reads=[("eiS", ck)], writes=[("a2ei", b)], dma="a2lde%d" % b)
                    P.add("sp", lambda h: h.dma_start(out=sgc[b][:], in_=sgS[ck * 128:(ck + 1) * 128, :]),
                          reads=[("sgS", ck)], writes=[("a2sg", b)], dma="a2lds%d" % b)
                    for hd in range(8):
                        pi = hd % 3
                        for d in range(2):
                            P.add("pe", lambda h, pi=pi, hd=hd, d=d: h.matmul(
                                ps[pi][:, d * 129:(d + 1) * 129], qc[b][:, hd, :], cst[b][:, d, hd * 129:(hd + 1) * 129],
                                start=True, stop=True), reads=[("a2qc", b), ("a2cs", b)], writes=[PS(pi)])
                        P.add("dve", lambda h, pi=pi, hd=hd: h.tensor_tensor(
                            out=tot[:, hd, :], in0=ps[pi][:, 0:258], in1=part2[b][:, hd, :], op=ALU.add),
                            reads=[PS(pi), ("a2part", b)], writes=["a2tot"])
                    for d in range(2):
                        P.add("dve", lambda h, d=d: h.scalar_tensor_tensor(
                            out=rr[:, d * 8:(d + 1) * 8], in0=tot[:, :, d * 129 + 128], scalar=-1.0,
                            in1=tot[:, :, d * 129 + 128], op0=ALU.mult, op1=ALU.max),
                            reads=["a2tot"], writes=["a2rr"])
                        P.add("dve", lambda h, d=d: h.tensor_tensor(
                            out=rr[:, d * 8:(d + 1) * 8], in0=rr[:, d * 8:(d + 1) * 8], in1=eic[b][:, d * 8:(d + 1) * 8], op=ALU.max),
                            reads=["a2rr", ("a2ei", b)], writes=["a2rr"])
                    P.add("dve", lambda h: h.reciprocal(out=rr[:], in_=rr[:]), reads=["a2rr"], writes=["a2rr"])
                    P.add("dve", lambda h: h.tensor_tensor(
                        out=hsb[:], in0=tot[:, :, 0:128], in1=rr[:, 0:8].unsqueeze(2).to_broadcast([128, 8, 128]), op=ALU.mult),
                        reads=["a2tot", "a2rr"], writes=[hstok])
                    P.add("dve", lambda h: h.tensor_tensor(
                        out=sqb[:], in0=tot[:, :, 129:257], in1=rr[:, 8:16].unsqueeze(2).to_broadcast([128, 8, 128]), op=ALU.mult),
                        reads=["a2tot", "a2rr"], writes=["a2sqb"])
                    P.add("dve", lambda h: h.tensor_tensor(out=hsb[:], in0=hsb[:], in1=sqb[:], op=ALU.add),
                          reads=[hstok, "a2sqb"], writes=[hstok])
                    P.add("act", lambda h: h.activation(out=sq2b[b][:], in_=hsb[:], func=AF.Square), reads=[hstok], writes=[("a2sq", b)])
                    P.add("pool", lambda h: h.tensor_tensor(out=hgob[b][:], in0=hgr[:], in1=sgc[b][:], op=ALU.mult),
                          reads=["a2hgr", ("a2sg", b)], writes=[("a2hgo", b)])

                def stage_B(idx):
                    it, gi, tb, ck, last = chunks[idx]
                    b = idx % 2
                    hsb, hstok = hs2[b], ("a2hs", b)
                    P.add("dve", lambda h: h.tensor_reduce(out=ss[:], in_=sq2b[b][:], axis=AX.X, op=ALU.add), reads=[("a2sq", b)], writes=["a2ss"])
                    P.add("act", lambda h: h.activation(out=ss[:], in_=ss[:], func=AF.Sqrt, bias=epsb[:, 0:1], scale=1.0 / 128),
                          reads=["a2ss", "epsb"], writes=["a2ss"])
                    P.add("dve", lambda h: h.reciprocal(out=ss[:], in_=ss[:]), reads=["a2ss"], writes=["a2ss"])
                    P.add("dve", lambda h: h.tensor_tensor(
                        out=hsb[:], in0=hsb[:], in1=ss[:, 0:8].unsqueeze(2).to_broadcast([128, 8, 128]), op=ALU.mult),
                        reads=[hstok, "a2ss"], writes=[hstok])
                    P.add("dve", lambda h: h.tensor_tensor(
                        out=hg[:, :].rearrange("p (h w) -> p h w", h=8), in0=hsb[:], in1=hgob[b][:, :].rearrange("p (h w) -> p h w", h=8), op=ALU.mult),
                        reads=[hstok, ("a2hgo", b)], writes=["a2hg"])
                    for c in range(8):
                        P.add("pe", lambda h, c=c: h.transpose(psb[:, c * 128:(c + 1) * 128], hg[:, c * 128:(c + 1) * 128], ident[:]),
                              reads=["a2hg", "ident"], writes=["psb"])
                    P.add("act", lambda h: h.activation(
                        out=hgT[:, :, tb * 128:(tb + 1) * 128], in_=psb[:, :].rearrange("p (c t) -> p c t", c=8), func=AF.Copy),
                        reads=["psb"], writes=["a2hgT"])
                    if last:
                        s0, n = groups[gi]
                        lw = 1 if gi == 0 else 0
                        ht, htok = hts[it % 2], ("a2ht", it % 2)
                        for oc in range(8):
                            pi = 3 + oc % 3
                            for k in range(KC):
                                P.add("pe", lambda h, pi=pi, oc=oc, k=k: h.matmul(
                                    ps[pi][:, :n], wout[:, k, oc * 128:(oc + 1) * 128], hgT[:, k, :n],
                                    start=(k == 0), stop=(k == KC - 1)), reads=["a2wout", "a2hgT"], writes=[PS(pi)])
                            resid(pi, ht, htok, oc, n, mcol(l, 2, lw))
                        store_h(ht, htok, "a2sth%d" % (it % 2), gi)

                stage_F(0)
                for idx in range(len(chunks)):
                    if idx + 1 < len(chunks):
                        stage_F(idx + 1)
                    stage_B(idx)
                P.barrier(dummy)

        for l in layers:
            kind, j = l % 3, l // 3
            if kind == 0:
                mlstm_phase(l, j)
            elif kind == 1:
                swa_phase(l, j)
            else:
                conv_phase(l, j)
            if not (stop_after_mixer and l == layers[-1]):
                mlp_phase(l)

        with ExitStack() as st:
            def sbt(name, shape, dt):
                return st.enter_context(sbuf_u(name, list(shape), dt))
            hts = [sbt("fht%d" % i, [128, 8, 512], F32) for i in range(2)]
            W = dict(sq=sbt("fsq", [128, 8, 512], BF16), tmp=sbt("ftmp", [128, 8, 512], F32),
                     rstd=sbt("frstd", [128, 512], F32), psn=6)
            if final:
                P.add("dve", lambda h: h.tensor_copy(out=gst_[:, 0:8], in_=fgt[:]), reads=["fgt", "gst"], writes=["gst"])
                for it, gi in enumerate(range(1, len(groups))):
                    s0, n = groups[gi]
                    ht, htok = hts[it % 2], ("fht", it % 2)
                    load_h(ht, htok, "fldh%d" % (it % 2), gi)
                    norm_mod(ht, htok, n, None, None, 0, None, W)
                    P.add("pool", lambda h, s0=s0, n=n: h.dma_start(out=outTv[:, :, s0 - CTX:s0 - CTX + n], in_=W["tmp"][:, :, :n]),
                          reads=[("tmp", c) for c in range(8)], writes=[("out", gi)], dma="fst")
            else:
                for it, gi in enumerate(range(len(groups))):
                    s0, n = groups[gi]
                    ht, htok = hts[it % 2], ("fht", it % 2)
                    load_h(ht, htok, "fldh%d" % (it % 2), gi)
                    P.add("sp", lambda h, s0=s0, n=n, ht=ht: h.dma_start(out=outTv[:, :, s0:s0 + n], in_=ht[:, :, :n]),
                          reads=[htok], writes=[("out", gi)], dma="fst%d" % (it % 2))
        info = P.emit()
    return nc, info


def _fm(v):
    v = np.asarray(v, np.float32)
    lead = v.shape[:-1]
    n = v.shape[-1] // 128
    a = v.reshape(lead + (n, 128))
    a = np.moveaxis(a, -1, 0)
    return np.ascontiguousarray(a.reshape(128, -1))


def _rope_tables(LAT):
    t = np.arange(LAT)
    row = (t // 64).astype(np.float64)
    col = (t % 64).astype(np.float64)
    inv = 10000.0 ** (-np.arange(16, dtype=np.float64) / 16)
    ar = np.float32(row[:, None].astype(np.float32) * inv.astype(np.float32)[None, :]).astype(np.float64)
    ac = np.float32(col[:, None].astype(np.float32) * inv.astype(np.float32)[None, :]).astype(np.float64)
    C = np.zeros((64, LAT)); S = np.zeros((64, LAT))
    C[0:16] = np.cos(ar).T; C[16:32] = np.cos(ar).T; C[32:48] = np.cos(ac).T; C[48:64] = np.cos(ac).T
    S[0:16] = -np.sin(ar).T; S[16:32] = np.sin(ar).T; S[32:48] = -np.sin(ac).T; S[48:64] = np.sin(ac).T
    C = np.concatenate([C, C], 0).astype(np.float32)
    S = np.concatenate([S, S], 0).astype(np.float32)
    return np.ascontiguousarray(C), np.ascontiguousarray(S)


def _swap_cols(w, nheads):
    perm = np.concatenate([np.arange(16, 32), np.arange(0, 16), np.arange(48, 64), np.arange(32, 48)])
    idx = np.concatenate([h * 64 + perm for h in range(nheads)])
    return w[..., idx]


def make_in_maps(inp, R=4):
    x = np.asarray(inp["x"], np.float32)
    B, LAT = x.shape[0], x.shape[1]
    LC = LAT // R
    C, S = _rope_tables(LAT)
    s_ = np.arange(128)
    U = (s_[:, None] <= s_[None, :]).astype(np.float32)
    L = (s_[:, None] >= s_[None, :]).astype(np.float32)
    tri = np.ascontiguousarray(np.concatenate([U, L], 1))
    bf = ml_dtypes.bfloat16
    msk = np.ascontiguousarray(np.concatenate([np.tile(U, (1, 4)), np.tile(L, (1, 4))], 1)).astype(bf)
    ident = np.eye(128, dtype=np.float32).astype(bf)
    f32 = lambda k: np.ascontiguousarray(np.asarray(inp[k], np.float32))
    wqkv = f32("b_w_qkv")
    wsw = np.ascontiguousarray(np.concatenate([_swap_cols(wqkv[..., 0:1024], 16), _swap_cols(wqkv[..., 1024:1280], 4)], -1))
    bg = f32("a_b_gate").reshape(2, 4, 8)[:, [0, 2, 1, 3], :].reshape(1, 64)
    conv = np.concatenate([_fm(f32("c_conv_w")[0, 0]), _fm(f32("c_conv_w")[0, 1]), _fm(f32("c_conv_w")[0, 2]), _fm(f32("c_conv_b")[0])], 1)
    common = dict(
        ada_w=f32("ada_w"), ada_bT=_fm(f32("ada_b")), norm_gT=_fm(f32("norm_g")), final_gT=_fm(f32("final_g")),
        mlp_w1=f32("mlp_w1"), mlp_w2=f32("mlp_w2"), a_w_in=f32("a_w_in"), a_w_gate=f32("a_w_gate"),
        a_b_gate_r=np.ascontiguousarray(np.tile(bg, (128, 1))),
        a_head_g_r=np.ascontiguousarray(np.tile(f32("a_head_g").reshape(1, -1), (128, 1))),
        a_w_out=f32("a_w_out"), b_w_qkv=wqkv, b_w_sw=wsw,
        b_sinks_r=np.ascontiguousarray(np.tile(f32("b_sinks").reshape(1, 16), (128, 1))),
        b_w_out=f32("b_w_out"), c_w_in=f32("c_w_in"), c_convT=np.ascontiguousarray(conv),
        c_w_out=f32("c_w_out"), ident=ident, tri=tri, msk=msk)
    maps = []
    for b in range(B):
        cv = np.ascontiguousarray(np.concatenate([_fm(np.asarray(inp["c"], np.float32)[b]), _fm(f32("c_ctx"))], 1))
        ctxT = np.asarray(inp["ctx"], np.float32)[b].T
        for r in range(R):
            m = dict(common)
            m["xT"] = np.ascontiguousarray(np.concatenate([ctxT, x[b, r * LC:(r + 1) * LC].T], 1))
            m["cvec"] = cv
            m["ropeC"] = np.ascontiguousarray(C[:, r * LC:(r + 1) * LC])
            m["ropeS"] = np.ascontiguousarray(S[:, r * LC:(r + 1) * LC])
            sel = np.zeros((128, 16), np.float32)
            if r > 0:
                sel[:, r - 1] = 1.0
            if r < R - 1:
                sel[:, 4 + r + 1] = 1.0
            sel[:, 8 + r] = 1.0
            sel[0:64, 12 + r] = 1.0
            sel[64:128, 12 + (R - 1 - r)] = 1.0
            m["sel"] = sel
            m["ada_w"] = np.ascontiguousarray(common["ada_w"][r:r + 1]) if R > 1 else common["ada_w"]
            m["ada_bT"] = _fm(f32("ada_b")[r]) if R > 1 else common["ada_bT"]
            vR = 1.0 if r < R - 1 else 0.0
            vL = 1.0 if r > 0 else 0.0
            m["mske"] = np.ascontiguousarray(np.concatenate([np.tile(U, (1, 4)) * vR, np.tile(L, (1, 4)) * vL], 1)).astype(bf)
            maps.append(m)
    return maps


_CACHE = {}
NR = 4


def kernel(**inputs):
    x = np.asarray(inputs["x"])
    B, LAT = int(x.shape[0]), int(x.shape[1])
    LC = LAT // NR
    if LC not in _CACHE:
        _CACHE[LC] = build_program(LC, R=NR)[0]
    nc = _CACHE[LC]
    maps = make_in_maps(inputs, NR)
    res = run_bass_kernel_spmd(nc, maps, core_ids=list(range(len(maps))))
    out = np.empty((B, LAT, D), np.float32)
    for b in range(B):
        for r in range(NR):
            out[b, r * LC:(r + 1) * LC] = res.results[b * NR + r]["outT"].T
    return out
```
